# Optimizing a Trainium2 kernel written in Bass

```python
import jax, jax.numpy as jnp
from jax import lax
import numpy as np

D_MODEL = 1024
BATCH = 8
SEQ = 2048
DEPTH = 2
DEC_BATCH = 32
DEC_SEQ = 64
PAST_LEN = 2048

CHUNK = 64
N_MIXERS = 2
N_GLA_LAYERS = (DEPTH + 1) // 2
N_FOX_LAYERS = DEPTH // 2

GLA_HEADS = 4
GLA_DK = D_MODEL // 2 // GLA_HEADS
GLA_DV = D_MODEL // GLA_HEADS
GLA_HK = GLA_HEADS * GLA_DK
GLA_HV = GLA_HEADS * GLA_DV
GLA_GATE_RANK = 16
GLA_GATE_TAU = 16.0
GLA_IN = 2 * GLA_HK + 2 * GLA_HV + GLA_GATE_RANK

FOX_HEADS = 16
FOX_HD = D_MODEL // FOX_HEADS
FOX_HW = FOX_HEADS * FOX_HD
FOX_IN = 3 * FOX_HW + FOX_HEADS
FOX_QBLOCK = 128
FOX_FORGET_BIAS_INIT = 3.0

D_FF = -(-8 * D_MODEL // (3 * 256)) * 256
EPS = 1e-6
MASK_VALUE = -1e30

kernel_name = "gla_fox_hybrid_stream_step"


def rmsnorm(x, g):
    xf = x.astype(jnp.float32)
    y = xf * lax.rsqrt(jnp.mean(xf * xf, axis=-1, keepdims=True) + EPS)
    return (y * g.astype(jnp.float32)).astype(x.dtype)


def swiglu(h, w_in, w_down):
    gu = h @ w_in
    gate, up = jnp.split(gu, [D_FF], axis=-1)
    return (jax.nn.silu(gate) * up) @ w_down


def gla_recurrence(q, k, v, g, s0):
    B, T = q.shape[:2]
    n = -(-T // CHUNK)
    pad = n * CHUNK - T

    def pad_t(a):
        return jnp.pad(a, ((0, 0), (0, pad), (0, 0), (0, 0)))

    def to_blocks(a):
        return a.reshape(B, n, CHUNK, a.shape[2], a.shape[3]).transpose(1, 0, 3, 2, 4)

    qc = to_blocks(pad_t(q)).astype(jnp.float32)
    kc = to_blocks(pad_t(k)).astype(jnp.float32)
    vc = to_blocks(pad_t(v)).astype(jnp.float32)
    gc = to_blocks(pad_t(g)).astype(jnp.float32)
    b = jnp.cumsum(gc, axis=3)
    b_last = b[:, :, :, -1:, :]
    qe = qc * jnp.exp(b) * (GLA_DK ** -0.5)
    ke = kc * jnp.exp(-b)
    kd = kc * jnp.exp(b_last - b)
    causal = jnp.tril(jnp.ones((CHUNK, CHUNK), dtype=bool))
    a = jnp.where(causal, jnp.einsum('nbhtd,nbhsd->nbhts', qe, ke), 0.0)
    o_intra = jnp.einsum('nbhts,nbhsv->nbhtv', a, vc)

    def step(s, xs):
        qe_c, kd_c, v_c, dec_c = xs
        o = jnp.einsum('bhtd,bhdv->bhtv', qe_c, s)
        s = s * dec_c[..., None] + jnp.einsum('bhsd,bhsv->bhdv', kd_c, v_c)
        return s, o

    s_fin, o_inter = lax.scan(step, s0.astype(jnp.float32),
                              (qe, kd, vc, jnp.exp(b_last[:, :, :, 0, :])))
    o = (o_intra + o_inter).transpose(1, 0, 3, 2, 4).reshape(B, n * CHUNK, GLA_HEADS, GLA_DV)[:, :T]
    return o.astype(q.dtype), s_fin.astype(s0.dtype)


def gla_mixer(h, s0, w_in, w_g2, b_g, norm_g, w_out):
    B, T, _ = h.shape
    proj = h @ w_in
    q, k, v, r, gl = jnp.split(proj, [GLA_HK, 2 * GLA_HK, 2 * GLA_HK + GLA_HV, 2 * GLA_HK + 2 * GLA_HV], axis=-1)
    q = q.reshape(B, T, GLA_HEADS, GLA_DK)
    k = k.reshape(B, T, GLA_HEADS, GLA_DK)
    v = v.reshape(B, T, GLA_HEADS, GLA_DV)
    g = (jax.nn.log_sigmoid((gl @ w_g2 + b_g).astype(jnp.float32)) / GLA_GATE_TAU)
    g = g.reshape(B, T, GLA_HEADS, GLA_DK)
    o, s = gla_recurrence(q, k, v, g, s0)
    o = rmsnorm(o, norm_g.reshape(GLA_HEADS, GLA_DV)).reshape(B, T, GLA_HV)
    return (o * jax.nn.silu(r)) @ w_out, s


def fox_attend(q, k, v, c_q, c_k, q_pos, k_pos):
    s = jnp.einsum('bqhd,bkhd->bhqk', q, k).astype(jnp.float32) * (FOX_HD ** -0.5)
    bias = c_q.transpose(0, 2, 1)[:, :, :, None] - c_k.transpose(0, 2, 1)[:, :, None, :]
    mask = k_pos[None, :] <= q_pos[:, None]
    p = jax.nn.softmax(jnp.where(mask, s + bias, MASK_VALUE), axis=-1)
    return jnp.einsum('bhqk,bkhd->bqhd', p.astype(v.dtype), v)


def fox_project(h, w_in, b_f):
    B, T, _ = h.shape
    proj = h @ w_in
    q, k, v, fl = jnp.split(proj, [FOX_HW, 2 * FOX_HW, 3 * FOX_HW], axis=-1)
    q = q.reshape(B, T, FOX_HEADS, FOX_HD)
    k = k.reshape(B, T, FOX_HEADS, FOX_HD)
    v = v.reshape(B, T, FOX_HEADS, FOX_HD)
    logf = jax.nn.log_sigmoid((fl + b_f).astype(jnp.float32))
    return q, k, v, logf


def fox_mixer_prompt(h, w_in, b_f, w_out):
    B, T, _ = h.shape
    q, k, v, logf = fox_project(h, w_in, b_f)
    c = jnp.cumsum(logf, axis=1)
    nb = T // FOX_QBLOCK
    qb = q.reshape(B, nb, FOX_QBLOCK, FOX_HEADS, FOX_HD).transpose(1, 0, 2, 3, 4)
    cb = c.reshape(B, nb, FOX_QBLOCK, FOX_HEADS).transpose(1, 0, 2, 3)
    pos = jnp.arange(T, dtype=jnp.int32)
    pb = pos.reshape(nb, FOX_QBLOCK)
    o = lax.map(lambda xs: fox_attend(xs[0], k, v, xs[1], c, xs[2], pos), (qb, cb, pb))
    o = o.transpose(1, 0, 2, 3, 4).reshape(B, T, FOX_HW)
    return o @ w_out, k, v, logf.astype(h.dtype)


def fox_mixer_sample(h, k_cache, v_cache, logf_cache, w_in, b_f, w_out):
    B, S, _ = h.shape
    P = k_cache.shape[1]
    q, k, v, logf = fox_project(h, w_in, b_f)
    k_all = jnp.concatenate([k_cache.astype(k.dtype), k], axis=1)
    v_all = jnp.concatenate([v_cache.astype(v.dtype), v], axis=1)
    c = jnp.cumsum(jnp.concatenate([logf_cache.astype(jnp.float32), logf], axis=1), axis=1)
    k_pos = jnp.arange(P + S, dtype=jnp.int32)
    q_pos = P + jnp.arange(S, dtype=jnp.int32)
    o = fox_attend(q, k_all, v_all, c[:, P:], c, q_pos, k_pos).reshape(B, S, FOX_HW)
    return o @ w_out, k, v, logf.astype(h.dtype)


def setup_inputs(seed: int = 0) -> dict:
    key = jax.random.key(seed)
    ks = jax.random.split(key, 24)
    nrm = lambda k, shape, scale: jax.random.normal(k, shape, jnp.float32) * scale
    return {
        "x_prompt": nrm(ks[0], (BATCH, SEQ, D_MODEL), 1.0),
        "x_sample": nrm(ks[1], (DEC_BATCH, DEC_SEQ, D_MODEL), 1.0),
        "state_gla": nrm(ks[2], (N_GLA_LAYERS, DEC_BATCH, GLA_HEADS, GLA_DK, GLA_DV), 0.5),
        "cache_fox_k": nrm(ks[3], (N_FOX_LAYERS, DEC_BATCH, PAST_LEN, FOX_HEADS, FOX_HD), 1.0),
        "cache_fox_v": nrm(ks[4], (N_FOX_LAYERS, DEC_BATCH, PAST_LEN, FOX_HEADS, FOX_HD), 1.0),
        "cache_fox_logf": jax.nn.log_sigmoid(FOX_FORGET_BIAS_INIT + nrm(ks[5], (N_FOX_LAYERS, DEC_BATCH, PAST_LEN, FOX_HEADS), 1.0)),
        "norm_mix": 1.0 + nrm(ks[6], (DEPTH, D_MODEL), 0.02),
        "gla_w_in": nrm(ks[7], (N_GLA_LAYERS, D_MODEL, GLA_IN), D_MODEL ** -0.5),
        "gla_w_g2": nrm(ks[8], (N_GLA_LAYERS, GLA_GATE_RANK, GLA_HK), GLA_GATE_RANK ** -0.5),
        "gla_b_g": nrm(ks[9], (N_GLA_LAYERS, GLA_HK), 0.1),
        "gla_norm": 1.0 + nrm(ks[10], (N_GLA_LAYERS, GLA_HV), 0.02),
        "gla_w_out": nrm(ks[11], (N_GLA_LAYERS, GLA_HV, D_MODEL), GLA_HV ** -0.5),
        "fox_w_in": nrm(ks[12], (N_FOX_LAYERS, D_MODEL, FOX_IN), D_MODEL ** -0.5),
        "fox_b_f": FOX_FORGET_BIAS_INIT + nrm(ks[13], (N_FOX_LAYERS, FOX_HEADS), 0.1),
        "fox_w_out": nrm(ks[14], (N_FOX_LAYERS, FOX_HW, D_MODEL), FOX_HW ** -0.5),
        "norm_ffn": 1.0 + nrm(ks[15], (DEPTH, D_MODEL), 0.02),
        "ffn_w_in": nrm(ks[16], (DEPTH, D_MODEL, 2 * D_FF), D_MODEL ** -0.5),
        "ffn_w_down": nrm(ks[17], (DEPTH, D_FF, D_MODEL), D_FF ** -0.5),
        "norm_final": 1.0 + nrm(ks[18], (D_MODEL,), 0.02),
    }


def reference(x_prompt, x_sample, state_gla, cache_fox_k, cache_fox_v, cache_fox_logf,
              norm_mix, gla_w_in, gla_w_g2, gla_b_g, gla_norm, gla_w_out,
              fox_w_in, fox_b_f, fox_w_out, norm_ffn, ffn_w_in, ffn_w_down, norm_final):
    xp, xs = x_prompt, x_sample
    gla_sp, gla_ss = [], []
    fox_kp, fox_vp, fox_fp, fox_ks, fox_vs, fox_fs = [], [], [], [], [], []
    for i in range(DEPTH):
        j = i // N_MIXERS
        hp = rmsnorm(xp, norm_mix[i])
        hs = rmsnorm(xs, norm_mix[i])
        if i % N_MIXERS == 0:
            s0 = jnp.zeros((xp.shape[0], GLA_HEADS, GLA_DK, GLA_DV), xp.dtype)
            op, sp = gla_mixer(hp, s0, gla_w_in[j], gla_w_g2[j], gla_b_g[j], gla_norm[j], gla_w_out[j])
            os_, ss = gla_mixer(hs, state_gla[j], gla_w_in[j], gla_w_g2[j], gla_b_g[j], gla_norm[j], gla_w_out[j])
            gla_sp.append(sp)
            gla_ss.append(ss)
        else:
            op, kp, vp, fp = fox_mixer_prompt(hp, fox_w_in[j], fox_b_f[j], fox_w_out[j])
            os_, ks_, vs_, fs_ = fox_mixer_sample(hs, cache_fox_k[j], cache_fox_v[j], cache_fox_logf[j],
                                                 fox_w_in[j], fox_b_f[j], fox_w_out[j])
            fox_kp.append(kp); fox_vp.append(vp); fox_fp.append(fp)
            fox_ks.append(ks_); fox_vs.append(vs_); fox_fs.append(fs_)
        xp = xp + op
        xs = xs + os_
        xp = xp + swiglu(rmsnorm(xp, norm_ffn[i]), ffn_w_in[i], ffn_w_down[i])
        xs = xs + swiglu(rmsnorm(xs, norm_ffn[i]), ffn_w_in[i], ffn_w_down[i])
    y_prompt = rmsnorm(xp, norm_final)
    y_sample = rmsnorm(xs, norm_final)
    gla_state_p = jnp.stack(gla_sp, axis=0)
    gla_state_s = jnp.stack(gla_ss, axis=0)
    fox_k_p = jnp.stack(fox_kp, axis=0)
    fox_v_p = jnp.stack(fox_vp, axis=0)
    fox_logf_p = jnp.stack(fox_fp, axis=0)
    fox_k_s = jnp.stack(fox_ks, axis=0)
    fox_v_s = jnp.stack(fox_vs, axis=0)
    fox_logf_s = jnp.stack(fox_fs, axis=0)
    return (y_prompt, y_sample, gla_state_p, fox_k_p, fox_v_p, fox_logf_p,
            gla_state_s, fox_k_s, fox_v_s, fox_logf_s)
```

```python
import numpy as np
import concourse.bass as bass
import concourse.mybir as mybir
from concourse.bass_utils import run_bass_kernel_spmd

F32 = mybir.dt.float32
BF16 = mybir.dt.bfloat16
AF = mybir.ActivationFunctionType
ALU = mybir.AluOpType

D = 1024
KC = 8
SEQ = 2048
NS = 4
DSEQ = 64
NTOK = SEQ + NS * DSEQ
TILES = [(0, 512), (512, 512), (1024, 512), (1536, 512), (2048, 256)]
DFF = 2816
FC = 22
GLA_IN = 3088
EPS = 1e-6
NCORES = 8

C_ID, C_UINC, C_USTR, C_MASK2, C_TRI, C_NSTR, C_NONE = 0, 128, 256, 384, 512, 640, 768
C_NW = 896
C_WG2 = 944
C_BF = 1456
CW = 1472


class Sched:
    def __init__(self, nc):
        self.nc = nc
        self.eng = {"pe": nc.tensor, "act": nc.scalar, "dve": nc.vector, "pool": nc.gpsimd, "sp": nc.sync}
        self.semh = {}
        self.cnt = {}
        self.waited = {e: {} for e in self.eng}
        self.lastw = {}
        self.readers = {}
        self.src_of = {}
        self._stack = []

    def add_sem(self, name, src):
        cm = self.nc.semaphore(name)
        h = cm.__enter__()
        self._stack.append(cm)
        self.semh[name] = h
        self.cnt[name] = 0
        self.src_of[name] = src

    def close(self):
        for cm in reversed(self._stack):
            cm.__exit__(None, None, None)

    def _wait(self, eng, reads, writes, disjoint=False):
        need = {}

        why = {}

        def add(tok, kind, key=None):
            sem, val = tok
            why[(sem, val)] = (key, kind)
            src = self.src_of[sem]
            if src == eng and eng == "pe":
                return
            if need.get(sem, 0) < val:
                need[sem] = val

        for k in reads:
            t = self.lastw.get(k)
            if t is not None:
                add(t, "raw", k)
        for k in writes:
            t = self.lastw.get(k)
            if t is not None and not (disjoint and self.src_of[t[0]] == eng):
                add(t, "waw", k)
            for sem, val in self.readers.get(k, {}).items():
                add((sem, val), "war", k)
        for sem, val in need.items():
            if self.waited[eng].get(sem, 0) >= val:
                continue
            assert self.cnt[sem] >= val, f"dependency on unsignaled op: {sem} {val} > {self.cnt[sem]} {why.get((sem, val))}"
            self.eng[eng].wait_ge(self.semh[sem], val)
            self.waited[eng][sem] = val

    def _record(self, tok, reads, writes):
        sem, val = tok
        for k in reads:
            r = self.readers.setdefault(k, {})
            if r.get(sem, 0) < val:
                r[sem] = val
        for k in writes:
            self.lastw[k] = tok
            self.readers[k] = {}

    def op(self, eng, fn, reads=(), writes=(), signal=True, disjoint=False):
        pk = [k for k in reads if isinstance(k, str) and k.startswith("pb")]
        if pk:
            reads = [k for k in reads if k not in pk]
            writes = list(writes) + [k for k in pk if k not in writes]
        self._wait(eng, reads, writes, disjoint)
        ins = fn(self.eng[eng])
        sem = "c_" + eng
        if signal:
            self.cnt[sem] += 1
            ins.then_inc(self.semh[sem], 1)
            tok = (sem, self.cnt[sem])
        else:
            tok = (sem, self.cnt[sem] + 1)
        self._record(tok, reads, writes)
        return tok

    def dma(self, queue, pairs, reads, writes, sem):
        self._wait(queue, reads, writes)
        for (o, i) in pairs:
            ins = self.eng[queue].dma_start(out=o, in_=i)
            ins.then_inc(self.semh[sem], 16)
            self.cnt[sem] += 16
        tok = (sem, self.cnt[sem])
        self._record(tok, reads, writes)
        return tok

    def barrier(self, engines=("pe", "act", "dve", "pool", "sp")):
        for e in engines:
            for sem, c in self.cnt.items():
                if c > 0 and self.waited[e].get(sem, 0) < c:
                    if self.src_of[sem] == e and e == "pe":
                        continue
                    self.eng[e].wait_ge(self.semh[sem], c)
                    self.waited[e][sem] = c


def mkap(t, off, pat):
    return bass.AP(t.tensor if hasattr(t, "tensor") else t, off, pat)


def build_program(cfg):
    nc = bass.Bass("TRN2", target_bir_lowering=False, dynamic_dma_scratch_size=4096)
    S = Sched(nc)
    for e in ("pe", "act", "dve", "pool"):
        S.add_sem("c_" + e, e)

    def dram_in(name, shape):
        return nc.dram_tensor(name, shape, F32, kind="ExternalInput").ap()

    def dram_out(name, shape):
        return nc.dram_tensor(name, shape, F32, kind="ExternalOutput").ap()

    xin = dram_in("xin", [NTOK, D])
    cst_d = dram_in("cst", [128, CW])
    gla_w_in = dram_in("gla_w_in", [D, GLA_IN])
    gla_w_out = dram_in("gla_w_out", [D, D])
    fox_w_in = dram_in("fox_w_in", [D, GLA_IN])
    fox_w_out = dram_in("fox_w_out", [D, D])
    ffn_w_in = dram_in("ffn_w_in", [2, D, 2 * DFF])
    ffn_w_down = dram_in("ffn_w_down", [2, DFF, D])
    state_in = dram_in("state_in", [NS, 4, 128, 256])
    ck_in = dram_in("ck_in", [NS, SEQ, D])
    cv_in = dram_in("cv_in", [NS, SEQ, D])
    clf_in = dram_in("clf_in", [NS, SEQ, 16])

    y_out = dram_out("y_out", [NTOK, D])
    gsp_out = dram_out("gsp_out", [4, 128, 256])
    gss_out = dram_out("gss_out", [NS, 4, 128, 256])
    k_out = dram_out("k_out", [NTOK, D])
    v_out = dram_out("v_out", [NTOK, D])
    lf_out = dram_out("lf_out", [NTOK, 16])

    ctxs = []

    def sb(name, shape, dt):
        cm = nc.sbuf_tensor(name, shape, dt)
        t = cm.__enter__()
        ctxs.append(cm)
        return t

    def ps(name, shape, dt=F32):
        cm = nc.psum_tensor(name, shape, dt)
        t = cm.__enter__()
        ctxs.append(cm)
        return t

    def release(n):
        for _ in range(n):
            ctxs.pop().__exit__(None, None, None)

    xT = sb("xT", [128, KC, NTOK], F32)
    cst = sb("cst_sb", [128, CW], F32)
    ones_bf = sb("ones_bf", [128, 128], BF16)
    NWB = 2
    WSLOT = 4096
    wbuf = [sb(f"wbuf{i}", [128, WSLOT], BF16) for i in range(NWB)]
    for i in range(NWB):
        S.add_sem(f"d_w{i}", "dma")
    S.add_sem("d_cst", "dma")
    S.add_sem("d_out", "dma")
    for i in range(2):
        S.add_sem(f"d_st{i}", "dma")
    wstate = {"i": 0}

    ident = cst[:, C_ID:C_ID + 128]

    def nwcol(idx, kc):
        return cst[:, C_NW + idx * 8 + kc:C_NW + idx * 8 + kc + 1]

    S.dma("sp", [(cst[:], cst_d[:, :])], [], ["cst"], "d_cst")
    S.op("pool", lambda e: e.memset(ones_bf[:], 1.0), [], ["ones_bf"])

    def wload(pairs_fn, nelem):
        i = wstate["i"] % NWB
        wstate["i"] += 1
        key = f"wbuf{i}"
        pairs = pairs_fn(wbuf[i])
        S.dma("pool", pairs, [], [key], f"d_w{i}")
        return wbuf[i], key

    def wload_cols(W2d, c0, ncols, kcn=KC, rows0=0):
        def f(buf):
            src = W2d[rows0:rows0 + kcn * 128, c0:c0 + ncols].rearrange("(kc p) c -> p kc c", p=128)
            dst = buf[:, 0:kcn * ncols].rearrange("p (kc c) -> p kc c", c=ncols)
            return [(dst, src)]
        return wload(f, kcn * ncols)

    pbank = [ps(f"pb{i}", [128, 512]) for i in range(8)]
    pstate = {"i": 0}

    def next_bank(lo=0, hi=8):
        i = lo + (pstate["i"] % (hi - lo))
        pstate["i"] += 1
        return pbank[i], f"pb{i}"

    NXST = 6
    xst = [sb(f"xst{i}", [128, D], F32) for i in range(NXST)]
    for i in range(2, NXST):
        S.add_sem(f"d_st{i}", "dma")
    for s in range(NTOK // 128):
        st, skey = xst[s % NXST], f"xst{s % NXST}"
        S.dma("sp", [(st[:], xin[s * 128:(s + 1) * 128, :])], [], [skey], f"d_st{s % NXST}")
        for half in range(2):
            pb, pkey = next_bank()
            for j in range(4):
                kc = half * 4 + j
                S.op("pe", lambda e, pb=pb, j=j, kc=kc, st=st: e.transpose(pb[:, j * 128:(j + 1) * 128], st[:, kc * 128:(kc + 1) * 128], ident),
                     [skey, "cst"], [pkey], signal=(j == 3))
            dst = xT[:, half * 4:(half + 1) * 4, s * 128:(s + 1) * 128]
            src = pb[:].rearrange("p (j t) -> p j t", t=128)
            tt = min(s * 128 // 512, 4)
            eng = "act" if half == 0 else "dve"
            if eng == "act":
                S.op("act", lambda e, dst=dst, src=src: e.copy(out=dst, in_=src), [pkey], [("x", tt)])
            else:
                S.op("dve", lambda e, dst=dst, src=src: e.tensor_copy(out=dst, in_=src), [pkey], [("x", tt)])

    def rmsnorm(tt, nidx, out_fn, out_keys, sq, sqkey, rstd, rkey, bank=None):
        t0, n = TILES[tt]
        S.op("act", lambda e: e.activation(out=sq[:, :, 0:n], in_=xT[:, :, t0:t0 + n], func=AF.Square),
             [("x", tt)], [sqkey])
        pb, pkey = bank() if bank is not None else next_bank()
        for kc in range(KC):
            S.op("pe", lambda e, kc=kc: e.matmul(pb[:, 0:n], ones_bf[:], sq[:, kc, 0:n], start=(kc == 0), stop=(kc == KC - 1)),
                 [sqkey, "ones_bf"], [pkey], signal=(kc == KC - 1))
        S.op("dve", lambda e: e.tensor_scalar(out=rstd[:, 0:n], in0=pb[:, 0:n], scalar1=1.0 / D, scalar2=EPS, op0=ALU.mult, op1=ALU.add),
             [pkey], [rkey])
        S.op("act", lambda e: e.activation(out=rstd[:, 0:n], in_=rstd[:, 0:n], func=AF.Ln), [rkey], [rkey])
        S.op("act", lambda e: e.activation(out=rstd[:, 0:n], in_=rstd[:, 0:n], func=AF.Exp, scale=-0.5), [rkey], [rkey])
        for kc in range(KC):
            S.op("dve", lambda e, kc=kc: e.scalar_tensor_tensor(out=out_fn(kc), in0=xT[:, kc, t0:t0 + n], scalar=nwcol(nidx, kc),
                                                               in1=rstd[:, 0:n], op0=ALU.mult, op1=ALU.mult),
                 [("x", tt), rkey, "cst"], out_keys, disjoint=True)

    def ffn_phase(layer):
        nidx = 1 if layer == 0 else 3
        groups = [[0, 1], [2, 3, 4]]
        GT = 1280
        hg = sb(f"ffn_h{layer}", [128, KC, GT], BF16)
        act = sb(f"ffn_act{layer}", [128, FC, GT], BF16)
        sq = sb(f"ffn_sq{layer}", [128, KC, 512], BF16)
        rstd = sb(f"ffn_rstd{layer}", [128, 512], F32)
        sg = [sb(f"ffn_sg{i}_{layer}", [128, 512], BF16) for i in range(2)]
        Win = ffn_w_in[layer]
        Wdn = ffn_w_down[layer]
        for g in groups:
            offs = {}
            o = 0
            for tt in g:
                offs[tt] = o
                o += TILES[tt][1]
            for tt in g:
                n = TILES[tt][1]
                rmsnorm(tt, nidx, lambda kc, tt=tt, n=n: hg[:, kc, offs[tt]:offs[tt] + n], [("ffn_h", tt)], sq, "ffn_sq", rstd, "ffn_rstd")
            for fs in range(FC // 2):
                def f(buf, fs=fs):
                    dst = buf[:, 0:KC * 512].rearrange("p (kc c) -> p kc c", c=512)
                    s1 = Win[:, fs * 256:(fs + 1) * 256].rearrange("(kc p) c -> p kc c", p=128)
                    s2 = Win[:, DFF + fs * 256:DFF + (fs + 1) * 256].rearrange("(kc p) c -> p kc c", p=128)
                    return [(dst[:, :, 0:256], s1), (dst[:, :, 256:512], s2)]
                wb, wkey = wload(f, KC * 512)
                wv = wb[:, 0:KC * 512].rearrange("p (kc c) -> p kc c", c=512)
                for tt in g:
                    n = TILES[tt][1]
                    ho = offs[tt]
                    for fc2 in range(2):
                        fch = fs * 2 + fc2
                        pg, pgk = next_bank()
                        for kc in range(KC):
                            S.op("pe", lambda e, kc=kc, pg=pg, fc2=fc2: e.matmul(pg[:, 0:n], wv[:, kc, fc2 * 128:(fc2 + 1) * 128], hg[:, kc, ho:ho + n],
                                                                               start=(kc == 0), stop=(kc == KC - 1)),
                                 [wkey, ("ffn_h", tt)], [pgk], signal=(kc == KC - 1))
                        pu, puk = next_bank()
                        for kc in range(KC):
                            S.op("pe", lambda e, kc=kc, pu=pu, fc2=fc2: e.matmul(pu[:, 0:n], wv[:, kc, 256 + fc2 * 128:256 + (fc2 + 1) * 128], hg[:, kc, ho:ho + n],
                                                                               start=(kc == 0), stop=(kc == KC - 1)),
                                 [wkey, ("ffn_h", tt)], [puk], signal=(kc == KC - 1))
                        sgi = fch % 2
                        S.op("act", lambda e, pg=pg, sgi=sgi: e.activation(out=sg[sgi][:, 0:n], in_=pg[:, 0:n], func=AF.Silu), [pgk], [f"ffn_sg{sgi}"])
                        S.op("dve", lambda e, pu=pu, sgi=sgi, fch=fch: e.tensor_tensor(out=act[:, fch, ho:ho + n], in0=pu[:, 0:n], in1=sg[sgi][:, 0:n], op=ALU.mult),
                             [puk, f"ffn_sg{sgi}"], [("ffn_act", tt, fch)])
            for ns in range(KC):
                def f(buf, ns=ns):
                    dst = buf[:, 0:FC * 128].rearrange("p (fc c) -> p fc c", c=128)
                    src = Wdn[:, ns * 128:(ns + 1) * 128].rearrange("(fc p) c -> p fc c", p=128)
                    return [(dst, src)]
                wb, wkey = wload(f, FC * 128)
                wv = wb[:, 0:FC * 128].rearrange("p (fc c) -> p fc c", c=128)
                for tt in g:
                    t0, n = TILES[tt]
                    ho = offs[tt]
                    pd, pdk = next_bank()
                    for fch in range(FC):
                        S.op("pe", lambda e, fch=fch, pd=pd: e.matmul(pd[:, 0:n], wv[:, fch, :], act[:, fch, ho:ho + n], start=(fch == 0), stop=(fch == FC - 1)),
                             [wkey, ("ffn_act", tt, fch)], [pdk], signal=(fch == FC - 1))
                    S.op("dve", lambda e, pd=pd, ns=ns: e.tensor_tensor(out=xT[:, ns, t0:t0 + n], in0=pd[:, 0:n], in1=xT[:, ns, t0:t0 + n], op=ALU.add),
                         [pdk, ("x", tt)], [("x", tt)], disjoint=True)
        S.barrier()
        release(6)

    release(NXST)
    S.barrier()


    def gla_phase():
        hT = sb("g_h", [128, KC, 512], BF16)
        sq = sb("g_sq", [128, 4, 512], BF16)
        yT = sb("g_yT", [128, KC, 512], BF16)
        rstd = sb("g_rstd", [128, 512], F32)
        glaug = sb("g_glaug", [17, 512], F32)
        sp = sb("g_sp", [128, 4, 512], F32)
        eb = sb("g_eb", [128, 4, 512], BF16)
        enb = sb("g_enb", [128, 4, 512], BF16)
        edT = sb("g_edT", [128, 4, 512], BF16)
        dec = [sb(f"g_dec{p}", [128, 4, 8], F32) for p in range(2)]
        qeT = [sb(f"g_qeT{p}", [128, 4, 512], BF16) for p in range(2)]
        keT = [sb(f"g_keT{p}", [128, 4, 512], BF16) for p in range(2)]
        kd = [sb(f"g_kd{p}", [128, 4, 512], BF16) for p in range(2)]
        vt = [sb(f"g_v{p}", [128, 4, 1024], BF16) for p in range(2)]
        sr = [sb(f"g_sr{p}", [128, KC, 512], BF16) for p in range(2)]
        Sf = sb("g_Sf", [128, 2, 4, 256], F32)
        Sb = sb("g_Sb", [128, 2, 4, 256], BF16)
        ATm = [sb(f"g_AT{h}", [128, 128], BF16) for h in range(4)]
        obuf = [sb(f"g_obuf{p}", [128, 4, 256], BF16) for p in range(2)]
        osq = sb("g_osq", [128, 4, 256], BF16)
        orstd = sb("g_orstd", [128, 4, 128], F32)
        r3 = sb("g_r3", [128, 8, 128], BF16)
        nalloc = 9 + 10 + 4 + 4 + 5
        S.add_sem("d_gs0", "dma")
        S.add_sem("d_gs1", "dma")
        rot = {"p": 0, "m": 0}
        PB_P = [5, 6, 7]

        def nbp():
            i = PB_P[rot["p"] % 3]
            rot["p"] += 1
            return pbank[i], f"pb{i}"

        Uinc = cst[:, C_UINC:C_UINC + 128]
        Ustr = cst[:, C_USTR:C_USTR + 128]
        mask2 = cst[:, C_MASK2:C_MASK2 + 128]
        wg2 = cst[0:17, C_WG2:C_WG2 + 512]
        S.op("pool", lambda e: e.memset(glaug[:], 1.0), [], ["g_glaug"])
        S.op("pool", lambda e: e.memset(Sf[:], 0.0), [], [("Sf", i_, h_) for i_ in range(2) for h_ in range(4)])
        S.op("pool", lambda e: e.memset(Sb[:], 0.0), [], [("Sb", 0, h) for h in range(4)] + [("Sb", 1, h) for h in range(4)])

        def gen_P(tt):
            par = tt % 2
            t0, n = TILES[tt]
            nsub = n // 128

            def proj_fm(wv, wkey, c0, m, pb, pkey):
                for kc in range(KC):
                    S.op("pe", lambda e: e.matmul(pb[0:m, 0:n], wv[:, kc, c0:c0 + m], hT[:, kc, 0:n], start=(kc == 0), stop=(kc == KC - 1)),
                         [wkey, "g_h"], [pkey], signal=(kc == KC - 1))

            def proj_tm(wv, wkey, sub, ncols, pb, pkey):
                for kc in range(KC):
                    S.op("pe", lambda e: e.matmul(pb[:, 0:ncols], hT[:, kc, sub * 128:(sub + 1) * 128], wv[:, kc, 0:ncols], start=(kc == 0), stop=(kc == KC - 1)),
                         [wkey, "g_h"], [pkey], signal=(kc == KC - 1))

            pb, pkey = nbp()
            for hf in range(2):
                S.op("act", lambda e: e.activation(out=sq[:, :, 0:n], in_=xT[:, hf * 4:(hf + 1) * 4, t0:t0 + n], func=AF.Square), [("x", tt)], ["g_sq"])
                for q in range(4):
                    kc = hf * 4 + q
                    S.op("pe", lambda e: e.matmul(pb[:, 0:n], ones_bf[:], sq[:, q, 0:n], start=(kc == 0), stop=(kc == KC - 1)),
                         ["g_sq", "ones_bf"], [pkey], signal=(q == 3))
                yield
            S.op("dve", lambda e: e.tensor_scalar(out=rstd[:, 0:n], in0=pb[:, 0:n], scalar1=1.0 / D, scalar2=EPS, op0=ALU.mult, op1=ALU.add), [pkey], ["g_rstd"])
            S.op("act", lambda e: e.activation(out=rstd[:, 0:n], in_=rstd[:, 0:n], func=AF.Ln), ["g_rstd"], ["g_rstd"])
            S.op("act", lambda e: e.activation(out=rstd[:, 0:n], in_=rstd[:, 0:n], func=AF.Exp, scale=-0.5), ["g_rstd"], ["g_rstd"])
            for kc in range(KC):
                S.op("dve", lambda e: e.scalar_tensor_tensor(out=hT[:, kc, 0:n], in0=xT[:, kc, t0:t0 + n], scalar=nwcol(0, kc), in1=rstd[:, 0:n], op0=ALU.mult, op1=ALU.mult),
                     [("x", tt), "g_rstd", "cst"], ["g_h"], disjoint=True)
                if kc % 4 == 3:
                    yield
            wb, wkey = wload_cols(gla_w_in, 3072, 16)
            wv = wb[:, 0:KC * 16].rearrange("p (kc c) -> p kc c", c=16)
            pb, pkey = nbp()
            proj_fm(wv, wkey, 0, 16, pb, pkey)
            S.op("act", lambda e: e.copy(out=glaug[0:16, 0:n], in_=pb[0:16, 0:n]), [pkey], ["g_glaug"])
            yield
            for sub in range(nsub):
                pb, pkey = nbp()
                S.op("pe", lambda e: e.matmul(pb[:, :], glaug[0:17, sub * 128:(sub + 1) * 128], wg2, start=True, stop=True), ["g_glaug", "cst"], [pkey])
                S.op("act", lambda e: e.activation(out=sp[:, sub, :], in_=pb[:, :], func=AF.Exp, scale=-1.0), [pkey], [("g_sp", sub)])
                S.op("dve", lambda e: e.tensor_scalar_add(sp[:, sub, :], sp[:, sub, :], 1.0), [("g_sp", sub)], [("g_sp", sub)])
                S.op("act", lambda e: e.activation(out=sp[:, sub, :], in_=sp[:, sub, :], func=AF.Ln), [("g_sp", sub)], [("g_sp", sub)])
                yield
            for j in range(2):
                wb, wkey = wload_cols(gla_w_in, 1024 + j * 512, 512)
                wv = wb[:, 0:KC * 512].rearrange("p (kc c) -> p kc c", c=512)
                for sub in range(nsub):
                    pb, pkey = nbp()
                    proj_tm(wv, wkey, sub, 512, pb, pkey)
                    if (sub + j) % 2 == 0:
                        S.op("act", lambda e: e.copy(out=vt[par][:, sub, j * 512:(j + 1) * 512], in_=pb[:, :]), [pkey], [("g_v", par, sub)])
                    else:
                        S.op("dve", lambda e: e.tensor_copy(out=vt[par][:, sub, j * 512:(j + 1) * 512], in_=pb[:, :]), [pkey], [("g_v", par, sub)])
                    yield
            for sub in range(nsub):
                pb, pkey = nbp()
                S.op("pe", lambda e: e.matmul(pb[:, :], Ustr, sp[:, sub, :], start=True, stop=True), [("g_sp", sub), "cst"], [pkey])
                S.op("act", lambda e: e.activation(out=edT[:, sub, :], in_=pb[:, :], func=AF.Exp), [pkey], [("g_edT", sub)])
                yield
            for h in range(4):
                pb, pkey = nbp()
                for sub in range(nsub):
                    S.op("pe", lambda e: e.matmul(pb[:, sub * 128:(sub + 1) * 128], sp[:, sub, h * 128:(h + 1) * 128], Uinc, start=True, stop=True),
                         [("g_sp", sub), "cst"], [pkey], signal=(sub == nsub - 1))
                S.op("act", lambda e: e.activation(out=eb[:, h, 0:n], in_=pb[:, 0:n], func=AF.Exp), [pkey], [("g_eb", h)])
                S.op("act", lambda e: e.activation(out=enb[:, h, 0:n], in_=pb[:, 0:n], func=AF.Exp, scale=-1.0), [pkey], [("g_enb", h)])
                S.op("act", lambda e: e.activation(out=dec[par][:, h, 0:n // 64], in_=pb[:, 0:n].rearrange("p (c j) -> p c j", j=64)[:, :, 63], func=AF.Exp),
                     [pkey], [("g_dec", par, h)])
                yield
            wb, wkey = wload_cols(gla_w_in, 0, 512)
            wv = wb[:, 0:KC * 512].rearrange("p (kc c) -> p kc c", c=512)
            for h in range(4):
                pb, pkey = nbp()
                proj_fm(wv, wkey, h * 128, 128, pb, pkey)
                S.op("dve", lambda e: e.scalar_tensor_tensor(out=qeT[par][:, h, 0:n], in0=pb[:, 0:n], scalar=float(128 ** -0.5), in1=eb[:, h, 0:n], op0=ALU.mult, op1=ALU.mult),
                     [pkey, ("g_eb", h)], [("g_qeT", par, h)])
                yield
            wb, wkey = wload_cols(gla_w_in, 512, 512)
            wv = wb[:, 0:KC * 512].rearrange("p (kc c) -> p kc c", c=512)
            for h in range(4):
                pb, pkey = nbp()
                proj_fm(wv, wkey, h * 128, 128, pb, pkey)
                S.op("dve", lambda e: e.tensor_tensor(out=keT[par][:, h, 0:n], in0=pb[:, 0:n], in1=enb[:, h, 0:n], op=ALU.mult),
                     [pkey, ("g_enb", h)], [("g_keT", par, h)])
                yield
            for sub in range(nsub):
                pb, pkey = nbp()
                proj_tm(wv, wkey, sub, 512, pb, pkey)
                S.op("dve", lambda e: e.tensor_tensor(out=kd[par][:, sub, :], in0=pb[:, :], in1=edT[:, sub, :], op=ALU.mult),
                     [pkey, ("g_edT", sub)], [("g_kd", par, sub)])
                yield
            for j in range(2):
                wb, wkey = wload_cols(gla_w_in, 2048 + j * 512, 512)
                wv = wb[:, 0:KC * 512].rearrange("p (kc c) -> p kc c", c=512)
                for c4 in range(4):
                    pb, pkey = nbp()
                    proj_fm(wv, wkey, c4 * 128, 128, pb, pkey)
                    S.op("act", lambda e: e.activation(out=sr[par][:, j * 4 + c4, 0:n], in_=pb[:, 0:n], func=AF.Silu), [pkey], [("g_sr", par, j * 4 + c4)])
                    yield

        def gen_R(tt):
            par = tt % 2
            t0, n = TILES[tt]
            is_sample = (tt == 4)
            pending_norms = []

            def norm_stages(pair, pc):
                ob_ = obuf[pair % 2]
                okb = ("g_obuf", pair % 2)
                S.op("act", lambda e: e.activation(out=osq[:, :, :], in_=ob_[:, :, :], func=AF.Square), [okb], ["g_osq"])
                yield
                pb, pkey = nbp()
                for h in range(4):
                    for dvc in range(2):
                        S.op("pe", lambda e: e.matmul(pb[:, h * 128:(h + 1) * 128], ones_bf[:], osq[:, h, dvc * 128:(dvc + 1) * 128], start=(dvc == 0), stop=(dvc == 1)),
                             ["g_osq", "ones_bf"], [pkey], signal=(h == 3 and dvc == 1))
                yield
                ov = orstd[:, :, :].rearrange("p h t -> p (h t)")
                S.op("dve", lambda e: e.tensor_scalar(out=ov, in0=pb[:, :], scalar1=1.0 / 256.0, scalar2=EPS, op0=ALU.mult, op1=ALU.add), [pkey], ["g_orstd"])
                S.op("act", lambda e: e.activation(out=ov, in_=ov, func=AF.Ln), ["g_orstd"], ["g_orstd"])
                S.op("act", lambda e: e.activation(out=ov, in_=ov, func=AF.Exp, scale=-0.5), ["g_orstd"], ["g_orstd"])
                yield
                for h in range(4):
                    for dvc in range(2):
                        hc = h * 2 + dvc
                        S.op("dve", lambda e: e.scalar_tensor_tensor(out=r3[:, hc, :], in0=sr[par][:, hc, pc:pc + 128], scalar=nwcol(5, hc), in1=orstd[:, h, :], op0=ALU.mult, op1=ALU.mult),
                             [("g_sr", par, hc), "g_orstd", "cst"], [("g_r3", hc)])
                    if h % 2 == 1:
                        yield
                for h in range(4):
                    for dvc in range(2):
                        hc = h * 2 + dvc
                        S.op("dve", lambda e: e.tensor_tensor(out=yT[:, hc, pc:pc + 128], in0=ob_[:, h, dvc * 128:(dvc + 1) * 128], in1=r3[:, hc, :], op=ALU.mult),
                             [okb, ("g_r3", hc)], ["g_yT"], disjoint=True)
                    if h % 2 == 1:
                        yield
            for pair in range(n // 128):
                sub = pair
                pc = pair * 128
                if is_sample:
                    for i in range(2):
                        seq = pair * 2 + i
                        S.dma("sp", [(Sf[:, i, :, :], state_in[seq].rearrange("h p v -> p h v"))], [], [("Sf", i, h_) for h_ in range(4)], f"d_st{i}")
                        for h in range(4):
                            if h % 2 == 0:
                                S.op("act", lambda e: e.copy(out=Sb[:, i, h, :], in_=Sf[:, i, h, :]), [("Sf", i, h)], [("Sb", i, h)])
                            else:
                                S.op("dve", lambda e: e.tensor_copy(out=Sb[:, i, h, :], in_=Sf[:, i, h, :]), [("Sf", i, h)], [("Sb", i, h)])
                for h in range(4):
                    S.op("pe", lambda e: e.matmul(pbank[2][:, h * 128:(h + 1) * 128], keT[par][:, h, pc:pc + 128], qeT[par][:, h, pc:pc + 128], start=True, stop=True),
                         [("g_keT", par, h), ("g_qeT", par, h)], ["pb2"])
                yield
                for h in range(4):
                    S.op("dve", lambda e: e.tensor_tensor(out=ATm[h][:], in0=pbank[2][:, h * 128:(h + 1) * 128], in1=mask2, op=ALU.mult), ["pb2", "cst"], [f"g_AT{h}"])
                yield
                for i in range(2):
                    c = pair * 2 + i
                    fs = i if is_sample else 0
                    bs = i if is_sample else (c % 2)
                    r0 = i * 64
                    for h in range(4):
                        ub = pbank[3 + h // 2]
                        uc = (h % 2) * 256
                        S.op("pe", lambda e: e.matmul(ub[:, uc:uc + 256], kd[par][r0:r0 + 64, sub, h * 128:(h + 1) * 128], vt[par][r0:r0 + 64, sub, h * 256:(h + 1) * 256], start=True, stop=True),
                             [("g_kd", par, sub), ("g_v", par, sub)], [f"pb{3 + h // 2}"])
                    yield
                    for h in range(4):
                        ob, okey = pbank[h // 2], f"pb{h // 2}"
                        for dvc in range(2):
                            ocol = (h % 2) * 256 + dvc * 128
                            if i == 0:
                                S.op("pe", lambda e: e.matmul(ob[:, ocol:ocol + 128], vt[par][:, sub, h * 256 + dvc * 128:h * 256 + (dvc + 1) * 128], ATm[h][:], start=(h % 2 == 0 and dvc == 0), stop=False, skip_group_check=True),
                                     [("g_v", par, sub), f"g_AT{h}"], [okey], signal=False)
                            S.op("pe", lambda e: e.matmul(ob[:, ocol + r0:ocol + r0 + 64], Sb[:, bs, h, dvc * 128:(dvc + 1) * 128], qeT[par][:, h, pc + r0:pc + r0 + 64], start=False, stop=(i == 1), skip_group_check=True),
                                 [("Sb", bs, h), ("g_qeT", par, h)], [okey], signal=(i == 1 or dvc == 1))
                    yield
                    for h in range(4):
                        ub = pbank[3 + h // 2]
                        uc = (h % 2) * 256
                        S.op("dve", lambda e: e.scalar_tensor_tensor(out=Sf[:, fs, h, :], in0=Sf[:, fs, h, :], scalar=dec[par][:, h, c:c + 1], in1=ub[:, uc:uc + 256], op0=ALU.mult, op1=ALU.add),
                             [("Sf", fs, h), ("g_dec", par, h), f"pb{3 + h // 2}"], [("Sf", fs, h)])
                        if not is_sample:
                            nbs = (c + 1) % 2
                            S.op("act", lambda e: e.copy(out=Sb[:, nbs, h, :], in_=Sf[:, 0, h, :]), [("Sf", 0, h)], [("Sb", nbs, h)])
                    yield
                if is_sample:
                    for i in range(2):
                        seq = pair * 2 + i
                        S.dma("sp", [(gss_out[seq].rearrange("h p v -> p h v"), Sf[:, i, :, :])], [("Sf", i, h_) for h_ in range(4)], [], f"d_gs{i}")
                ob_ = obuf[pair % 2]
                okb = ("g_obuf", pair % 2)
                S.op("act", lambda e: e.copy(out=ob_[:, 0:2, :], in_=pbank[0][:, :].rearrange("p (h c) -> p h c", h=2)), ["pb0"], [okb])
                S.op("dve", lambda e: e.tensor_copy(out=ob_[:, 2:4, :], in_=pbank[1][:, :].rearrange("p (h c) -> p h c", h=2)), ["pb1"], [okb])
                yield
                pending_norms.append((pair, pc))
                while len(pending_norms) > 1:
                    for _ in norm_stages(*pending_norms.pop(0)):
                        yield
            while pending_norms:
                for _ in norm_stages(*pending_norms.pop(0)):
                    yield
            if tt == 3:
                S.dma("sp", [(gsp_out.rearrange("h p v -> p h v"), Sf[:, 0, :, :])], [("Sf", 0, h_) for h_ in range(4)], [], "d_gs0")
            for j in range(2):
                wb, wkey = wload_cols(gla_w_out, j * 512, 512)
                wv = wb[:, 0:KC * 512].rearrange("p (kc c) -> p kc c", c=512)
                for c4 in range(4):
                    ncx = j * 4 + c4
                    pb, pkey = nbp()
                    for kc in range(KC):
                        S.op("pe", lambda e: e.matmul(pb[:, 0:n], wv[:, kc, c4 * 128:(c4 + 1) * 128], yT[:, kc, 0:n], start=(kc == 0), stop=(kc == KC - 1)),
                             [wkey, "g_yT"], [pkey], signal=(kc == KC - 1))
                    S.op("dve", lambda e: e.tensor_tensor(out=xT[:, ncx, t0:t0 + n], in0=pb[:, 0:n], in1=xT[:, ncx, t0:t0 + n], op=ALU.add),
                         [pkey, ("x", tt)], [("x", tt)], disjoint=True)
                    yield

        def run(g):
            for _ in g:
                pass

        def interleave(gr, gp, ratio):
            ar, ap = True, gp is not None
            while ar or ap:
                if ar:
                    try:
                        next(gr)
                    except StopIteration:
                        ar = False
                for _ in range(ratio):
                    if ap:
                        try:
                            next(gp)
                        except StopIteration:
                            ap = False

        tiles = list(cfg.get("gla_tiles", range(5)))
        run(gen_P(tiles[0]))
        for idx, tt in enumerate(tiles):
            nxt = tiles[idx + 1] if idx + 1 < len(tiles) else None
            interleave(gen_R(tt), gen_P(nxt) if nxt is not None else None, cfg.get("gla_ratio", 1))
        S.barrier()
        release(nalloc)

    if cfg.get("gla", True):
        gla_phase()

    if cfg.get("ffn0", True):
        ffn_phase(0)

    def fox_phase(sample):
        sfx = "s" if sample else "p"
        hT = sb("f_h" + sfx, [128, KC, 512], BF16)
        sqq = sb("f_sqq" + sfx, [128, KC, 512], BF16)
        rstd = sb("f_rstd" + sfx, [128, 512], F32)
        kT = sb("f_kT" + sfx, [128, KC, 2112 if sample else 2048], BF16)
        v65 = sb("f_v65" + sfx, [128, 16, 1040], BF16)
        PT = [sb(f"f_PT{i}" + sfx, [128, 512], BF16) for i in range(3)]
        o_n = sb("f_on" + sfx, [128, 4, 1024], BF16)
        spl = sb("f_spl" + sfx, [128, 17, 16], F32)
        bias = sb("f_bias" + sfx, [128, 17, 16], F32)
        lfz = sb("f_lfz" + sfx, [128, 16], F32)
        lfst = sb("f_lfst" + sfx, [128, 16], F32)
        rl = sb("f_rl" + sfx, [128, 4], F32)
        zeros = sb("f_zeros" + sfx, [128, 512], BF16)
        ident_bf = sb("f_idbf" + sfx, [128, 128], BF16)
        cstage = [sb(f"f_cst{i}" + sfx, [128, 1024 if sample else 512], F32) for i in range(2)]
        nalloc = 18
        if sample:
            kTn = sb("f_kTn", [128, KC, 256], BF16)
            lfc = sb("f_lfc", [128, 2, 16], F32)
            lfc_c = sb("f_lfcc", [128, 17, 16], F32)
            vn = sb("f_vn", [128, 2, 1040], BF16)
            wexp = lfc_c
            nalloc += 4
        else:
            qz2 = sb("f_qz2", [128, KC, 512], BF16)
            ttot = sb("f_ttot", [128, 16, 16], F32)
            sfx = sb("f_sfx", [128, 16, 16], F32)
            nalloc += 3
        oT = hT
        qT = sqq
        if not sample:
            for nm in ("d_c0", "d_c1", "d_lf", "d_lfc"):
                S.add_sem(nm, "dma")
        tri = cst[:, C_TRI:C_TRI + 128]
        NSTR = cst[:, C_NSTR:C_NSTR + 128]
        NONE = cst[:, C_NONE:C_NONE + 128]
        bfb = cst[:, C_BF:C_BF + 16]
        S.op("pool", lambda e: e.memset(zeros[:], 0.0), [], ["f_zeros"])
        S.op("pool", lambda e: e.memset(v65[:], 1.0), [], [("f_v65", kt) for kt in range(16)])
        if not sample:
            S.op("pool", lambda e: e.memset(qz2[:], 0.0), [], ["f_qz2"])
        S.op("dve", lambda e: e.tensor_copy(out=ident_bf[:], in_=ident), ["cst"], ["f_idbf"])
        if sample:
            S.op("pool", lambda e: e.memset(vn[:], 1.0), [], [("f_vn", 0), ("f_vn", 1)])
        qoff = {"c": 0}
        rot = {"s": 0, "o": 0, "m": 0, "c": 0, "p": 0}

        def nbk(which, lst):
            i = lst[rot[which] % len(lst)]
            rot[which] += 1
            return pbank[i], f"pb{i}"

        def stage_slot():
            i = rot["c"] % 2
            rot["c"] += 1
            return cstage[i], f"f_cst{i}", f"d_c{i}"

        def compute_bias(nk):
            pb, pkey = pbank[4], "pb4"
            rhs = spl[:, 0:nk, :]
            S.op("pe", lambda e: e.matmul(pb[:, 0:nk * 16], NSTR, rhs, start=True, stop=True), [("f_spl", a) for a in range(nk)] + ["cst"], [pkey])
            S.op("dve", lambda e: e.tensor_copy(out=bias[:, 0:nk, :], in_=pb[:, 0:nk * 16].rearrange("p (a h) -> p a h", h=16)), [pkey], ["f_bias"])
            S.op("pe", lambda e: e.matmul(pb[:, 0:nk * 16], NONE, rhs, start=True, stop=True), [("f_spl", a) for a in range(nk)] + ["cst"], [pkey])
            S.op("dve", lambda e: e.tensor_copy(out=ttot[:, 0:nk, :], in_=pb[:, 0:nk * 16].rearrange("p (a h) -> p a h", h=16)), [pkey], ["f_ttot"])
            for a in range(nk - 2, -1, -1):
                if a == nk - 2:
                    S.op("dve", lambda e: e.tensor_copy(out=sfx[:, a, :], in_=ttot[:, a + 1, :]), ["f_ttot"], ["f_sfx"])
                else:
                    S.op("dve", lambda e: e.tensor_tensor(out=sfx[:, a, :], in0=sfx[:, a + 1, :], in1=ttot[:, a + 1, :], op=ALU.add), ["f_ttot", "f_sfx"], ["f_sfx"])
            if nk > 1:
                S.op("dve", lambda e: e.tensor_tensor(out=bias[:, 0:nk - 1, :], in0=bias[:, 0:nk - 1, :], in1=sfx[:, 0:nk - 1, :], op=ALU.add), ["f_bias", "f_sfx"], ["f_bias"])

        SB_P = [0, 1, 5, 6]

        def attend_prompt(tt):
            nk = 4 * tt + 4
            items = []
            for h in range(16):
                for a in range(nk):
                    j = a - 4 * tt
                    items.append(dict(h=h, a=a, c0=(j * 128 if j > 0 else 0), diag=(j >= 0), first=(a == 0), last=(a == nk - 1)))
            n_it = len(items)
            Sres = [None] * n_it
            Eres = [None] * n_it
            obank = {}

            def do_S(i):
                it = items[i]
                h, a, c0 = it["h"], it["a"], it["c0"]
                hp, hc = h % 2, h // 2
                pr = slice(hp * 64, hp * 64 + 64)
                pbS, skey = nbk("s", SB_P)
                qb_, qk_ = (sqq, "f_sqq") if h < 8 else (qz2, "f_qz2")
                S.op("pe", lambda e: e.matmul(pbS[:, c0:512], kT[:, hc, a * 128:(a + 1) * 128], qb_[:, h % 8, c0:512], start=True, stop=True),
                     [("f_kT", a), qk_], [skey])
                Sres[i] = (pbS, skey)

            def do_E(i):
                it = items[i]
                h, a, c0 = it["h"], it["a"], it["c0"]
                pbS, skey = Sres[i]
                pi = rot["p"] % 3
                rot["p"] += 1
                pt, ptk = PT[pi], f"f_PT{pi}"
                S.op("act", lambda e: e.activation(out=pt[:, c0:512], in_=pbS[:, c0:512], func=AF.Exp, bias=bias[:, a, h:h + 1], scale=0.125),
                     [skey, "f_bias"], [ptk])
                if it["diag"]:
                    S.op("dve", lambda e: e.tensor_tensor(out=pt[:, c0:c0 + 128], in0=pt[:, c0:c0 + 128], in1=tri, op=ALU.mult), [ptk, "cst"], [ptk])
                Eres[i] = (pt, ptk)

            def do_PV(i):
                it = items[i]
                h, a, c0 = it["h"], it["a"], it["c0"]
                pt, ptk = Eres[i]
                if it["first"]:
                    obank[h] = nbk("o", [2, 3])
                    pbO, okey = obank[h]
                    S.op("pe", lambda e: e.matmul(pbO[:, :], zeros[:, 0:128], zeros[:, :], start=True, stop=False, skip_group_check=True), ["f_zeros"], [okey], signal=False)
                pbO, okey = obank[h]
                for qt in range(c0 // 128, 4):
                    is_last = (it["last"] and qt == 3)
                    S.op("pe", lambda e: e.matmul(pbO[:, qt * 128:qt * 128 + 65], pt[:, qt * 128:(qt + 1) * 128], v65[:, a, h * 65:(h + 1) * 65], start=False, stop=is_last, skip_group_check=True),
                         [ptk, ("f_v65", a)], [okey], signal=(qt == 3))
                if it["last"]:
                    ov = pbO[:, :].rearrange("p (q c) -> p q c", c=128)
                    S.op("dve", lambda e: e.reciprocal(out=rl[:, 0:4], in_=ov[:, 0:4, 64]), [okey], ["f_rl"])
                    rb = mkap(rl, 0, [[4, 128], [1, 4], [0, 64]])
                    S.op("dve", lambda e: e.tensor_tensor(out=o_n[:, 0:4, h * 64:(h + 1) * 64], in0=ov[:, 0:4, 0:64], in1=rb, op=ALU.mult),
                         [okey, "f_rl"], ["f_on"])

            do_S(0)
            if n_it > 1:
                do_S(1)
            for i in range(n_it):
                do_E(i)
                if i + 2 < n_it:
                    do_S(i + 2)
                if i >= 1:
                    do_PV(i - 1)
            do_PV(n_it - 1)

        SB_S = [0, 1, 5, 6, 7, 4]
        PTS = [(PT[0], "f_PT0"), (PT[1], "f_PT1"), (PT[2], "f_PT2")]

        def attend_sample(seq, sub, r0):
            ptl = PTS + [(o_n[:, 1, 0:512], "f_on1a"), (o_n[:, 1, 512:1024], "f_on1b"), (o_n[:, 2, 0:512], "f_on2a")]
            qc = seq * 64
            Sres = {}
            Eres = {}

            def do_S(h):
                hp, hc = h % 2, h // 2
                pr = slice(hp * 64, hp * 64 + 64)
                banks = [nbk("s", SB_S) for _ in range(3)]
                for g in range(2):
                    pbS, skey = banks[g]
                    for j in range(8):
                        a = g * 8 + j
                        S.op("pe", lambda e: e.matmul(pbS[:, j * 64:(j + 1) * 64], kT[pr, hc, a * 128:(a + 1) * 128], qT[pr, hc, qc:qc + 64], start=True, stop=True),
                             [("f_kT", a), "f_sqq"], [skey], signal=(j == 7))
                pbS, skey = banks[2]
                S.op("pe", lambda e: e.matmul(pbS[r0:r0 + 64, 0:64], kT[pr, hc, 2048:2112], qT[pr, hc, qc:qc + 64], start=True, stop=True),
                     [("f_kT", 16), "f_sqq"], [skey])
                Sres[h] = banks

            def do_E(h):
                banks = Sres[h]
                pts = []
                for g in range(3):
                    pi = rot["p"] % 6
                    rot["p"] += 1
                    pt, ptk = ptl[pi]
                    pbS, skey = banks[g]
                    if g < 2:
                        S.op("act", lambda e: e.activation(out=pt[:, 0:512], in_=pbS[:, 0:512], func=AF.Exp, scale=0.125), [skey], [ptk])
                    else:
                        S.op("act", lambda e: e.activation(out=pt[r0:r0 + 64, 0:64], in_=pbS[r0:r0 + 64, 0:64], func=AF.Exp, scale=0.125), [skey], [ptk])
                        S.op("dve", lambda e: e.tensor_tensor(out=pt[r0:r0 + 64, 0:64], in0=pt[r0:r0 + 64, 0:64], in1=tri[r0:r0 + 64, r0:r0 + 64], op=ALU.mult), [ptk, "cst"], [ptk])
                    pts.append((pt, ptk))
                Eres[h] = pts

            def do_PV(h):
                pts = Eres[h]
                pbO, okey = nbk("o", [2, 3])
                for a in range(16):
                    pt, ptk = pts[a // 8]
                    j = a % 8
                    S.op("pe", lambda e: e.matmul(pbO[0:64, 0:65], pt[:, j * 64:(j + 1) * 64], v65[:, a, h * 65:(h + 1) * 65], start=(a == 0), stop=False),
                         [ptk, ("f_v65", a)], [okey], signal=False)
                pt, ptk = pts[2]
                S.op("pe", lambda e: e.matmul(pbO[0:64, 0:65], pt[r0:r0 + 64, 0:64], vn[r0:r0 + 64, sub, h * 65:(h + 1) * 65], start=False, stop=True),
                     [ptk, ("f_vn", sub)], [okey])
                S.op("dve", lambda e: e.reciprocal(out=rl[0:64, 0:1], in_=pbO[0:64, 64:65]), [okey], ["f_rl"])
                S.op("dve", lambda e: e.tensor_scalar(out=o_n[0:64, 0, h * 64:(h + 1) * 64], in0=pbO[0:64, 0:64], scalar1=rl[0:64, 0:1], scalar2=None, op0=ALU.mult),
                     [okey, "f_rl"], ["f_on"])

            do_S(0)
            for h in range(16):
                do_E(h)
                if h + 1 < 16:
                    do_S(h + 1)
                do_PV(h)

        def proj_fm(wv, wkey, c0, n, pb, pkey):
            for kc in range(KC):
                S.op("pe", lambda e: e.matmul(pb[:, 0:n], wv[:, kc, c0:c0 + 128], hT[:, kc, 0:n], start=(kc == 0), stop=(kc == KC - 1)),
                     [wkey, "f_h"], [pkey], signal=(kc == KC - 1))

        def proj_tm(wv, wkey, sub, ncols, pb, pkey):
            for kc in range(KC):
                S.op("pe", lambda e: e.matmul(pb[:, 0:ncols], hT[:, kc, sub * 128:(sub + 1) * 128], wv[:, kc, 0:ncols], start=(kc == 0), stop=(kc == KC - 1)),
                     [wkey, "f_h"], [pkey], signal=(kc == KC - 1))

        for tt in ([4] if sample else cfg.get("fox_tiles", [0, 1, 2, 3])):
            t0, n = TILES[tt]
            nsub = n // 128
            is_sample = (tt == 4)
            rmsnorm(tt, 2, lambda kc: hT[:, kc, 0:n], ["f_h"], sqq, "f_sqq", rstd, "f_rstd")
            if not is_sample:
                sq4 = sqq[:, :, :].rearrange("p (a b) c -> p a b c", b=2)
                S.op("pool", lambda e: e.memset(sq4[64:128, :, 0, :], 0.0), ["f_rstd"], ["f_sqq"])
                S.op("pool", lambda e: e.memset(sq4[0:64, :, 1, :], 0.0), ["f_rstd"], ["f_sqq"])
            wb, wkey = wload_cols(fox_w_in, 3072, 16)
            wv = wb[:, 0:KC * 16].rearrange("p (kc c) -> p kc c", c=16)
            for sub in range(nsub):
                kt = (4 * tt + sub) if not is_sample else 16
                pb, pkey = nbk("m", [5, 6, 7])
                proj_tm(wv, wkey, sub, 16, pb, pkey)
                S.op("dve", lambda e: e.tensor_tensor(out=lfz[:, :], in0=pb[:, 0:16], in1=bfb, op=ALU.add), [pkey, "cst"], ["f_lfz"])
                S.op("act", lambda e: e.activation(out=lfz[:, :], in_=lfz[:, :], func=AF.Exp, scale=-1.0), ["f_lfz"], ["f_lfz"])
                S.op("dve", lambda e: e.tensor_scalar_add(lfz[:, :], lfz[:, :], 1.0), ["f_lfz"], ["f_lfz"])
                if not is_sample:
                    S.op("act", lambda e: e.activation(out=spl[:, kt, :], in_=lfz[:, :], func=AF.Ln), ["f_lfz"], [("f_spl", kt)])
                    src, skey2 = spl[:, kt, :], ("f_spl", kt)
                else:
                    S.op("act", lambda e: e.activation(out=lfc[:, sub, :], in_=lfz[:, :], func=AF.Ln), ["f_lfz"], [("f_spn", sub)])
                    src, skey2 = lfc[:, sub, :], ("f_spn", sub)
                S.op("act", lambda e: e.mul(out=lfst[:, :], in_=src, mul=-1.0), [skey2], ["f_lfst"])
                S.dma("sp", [(lf_out[t0 + sub * 128:t0 + (sub + 1) * 128, :], lfst[:, :])], ["f_lfst"], [], "d_lf")
            for j in range(2):
                wb, wkey = wload_cols(fox_w_in, j * 512, 512)
                wv = wb[:, 0:KC * 512].rearrange("p (kc c) -> p kc c", c=512)
                for c4 in range(4):
                    pb, pkey = nbk("m", [5, 6, 7])
                    proj_fm(wv, wkey, c4 * 128, n, pb, pkey)
                    if is_sample:
                        if c4 % 2 == 0:
                            S.op("act", lambda e: e.copy(out=qT[:, j * 4 + c4, 0:n], in_=pb[:, 0:n]), [pkey], ["f_sqq"])
                        else:
                            S.op("dve", lambda e: e.tensor_copy(out=qT[:, j * 4 + c4, 0:n], in_=pb[:, 0:n]), [pkey], ["f_sqq"])
                    else:
                        hcq = j * 4 + c4
                        for hp in range(2):
                            hh = 2 * hcq + hp
                            qb_, qk_ = (sqq, "f_sqq") if hh < 8 else (qz2, "f_qz2")
                            pr_ = slice(hp * 64, hp * 64 + 64)
                            if hp == 0:
                                S.op("act", lambda e: e.copy(out=qb_[pr_, hh % 8, 0:n], in_=pb[pr_, 0:n]), [pkey], [qk_])
                            else:
                                S.op("dve", lambda e: e.tensor_copy(out=qb_[pr_, hh % 8, 0:n], in_=pb[pr_, 0:n]), [pkey], [qk_])
            for j in range(2):
                wb, wkey = wload_cols(fox_w_in, 1024 + j * 512, 512)
                wv = wb[:, 0:KC * 512].rearrange("p (kc c) -> p kc c", c=512)
                for c4 in range(4):
                    pb, pkey = nbk("m", [5, 6, 7])
                    proj_fm(wv, wkey, c4 * 128, n, pb, pkey)
                    if not is_sample:
                        dst = kT[:, j * 4 + c4, t0:t0 + n]
                        dkeys = [("f_kT", 4 * tt + q) for q in range(4)]
                    else:
                        dst = kTn[:, j * 4 + c4, 0:n]
                        dkeys = ["f_kTn"]
                    if c4 % 2 == 0:
                        S.op("act", lambda e: e.copy(out=dst, in_=pb[:, 0:n]), [pkey], dkeys)
                    else:
                        S.op("dve", lambda e: e.tensor_copy(out=dst, in_=pb[:, 0:n]), [pkey], dkeys)
                for sub in range(nsub):
                    pb, pkey = nbk("m", [5, 6, 7])
                    proj_tm(wv, wkey, sub, 512, pb, pkey)
                    st, stk, sem = stage_slot()
                    S.op("act", lambda e: e.copy(out=st[:, 0:512], in_=pb[:, :]), [pkey], [stk])
                    S.dma("sp", [(k_out[t0 + sub * 128:t0 + (sub + 1) * 128, j * 512:(j + 1) * 512], st[:, 0:512])], [stk], [], sem)
            for j in range(2):
                wb, wkey = wload_cols(fox_w_in, 2048 + j * 512, 512)
                wv = wb[:, 0:KC * 512].rearrange("p (kc c) -> p kc c", c=512)
                for sub in range(nsub):
                    pb, pkey = nbk("m", [5, 6, 7])
                    proj_tm(wv, wkey, sub, 512, pb, pkey)
                    st, stk, sem = stage_slot()
                    S.op("act", lambda e: e.copy(out=st[:, 0:512], in_=pb[:, :]), [pkey], [stk])
                    S.dma("sp", [(v_out[t0 + sub * 128:t0 + (sub + 1) * 128, j * 512:(j + 1) * 512], st[:, 0:512])], [stk], [], sem)
                    if not is_sample:
                        kt = 4 * tt + sub
                        dst = v65[:, kt, j * 520:(j + 1) * 520].rearrange("p (h c) -> p h c", c=65)[:, :, 0:64]
                        S.op("dve", lambda e: e.tensor_copy(out=dst, in_=pb[:, :].rearrange("p (h c) -> p h c", c=64)), [pkey], [("f_v65", kt)])
                    else:
                        dst = vn[:, sub, j * 520:(j + 1) * 520].rearrange("p (h c) -> p h c", c=65)[:, :, 0:64]
                        S.op("dve", lambda e: e.tensor_copy(out=dst, in_=pb[:, :].rearrange("p (h c) -> p h c", c=64)), [pkey], [("f_vn", sub)])
            if not is_sample:
                nk = 4 * tt + 4
                compute_bias(nk)
                attend_prompt(tt)
                for hc in range(KC):
                    pb, pkey = nbk("m", [5, 6, 7])
                    for qt in range(4):
                        S.op("pe", lambda e: e.matmul(pb[:, qt * 128:(qt + 1) * 128], o_n[:, qt, hc * 128:(hc + 1) * 128], ident_bf[:], start=True, stop=True),
                             ["f_on", "f_idbf"], [pkey], signal=(qt == 3))
                    if hc % 2 == 0:
                        S.op("act", lambda e: e.copy(out=oT[:, hc, 0:512], in_=pb[:, :]), [pkey], ["f_h"])
                    else:
                        S.op("dve", lambda e: e.tensor_copy(out=oT[:, hc, 0:512], in_=pb[:, :]), [pkey], ["f_h"])
            else:
                for seq in range(NS):
                    sub, r0 = seq // 2, (seq % 2) * 64
                    S.dma("sp", [(lfc_c[:, 0:16, :], clf_in[seq].rearrange("(a p) h -> p a h", p=128))], [], ["f_lfcc"], "d_lfc")
                    S.op("act", lambda e: e.mul(out=spl[:, 0:16, :], in_=lfc_c[:, 0:16, :], mul=-1.0), ["f_lfcc"], [("f_spl", a) for a in range(16)])
                    S.op("dve", lambda e: e.tensor_copy(out=spl[:, 16, :], in_=lfc[:, sub, :]), [("f_spn", sub)], [("f_spl", 16)])
                    compute_bias([(a, 0, 128) for a in range(16)] + [(16, r0, 64)])
                    S.op("act", lambda e: e.activation(out=wexp[:, :, :], in_=bias[:, :, :], func=AF.Exp), ["f_bias"], ["f_lfcc"])
                    vn3 = vn[r0:r0 + 64, sub, :].rearrange("p (h c) -> p h c", c=65)
                    wn3 = mkap(wexp, r0 * 17 * 16 + 16 * 16, [[17 * 16, 64], [1, 16], [0, 64]])
                    S.op("pool", lambda e: e.tensor_tensor(out=vn3[:, :, 0:64], in0=vn3[:, :, 0:64], in1=wn3, op=ALU.mult), [("f_vn", sub), "f_lfcc"], [("f_vn", sub)])
                    S.op("pool", lambda e: e.tensor_copy(out=vn3[:, :, 64], in_=wexp[r0:r0 + 64, 16, :]), ["f_lfcc"], [("f_vn", sub)])
                    for a in range(16):
                        st, stk, sem = stage_slot()
                        S.dma("sp", [(st[:, :], ck_in[seq, a * 128:(a + 1) * 128, :])], [], [stk], sem)
                        for half in range(2):
                            pb, pkey = nbk("m", [5, 6, 7])
                            for q in range(4):
                                kc = half * 4 + q
                                S.op("pe", lambda e: e.transpose(pb[:, q * 128:(q + 1) * 128], st[:, kc * 128:(kc + 1) * 128], ident), [stk, "cst"], [pkey], signal=(q == 3))
                            dst = kT[:, half * 4:(half + 1) * 4, a * 128:(a + 1) * 128]
                            srcp = pb[:, :].rearrange("p (q t) -> p q t", t=128)
                            if half == 0:
                                S.op("act", lambda e: e.copy(out=dst, in_=srcp), [pkey], [("f_kT", a)])
                            else:
                                S.op("dve", lambda e: e.tensor_copy(out=dst, in_=srcp), [pkey], [("f_kT", a)])
                        st, stk, sem = stage_slot()
                        S.dma("sp", [(st[:, :], cv_in[seq, a * 128:(a + 1) * 128, :])], [], [stk], sem)
                        v3 = v65[:, a, :].rearrange("p (h c) -> p h c", c=65)
                        wb3 = mkap(wexp, a * 16, [[17 * 16, 128], [1, 16], [0, 64]])
                        S.op("pool", lambda e: e.tensor_tensor(out=v3[:, :, 0:64], in0=st[:, :].rearrange("p (h c) -> p h c", c=64), in1=wb3, op=ALU.mult),
                             [stk, "f_lfcc"], [("f_v65", a)])
                        S.op("pool", lambda e: e.tensor_copy(out=v3[:, :, 64], in_=wexp[:, a, :]), ["f_lfcc"], [("f_v65", a)])
                    S.op("dve", lambda e: e.tensor_copy(out=kT[:, :, 2048:2112], in_=kTn[:, :, seq * 64:(seq + 1) * 64]), ["f_kTn"], [("f_kT", 16)])
                    attend_sample(seq, sub, r0)
                    pb, pkey = nbk("m", [5, 6, 7])
                    for hc in range(KC):
                        S.op("pe", lambda e: e.matmul(pb[:, hc * 64:(hc + 1) * 64], o_n[0:64, 0, hc * 128:(hc + 1) * 128], ident_bf[0:64, 0:64], start=True, stop=True),
                             ["f_on", "f_idbf"], [pkey], signal=(hc == KC - 1))
                    S.op("act", lambda e: e.copy(out=oT[:, :, seq * 64:(seq + 1) * 64], in_=pb[:, :].rearrange("p (c q) -> p c q", q=64)), [pkey], ["f_h"])
            for j in range(2):
                wb, wkey = wload_cols(fox_w_out, j * 512, 512)
                wv = wb[:, 0:KC * 512].rearrange("p (kc c) -> p kc c", c=512)
                for c4 in range(4):
                    ncx = j * 4 + c4
                    pb, pkey = nbk("m", [5, 6, 7])
                    for kc in range(KC):
                        S.op("pe", lambda e: e.matmul(pb[:, 0:n], wv[:, kc, c4 * 128:(c4 + 1) * 128], oT[:, kc, 0:n], start=(kc == 0), stop=(kc == KC - 1)),
                             [wkey, "f_h"], [pkey], signal=(kc == KC - 1))
                    S.op("dve", lambda e: e.tensor_tensor(out=xT[:, ncx, t0:t0 + n], in0=pb[:, 0:n], in1=xT[:, ncx, t0:t0 + n], op=ALU.add),
                         [pkey, ("x", tt)], [("x", tt)], disjoint=True)
        S.barrier()
        release(nalloc)


    def fox_sample():
        tt = 4
        t0, n = TILES[tt]
        hT = sb("fs_h", [128, KC, 256], BF16)
        sq = sb("fs_sq", [128, KC, 256], BF16)
        rstd = sb("fs_rstd", [128, 512], F32)
        qz = sb("fs_qz", [128, 16, 256], BF16)
        kTn = sb("fs_kTn", [128, KC, 256], BF16)
        vn = sb("fs_vn", [128, 2, 1040], BF16)
        lfn = sb("fs_lfn", [128, 2, 16], F32)
        lfz = sb("fs_lfz", [128, 16], F32)
        lfst = sb("fs_lfst", [128, 16], F32)
        lfcc = sb("fs_lfcc", [128, NS, 16, 16], F32)
        spl4 = sb("fs_spl4", [128, NS, 17, 16], F32)
        ttot4 = sb("fs_ttot4", [128, NS, 17, 16], F32)
        sfx4 = sb("fs_sfx4", [128, NS, 17, 16], F32)
        wexp = sb("fs_wexp", [128, NS, 17, 16], F32)
        NST = 4
        kst = [sb(f"fs_kst{i}", [128, D], F32) for i in range(NST)]
        vst = [sb(f"fs_vst{i}", [128, D], F32) for i in range(NST)]
        kTt = [sb(f"fs_kTt{i}", [128, KC, 128], BF16) for i in range(3)]
        v65t = [sb(f"fs_v65t{i}", [128, 1040], BF16) for i in range(3)]
        PTs = [sb(f"fs_PT{i}", [128, 512], BF16) for i in range(4)]
        o_n = sb("fs_on", [128, D], BF16)
        rl = sb("fs_rl", [128, 16], F32)
        zeros = sb("fs_zeros", [128, 512], BF16)
        ident_bf = sb("fs_idbf", [128, 128], BF16)
        nalloc = 19 + 2 * NST + 3 + 3 + 4
        oT = hT
        for i in range(NST):
            S.add_sem(f"d_ks{i}", "dma")
            S.add_sem(f"d_vs{i}", "dma")
        S.add_sem("d_lfcc", "dma")
        tri = cst[:, C_TRI:C_TRI + 128]
        NSTR = cst[:, C_NSTR:C_NSTR + 128]
        NONE = cst[:, C_NONE:C_NONE + 128]
        bfb = cst[:, C_BF:C_BF + 16]
        S.op("pool", lambda e: e.memset(zeros[:], 0.0), [], ["fs_zeros"])
        S.op("pool", lambda e: e.memset(qz[:], 0.0), [], [("fs_q", h_) for h_ in range(16)])
        S.op("pool", lambda e: e.memset(vn[:], 1.0), [], [("fs_vn", 0), ("fs_vn", 1)])
        S.op("dve", lambda e: e.tensor_copy(out=ident_bf[:], in_=ident), ["cst"], ["fs_idbf"])
        S.dma("sp", [(lfcc[:, q, :, :], clf_in[q].rearrange("(a p) h -> p a h", p=128)) for q in range(NS)], [], ["fs_lfcc"], "d_lfcc")
        rot = {"m": 0, "k": 0, "v": 0, "kt": 0, "vt": 0, "p": 0, "o": 0}

        def nbk(which, lst):
            i = lst[rot[which] % len(lst)]
            rot[which] += 1
            return pbank[i], f"pb{i}"

        def proj_fm(wv, wkey, c0, pb, pkey):
            for kc in range(KC):
                S.op("pe", lambda e: e.matmul(pb[:, 0:n], wv[:, kc, c0:c0 + 128], hT[:, kc, 0:n], start=(kc == 0), stop=(kc == KC - 1)),
                     [wkey, "fs_h"], [pkey], signal=(kc == KC - 1))

        def proj_tm(wv, wkey, sub, ncols, pb, pkey):
            for kc in range(KC):
                S.op("pe", lambda e: e.matmul(pb[:, 0:ncols], hT[:, kc, sub * 128:(sub + 1) * 128], wv[:, kc, 0:ncols], start=(kc == 0), stop=(kc == KC - 1)),
                     [wkey, "fs_h"], [pkey], signal=(kc == KC - 1))

        MB = [5, 6, 7]
        rmsnorm(tt, 2, lambda kc: hT[:, kc, 0:n], ["fs_h"], sq, "fs_sq", rstd, "fs_rstd")
        wb, wkey = wload_cols(fox_w_in, 3072, 16)
        wv = wb[:, 0:KC * 16].rearrange("p (kc c) -> p kc c", c=16)
        for sub in range(2):
            pb, pkey = nbk("m", MB)
            proj_tm(wv, wkey, sub, 16, pb, pkey)
            S.op("dve", lambda e: e.tensor_tensor(out=lfz[:, :], in0=pb[:, 0:16], in1=bfb, op=ALU.add), [pkey, "cst"], ["fs_lfz"])
            S.op("act", lambda e: e.activation(out=lfz[:, :], in_=lfz[:, :], func=AF.Exp, scale=-1.0), ["fs_lfz"], ["fs_lfz"])
            S.op("dve", lambda e: e.tensor_scalar_add(lfz[:, :], lfz[:, :], 1.0), ["fs_lfz"], ["fs_lfz"])
            S.op("act", lambda e: e.activation(out=lfn[:, sub, :], in_=lfz[:, :], func=AF.Ln), ["fs_lfz"], [("fs_lfn", sub)])
            S.op("act", lambda e: e.mul(out=lfst[:, :], in_=lfn[:, sub, :], mul=-1.0), [("fs_lfn", sub)], ["fs_lfst"])
            S.dma("sp", [(lf_out[t0 + sub * 128:t0 + (sub + 1) * 128, :], lfst[:, :])], ["fs_lfst"], [], "d_lf")
        for j in range(2):
            wb, wkey = wload_cols(fox_w_in, j * 512, 512)
            wv = wb[:, 0:KC * 512].rearrange("p (kc c) -> p kc c", c=512)
            for c4 in range(4):
                pb, pkey = nbk("m", MB)
                proj_fm(wv, wkey, c4 * 128, pb, pkey)
                hcq = j * 4 + c4
                S.op("act", lambda e: e.copy(out=qz[0:64, 2 * hcq, 0:n], in_=pb[0:64, 0:n]), [pkey], [("fs_q", 2 * hcq)])
                S.op("dve", lambda e: e.tensor_copy(out=qz[64:128, 2 * hcq + 1, 0:n], in_=pb[64:128, 0:n]), [pkey], [("fs_q", 2 * hcq + 1)])
        for j in range(2):
            wb, wkey = wload_cols(fox_w_in, 1024 + j * 512, 512)
            wv = wb[:, 0:KC * 512].rearrange("p (kc c) -> p kc c", c=512)
            for c4 in range(4):
                pb, pkey = nbk("m", MB)
                proj_fm(wv, wkey, c4 * 128, pb, pkey)
                if c4 % 2 == 0:
                    S.op("act", lambda e: e.copy(out=kTn[:, j * 4 + c4, 0:n], in_=pb[:, 0:n]), [pkey], ["fs_kTn"])
                else:
                    S.op("dve", lambda e: e.tensor_copy(out=kTn[:, j * 4 + c4, 0:n], in_=pb[:, 0:n]), [pkey], ["fs_kTn"])
            for sub in range(2):
                pb, pkey = nbk("m", MB)
                proj_tm(wv, wkey, sub, 512, pb, pkey)
                i = rot["k"] % NST
                rot["k"] += 1
                S.op("act", lambda e: e.copy(out=kst[i][:, 0:512], in_=pb[:, :]), [pkey], [f"fs_kst{i}"])
                S.dma("sp", [(k_out[t0 + sub * 128:t0 + (sub + 1) * 128, j * 512:(j + 1) * 512], kst[i][:, 0:512])], [f"fs_kst{i}"], [], f"d_ks{i}")
        for j in range(2):
            wb, wkey = wload_cols(fox_w_in, 2048 + j * 512, 512)
            wv = wb[:, 0:KC * 512].rearrange("p (kc c) -> p kc c", c=512)
            for sub in range(2):
                pb, pkey = nbk("m", MB)
                proj_tm(wv, wkey, sub, 512, pb, pkey)
                i = rot["v"] % NST
                rot["v"] += 1
                S.op("act", lambda e: e.copy(out=vst[i][:, 0:512], in_=pb[:, :]), [pkey], [f"fs_vst{i}"])
                S.dma("sp", [(v_out[t0 + sub * 128:t0 + (sub + 1) * 128, j * 512:(j + 1) * 512], vst[i][:, 0:512])], [f"fs_vst{i}"], [], f"d_vs{i}")
                dst = vn[:, sub, j * 520:(j + 1) * 520].rearrange("p (h c) -> p h c", c=65)[:, :, 0:64]
                S.op("dve", lambda e: e.tensor_copy(out=dst, in_=pb[:, :].rearrange("p (h c) -> p h c", c=64)), [pkey], [("fs_vn", sub)])
        S.op("act", lambda e: e.mul(out=spl4[:, :, 0:16, :], in_=lfcc[:, :, :, :], mul=-1.0), ["fs_lfcc"], ["fs_spl4"])
        S.op("pool", lambda e: e.memset(spl4[:, :, 16, :], 0.0), [], ["fs_spl4"])
        for seq in range(NS):
            sub, r0 = seq // 2, (seq % 2) * 64
            S.op("dve", lambda e: e.tensor_copy(out=spl4[r0:r0 + 64, seq, 16, :], in_=lfn[r0:r0 + 64, sub, :]), [("fs_lfn", sub)], ["fs_spl4"])
        flat = spl4[:, :, :, :].rearrange("p s a h -> p (s a h)")
        wflat = wexp[:, :, :, :].rearrange("p s a h -> p (s a h)")
        tflat = ttot4[:, :, :, :].rearrange("p s a h -> p (s a h)")
        NT = NS * 272
        for (c0, c1) in ((0, 512), (512, 1024), (1024, NT)):
            pb, pkey = nbk("m", MB)
            S.op("pe", lambda e: e.matmul(pb[:, 0:c1 - c0], NSTR, flat[:, c0:c1], start=True, stop=True), ["fs_spl4", "cst"], [pkey])
            S.op("dve", lambda e: e.tensor_copy(out=wflat[:, c0:c1], in_=pb[:, 0:c1 - c0]), [pkey], ["fs_wexp"])
            pb, pkey = nbk("m", MB)
            S.op("pe", lambda e: e.matmul(pb[:, 0:c1 - c0], NONE, flat[:, c0:c1], start=True, stop=True), ["fs_spl4", "cst"], [pkey])
            S.op("dve", lambda e: e.tensor_copy(out=tflat[:, c0:c1], in_=pb[:, 0:c1 - c0]), [pkey], ["fs_ttot4"])
        for a in range(15, -1, -1):
            if a == 15:
                S.op("dve", lambda e: e.tensor_copy(out=sfx4[:, :, a, :], in_=ttot4[:, :, a + 1, :]), ["fs_ttot4"], ["fs_sfx4"])
            else:
                S.op("dve", lambda e: e.tensor_tensor(out=sfx4[:, :, a, :], in0=sfx4[:, :, a + 1, :], in1=ttot4[:, :, a + 1, :], op=ALU.add), ["fs_ttot4", "fs_sfx4"], ["fs_sfx4"])
        S.op("dve", lambda e: e.tensor_tensor(out=wexp[:, :, 0:16, :], in0=wexp[:, :, 0:16, :], in1=sfx4[:, :, 0:16, :], op=ALU.add), ["fs_wexp", "fs_sfx4"], ["fs_wexp"])
        S.op("act", lambda e: e.activation(out=wflat[:, :], in_=wflat[:, :], func=AF.Exp), ["fs_wexp"], ["fs_wexp"])
        for seq in range(NS):
            sub, r0 = seq // 2, (seq % 2) * 64
            vn3 = vn[r0:r0 + 64, sub, :].rearrange("p (h c) -> p h c", c=65)
            wn3 = mkap(wexp, r0 * NS * 272 + seq * 272 + 256, [[NS * 272, 64], [1, 16], [0, 64]])
            S.op("pool", lambda e: e.tensor_tensor(out=vn3[:, :, 0:64], in0=vn3[:, :, 0:64], in1=wn3, op=ALU.mult), [("fs_vn", sub), "fs_wexp"], [("fs_vn", sub)])
            S.op("pool", lambda e: e.tensor_copy(out=vn3[:, :, 64], in_=wexp[r0:r0 + 64, seq, 16, :]), ["fs_wexp"], [("fs_vn", sub)])

        ACC = [2, 3, 4]

        def acc_of(h):
            b = ACC[h // 7]
            return pbank[b], f"pb{b}", (h % 7) * 65

        def do_T(seq, a):
            ki = rot["k"] % NST
            rot["k"] += 1
            S.dma("sp", [(kst[ki][:, :], ck_in[seq, a * 128:(a + 1) * 128, :])], [], [f"fs_kst{ki}"], f"d_ks{ki}")
            vi = rot["v"] % NST
            rot["v"] += 1
            S.dma("sp", [(vst[vi][:, :], cv_in[seq, a * 128:(a + 1) * 128, :])], [], [f"fs_vst{vi}"], f"d_vs{vi}")
            return ki, vi

        def do_X(seq, a, ki, vi):
            kt_i = rot["kt"] % 3
            rot["kt"] += 1
            for half in range(2):
                pb, pkey = nbk("m", [5, 6])
                for q in range(4):
                    kc = half * 4 + q
                    S.op("pe", lambda e: e.transpose(pb[:, q * 128:(q + 1) * 128], kst[ki][:, kc * 128:(kc + 1) * 128], ident), [f"fs_kst{ki}", "cst"], [pkey], signal=(q == 3))
                dst = kTt[kt_i][:, half * 4:(half + 1) * 4, :]
                srcp = pb[:, :].rearrange("p (q t) -> p q t", t=128)
                if half == 0:
                    S.op("act", lambda e: e.copy(out=dst, in_=srcp), [pkey], [f"fs_kTt{kt_i}"])
                else:
                    S.op("dve", lambda e: e.tensor_copy(out=dst, in_=srcp), [pkey], [f"fs_kTt{kt_i}"])
            vt_i = rot["vt"] % 3
            rot["vt"] += 1
            v3 = v65t[vt_i][:, :].rearrange("p (h c) -> p h c", c=65)
            wb3 = mkap(wexp, seq * 272 + a * 16, [[NS * 272, 128], [1, 16], [0, 64]])
            S.op("pool", lambda e: e.tensor_tensor(out=v3[:, :, 0:64], in0=vst[vi][:, :].rearrange("p (h c) -> p h c", c=64), in1=wb3, op=ALU.mult),
                 [f"fs_vst{vi}", "fs_wexp"], [f"fs_v65t{vt_i}"])
            S.op("pool", lambda e: e.tensor_copy(out=v3[:, :, 64], in_=wexp[:, seq, a, :]), ["fs_wexp"], [f"fs_v65t{vt_i}"])
            return kt_i, vt_i

        def do_S(seq, kt_i):
            qc = seq * 64
            res = []
            for g in range(2):
                pbS, skey = pbank[g], f"pb{g}"
                for j in range(8):
                    h = g * 8 + j
                    S.op("pe", lambda e: e.matmul(pbS[:, j * 64:(j + 1) * 64], kTt[kt_i][:, h // 2, :], qz[:, h, qc:qc + 64], start=True, stop=True),
                         [f"fs_kTt{kt_i}", ("fs_q", h)], [skey], signal=(j == 7))
                pi = rot["p"] % 4
                rot["p"] += 1
                S.op("act", lambda e: e.activation(out=PTs[pi][:, :], in_=pbS[:, :], func=AF.Exp, scale=0.125), [skey], [f"fs_PT{pi}"])
                res.append(pi)
            return res

        def do_PV(pis, vt_i):
            for h in range(16):
                pbO, okey, oc = acc_of(h)
                pi = pis[h // 8]
                S.op("pe", lambda e: e.matmul(pbO[0:64, oc:oc + 65], PTs[pi][:, (h % 8) * 64:(h % 8 + 1) * 64], v65t[vt_i][:, h * 65:(h + 1) * 65], start=False, stop=False, skip_group_check=True),
                     [f"fs_PT{pi}", f"fs_v65t{vt_i}"], [okey], signal=(h in (6, 13, 15)))

        for seq in range(NS):
            sub, r0 = seq // 2, (seq % 2) * 64
            qc = seq * 64
            for b in ACC:
                S.op("pe", lambda e: e.matmul(pbank[b][:, :], zeros[:, 0:128], zeros[:, :], start=True, stop=False, skip_group_check=True), ["fs_zeros"], [f"pb{b}"], signal=False)
            slots = {}
            xs = {}
            slots[0] = do_T(seq, 0)
            slots[1] = do_T(seq, 1)
            xs[0] = do_X(seq, 0, *slots[0])
            prev = None
            for a in range(16):
                if a + 2 < 16:
                    slots[a + 2] = do_T(seq, a + 2)
                if a + 1 < 16:
                    xs[a + 1] = do_X(seq, a + 1, *slots[a + 1])
                pis = do_S(seq, xs[a][0])
                if prev is not None:
                    do_PV(*prev)
                prev = (pis, xs[a][1])
            pisn = []
            for g in range(2):
                pbS, skey = pbank[g], f"pb{g}"
                for j in range(8):
                    h = g * 8 + j
                    S.op("pe", lambda e: e.matmul(pbS[r0:r0 + 64, j * 64:(j + 1) * 64], kTn[:, h // 2, qc:qc + 64], qz[:, h, qc:qc + 64], start=True, stop=True),
                         ["fs_kTn", ("fs_q", h)], [skey], signal=(j == 7))
                pi = rot["p"] % 4
                rot["p"] += 1
                S.op("act", lambda e: e.activation(out=PTs[pi][r0:r0 + 64, :], in_=pbS[r0:r0 + 64, :], func=AF.Exp, scale=0.125), [skey], [f"fs_PT{pi}"])
                pv3 = PTs[pi][r0:r0 + 64, :].rearrange("p (h c) -> p h c", c=64)
                trb = mkap(cst, r0 * CW + C_TRI + r0, [[CW, 64], [0, 8], [1, 64]])
                S.op("dve", lambda e: e.tensor_tensor(out=pv3, in0=pv3, in1=trb, op=ALU.mult), [f"fs_PT{pi}", "cst"], [f"fs_PT{pi}"])
                pisn.append(pi)
            do_PV(*prev)
            for h in range(16):
                pbO, okey, oc = acc_of(h)
                pi = pisn[h // 8]
                S.op("pe", lambda e: e.matmul(pbO[0:64, oc:oc + 65], PTs[pi][r0:r0 + 64, (h % 8) * 64:(h % 8 + 1) * 64], vn[r0:r0 + 64, sub, h * 65:(h + 1) * 65], start=False, stop=(h in (6, 13, 15)), skip_group_check=True),
                     [f"fs_PT{pi}", ("fs_vn", sub)], [okey], signal=(h in (6, 13, 15)))
            for bi, b in enumerate(ACC):
                nh = 7 if bi < 2 else 2
                h0 = bi * 7
                av = pbank[b][0:64, 0:nh * 65].rearrange("p (h c) -> p h c", c=65)
                S.op("dve", lambda e: e.reciprocal(out=rl[0:64, h0:h0 + nh], in_=av[:, :, 64]), [f"pb{b}"], ["fs_rl"])
                rb = mkap(rl, h0, [[16, 64], [1, nh], [0, 64]])
                S.op("dve", lambda e: e.tensor_tensor(out=o_n[0:64, h0 * 64:(h0 + nh) * 64].rearrange("p (h c) -> p h c", c=64), in0=av[:, :, 0:64], in1=rb, op=ALU.mult),
                     [f"pb{b}", "fs_rl"], ["fs_on"])
            pb, pkey = nbk("m", [5, 6])
            for hc in range(KC):
                S.op("pe", lambda e: e.matmul(pb[:, hc * 64:(hc + 1) * 64], o_n[0:64, hc * 128:(hc + 1) * 128], ident_bf[0:64, 0:64], start=True, stop=True),
                     ["fs_on", "fs_idbf"], [pkey], signal=(hc == KC - 1))
            S.op("act", lambda e: e.copy(out=oT[:, :, qc:qc + 64], in_=pb[:, :].rearrange("p (c q) -> p c q", q=64)), [pkey], ["fs_h"])
        for j in range(2):
            wb, wkey = wload_cols(fox_w_out, j * 512, 512)
            wv = wb[:, 0:KC * 512].rearrange("p (kc c) -> p kc c", c=512)
            for c4 in range(4):
                ncx = j * 4 + c4
                pb, pkey = nbk("m", MB)
                for kc in range(KC):
                    S.op("pe", lambda e: e.matmul(pb[:, 0:n], wv[:, kc, c4 * 128:(c4 + 1) * 128], oT[:, kc, 0:n], start=(kc == 0), stop=(kc == KC - 1)),
                         [wkey, "fs_h"], [pkey], signal=(kc == KC - 1))
                S.op("dve", lambda e: e.tensor_tensor(out=xT[:, ncx, t0:t0 + n], in0=pb[:, 0:n], in1=xT[:, ncx, t0:t0 + n], op=ALU.add),
                     [pkey, ("x", tt)], [("x", tt)], disjoint=True)
        S.barrier()
        release(nalloc)


    def fox_prompt():
        hTb = [sb(f"fp_h{p}", [128, KC, 512], BF16) for p in range(2)]
        sqq = sb("fp_sqq", [128, KC, 512], BF16)
        qz2 = sb("fp_qz2", [128, KC, 512], BF16)
        rstd = sb("fp_rstd", [128, 512], F32)
        kT = sb("fp_kT", [128, KC, 2048], BF16)
        v65 = sb("fp_v65", [128, 16, 1040], BF16)
        PT = [sb(f"fp_PT{i}", [128, 512], BF16) for i in range(3)]
        o_n = sb("fp_on", [128, 4, 1024], BF16)
        spl = sb("fp_spl", [128, 16, 16], F32)
        bias = sb("fp_bias", [128, 16, 16], F32)
        ttot = sb("fp_ttot", [128, 16, 16], F32)
        sfx = sb("fp_sfx", [128, 16, 16], F32)
        lfz = sb("fp_lfz", [128, 16], F32)
        lfst = sb("fp_lfst", [128, 16], F32)
        rl = sb("fp_rl", [128, 4], F32)
        zeros = sb("fp_zeros", [128, 512], BF16)
        ident_bf = sb("fp_idbf", [128, 128], BF16)
        cstage = [sb(f"fp_cst{i}", [128, 512], F32) for i in range(2)]
        nalloc = 2 + 5 + 3 + 10 + 2
        for nm in ("d_c0", "d_c1", "d_lf"):
            S.add_sem(nm, "dma")
        sqv = o_n[:, :, :].rearrange("p a (b c) -> p (a b) c", b=2)
        tri = cst[:, C_TRI:C_TRI + 128]
        NSTR = cst[:, C_NSTR:C_NSTR + 128]
        NONE = cst[:, C_NONE:C_NONE + 128]
        bfb = cst[:, C_BF:C_BF + 16]
        S.op("pool", lambda e: e.memset(zeros[:], 0.0), [], ["fp_zeros"])
        S.op("pool", lambda e: e.memset(v65[:], 1.0), [], [("fp_v65", kt) for kt in range(16)])
        S.op("pool", lambda e: e.memset(sqq[:], 0.0), [], [("fp_q", h_) for h_ in range(8)])
        S.op("pool", lambda e: e.memset(qz2[:], 0.0), [], [("fp_q", h_) for h_ in range(8, 16)])
        S.op("dve", lambda e: e.tensor_copy(out=ident_bf[:], in_=ident), ["cst"], ["fp_idbf"])
        rot = {"s": 0, "o": 0, "m": 0, "c": 0, "p": 0, "k": 0}

        def nbk(which, lst):
            i = lst[rot[which] % len(lst)]
            rot[which] += 1
            return pbank[i], f"pb{i}"

        def stage_slot():
            i = rot["c"] % 2
            rot["c"] += 1
            return cstage[i], f"fp_cst{i}", f"d_c{i}"

        def gen_KV(tt):
            KVB = [6, 7]
            t0, n = TILES[tt]
            p = tt % 2
            hT = hTb[p]
            hk = ("fp_h", p)

            def proj_fm(wv, wkey, c0, pb, pkey):
                for kc in range(KC):
                    S.op("pe", lambda e: e.matmul(pb[:, 0:n], wv[:, kc, c0:c0 + 128], hT[:, kc, 0:n], start=(kc == 0), stop=(kc == KC - 1)),
                         [wkey, hk], [pkey], signal=(kc == KC - 1))
                    if kc == 3:
                        yield

            def proj_tm(wv, wkey, sub, ncols, pb, pkey):
                for kc in range(KC):
                    S.op("pe", lambda e: e.matmul(pb[:, 0:ncols], hT[:, kc, sub * 128:(sub + 1) * 128], wv[:, kc, 0:ncols], start=(kc == 0), stop=(kc == KC - 1)),
                         [wkey, hk], [pkey], signal=(kc == KC - 1))
                    if kc == 3 and ncols > 16:
                        yield

            rmsnorm(tt, 2, lambda kc: hT[:, kc, 0:n], [hk], sqv, "fp_on", rstd, "fp_rstd", bank=lambda: nbk("k", KVB))
            yield
            wb, wkey = wload_cols(fox_w_in, 3072, 16)
            wv = wb[:, 0:KC * 16].rearrange("p (kc c) -> p kc c", c=16)
            for sub in range(4):
                kt = 4 * tt + sub
                pb, pkey = nbk("k", KVB)
                yield from proj_tm(wv, wkey, sub, 16, pb, pkey)
                S.op("dve", lambda e: e.tensor_tensor(out=lfz[:, :], in0=pb[:, 0:16], in1=bfb, op=ALU.add), [pkey, "cst"], ["fp_lfz"])
                S.op("act", lambda e: e.activation(out=lfz[:, :], in_=lfz[:, :], func=AF.Exp, scale=-1.0), ["fp_lfz"], ["fp_lfz"])
                S.op("dve", lambda e: e.tensor_scalar_add(lfz[:, :], lfz[:, :], 1.0), ["fp_lfz"], ["fp_lfz"])
                S.op("act", lambda e: e.activation(out=spl[:, kt, :], in_=lfz[:, :], func=AF.Ln), ["fp_lfz"], [("fp_spl", kt)])
                S.op("dve", lambda e: e.tensor_scalar(out=lfst[:, :], in0=spl[:, kt, :], scalar1=-1.0, scalar2=None, op0=ALU.mult), [("fp_spl", kt)], ["fp_lfst"])
                S.dma("sp", [(lf_out[t0 + sub * 128:t0 + (sub + 1) * 128, :], lfst[:, :])], ["fp_lfst"], [], "d_lf")
                yield
            for j in range(2):
                wb, wkey = wload_cols(fox_w_in, 1024 + j * 512, 512)
                wv = wb[:, 0:KC * 512].rearrange("p (kc c) -> p kc c", c=512)
                for c4 in range(4):
                    pb, pkey = nbk("k", KVB)
                    yield from proj_fm(wv, wkey, c4 * 128, pb, pkey)
                    S.op("dve", lambda e: e.tensor_copy(out=kT[:, j * 4 + c4, t0:t0 + n], in_=pb[:, 0:n]), [pkey], [("fp_kT", 4 * tt + q) for q in range(4)], disjoint=True)
                    yield
                for sub in range(4):
                    pb, pkey = nbk("k", KVB)
                    yield from proj_tm(wv, wkey, sub, 512, pb, pkey)
                    st, stk, sem = stage_slot()
                    S.op("dve", lambda e: e.tensor_copy(out=st[:, 0:512], in_=pb[:, :]), [pkey], [stk])
                    S.dma("sp", [(k_out[t0 + sub * 128:t0 + (sub + 1) * 128, j * 512:(j + 1) * 512], st[:, 0:512])], [stk], [], sem)
                    yield
            for j in range(2):
                wb, wkey = wload_cols(fox_w_in, 2048 + j * 512, 512)
                wv = wb[:, 0:KC * 512].rearrange("p (kc c) -> p kc c", c=512)
                for sub in range(4):
                    kt = 4 * tt + sub
                    pb, pkey = nbk("k", KVB)
                    yield from proj_tm(wv, wkey, sub, 512, pb, pkey)
                    st, stk, sem = stage_slot()
                    S.op("dve", lambda e: e.tensor_copy(out=st[:, 0:512], in_=pb[:, :]), [pkey], [stk])
                    S.dma("sp", [(v_out[t0 + sub * 128:t0 + (sub + 1) * 128, j * 512:(j + 1) * 512], st[:, 0:512])], [stk], [], sem)
                    dst = v65[:, kt, j * 520:(j + 1) * 520].rearrange("p (h c) -> p h c", c=65)[:, :, 0:64]
                    S.op("dve", lambda e: e.tensor_copy(out=dst, in_=pb[:, :].rearrange("p (h c) -> p h c", c=64)), [pkey], [("fp_v65", kt)])
                    yield

        def compute_bias(nk):
            pb, pkey = pbank[4], "pb4"
            rhs = spl[:, 0:nk, :]
            S.op("pe", lambda e: e.matmul(pb[:, 0:nk * 16], NSTR, rhs, start=True, stop=True), [("fp_spl", a) for a in range(nk)] + ["cst"], [pkey])
            S.op("dve", lambda e: e.tensor_copy(out=bias[:, 0:nk, :], in_=pb[:, 0:nk * 16].rearrange("p (a h) -> p a h", h=16)), [pkey], ["fp_bias"])
            S.op("pe", lambda e: e.matmul(pb[:, 0:nk * 16], NONE, rhs, start=True, stop=True), [("fp_spl", a) for a in range(nk)] + ["cst"], [pkey])
            S.op("dve", lambda e: e.tensor_copy(out=ttot[:, 0:nk, :], in_=pb[:, 0:nk * 16].rearrange("p (a h) -> p a h", h=16)), [pkey], ["fp_ttot"])
            for a in range(nk - 2, -1, -1):
                if a == nk - 2:
                    S.op("dve", lambda e: e.tensor_copy(out=sfx[:, a, :], in_=ttot[:, a + 1, :]), ["fp_ttot"], ["fp_sfx"])
                else:
                    S.op("dve", lambda e: e.tensor_tensor(out=sfx[:, a, :], in0=sfx[:, a + 1, :], in1=ttot[:, a + 1, :], op=ALU.add), ["fp_ttot", "fp_sfx"], ["fp_sfx"])
            S.op("dve", lambda e: e.tensor_tensor(out=bias[:, 0:nk - 1, :], in0=bias[:, 0:nk - 1, :], in1=sfx[:, 0:nk - 1, :], op=ALU.add), ["fp_bias", "fp_sfx"], ["fp_bias"])

        SB_P = [0, 1, 5]

        def attend_prompt(tt, hook, period):
            nk = 4 * tt + 4
            items = []
            for h in range(16):
                for a in range(nk):
                    j = a - 4 * tt
                    items.append(dict(h=h, a=a, c0=(j * 128 if j > 0 else 0), diag=(j >= 0), first=(a == 0), last=(a == nk - 1)))
            n_it = len(items)
            Sres = [None] * n_it
            Eres = [None] * n_it
            obank = {}

            def do_S(i):
                it = items[i]
                h, a, c0 = it["h"], it["a"], it["c0"]
                pbS, skey = nbk("s", SB_P)
                qb_, qk_ = (sqq if h < 8 else qz2), ("fp_q", h)
                S.op("pe", lambda e: e.matmul(pbS[:, c0:512], kT[:, h // 2, a * 128:(a + 1) * 128], qb_[:, h % 8, c0:512], start=True, stop=True),
                     [("fp_kT", a), qk_], [skey])
                Sres[i] = (pbS, skey)

            def do_E(i):
                it = items[i]
                h, a, c0 = it["h"], it["a"], it["c0"]
                pbS, skey = Sres[i]
                pi = rot["p"] % 3
                rot["p"] += 1
                pt, ptk = PT[pi], f"fp_PT{pi}"
                S.op("act", lambda e: e.activation(out=pt[:, c0:512], in_=pbS[:, c0:512], func=AF.Exp, bias=bias[:, a, h:h + 1], scale=0.125),
                     [skey, "fp_bias"], [ptk])
                if it["diag"]:
                    S.op("dve", lambda e: e.tensor_tensor(out=pt[:, c0:c0 + 128], in0=pt[:, c0:c0 + 128], in1=tri, op=ALU.mult), [ptk, "cst"], [ptk])
                Eres[i] = (pt, ptk)

            def do_PV(i):
                it = items[i]
                h, a, c0 = it["h"], it["a"], it["c0"]
                pt, ptk = Eres[i]
                if it["first"]:
                    obank[h] = nbk("o", [2, 3])
                    pbO, okey = obank[h]
                    S.op("pe", lambda e: e.matmul(pbO[:, :], zeros[:, 0:128], zeros[:, :], start=True, stop=False, skip_group_check=True), ["fp_zeros"], [okey], signal=False)
                pbO, okey = obank[h]
                for qt in range(c0 // 128, 4):
                    is_last = (it["last"] and qt == 3)
                    S.op("pe", lambda e: e.matmul(pbO[:, qt * 128:qt * 128 + 65], pt[:, qt * 128:(qt + 1) * 128], v65[:, a, h * 65:(h + 1) * 65], start=False, stop=is_last, skip_group_check=True),
                         [ptk, ("fp_v65", a)], [okey], signal=(qt == 3))
                if it["last"]:
                    ov = pbO[:, :].rearrange("p (q c) -> p q c", c=128)
                    S.op("dve", lambda e: e.reciprocal(out=rl[:, 0:4], in_=ov[:, 0:4, 64]), [okey], ["fp_rl"])
                    rb = mkap(rl, 0, [[4, 128], [1, 4], [0, 64]])
                    S.op("dve", lambda e: e.tensor_tensor(out=o_n[:, 0:4, h * 64:(h + 1) * 64], in0=ov[:, 0:4, 0:64], in1=rb, op=ALU.mult),
                         [okey, "fp_rl"], ["fp_on"])

            do_S(0)
            if n_it > 1:
                do_S(1)
            for i in range(n_it):
                do_E(i)
                if i + 2 < n_it:
                    do_S(i + 2)
                if i >= 1:
                    do_PV(i - 1)
                if hook is not None and i % period == period - 1:
                    hook()
            do_PV(n_it - 1)

        tiles = list(cfg.get("fox_tiles", [0, 1, 2, 3]))
        g0 = gen_KV(tiles[0])
        for _ in g0:
            pass
        for idx, tt in enumerate(tiles):
            t0, n = TILES[tt]
            p = tt % 2
            hT = hTb[p]
            hk = ("fp_h", p)
            for j in range(2):
                wb, wkey = wload_cols(fox_w_in, j * 512, 512)
                wv = wb[:, 0:KC * 512].rearrange("p (kc c) -> p kc c", c=512)
                for c4 in range(4):
                    pb, pkey = nbk("m", [5, 6, 7])
                    for kc in range(KC):
                        S.op("pe", lambda e: e.matmul(pb[:, 0:n], wv[:, kc, c4 * 128:(c4 + 1) * 128], hT[:, kc, 0:n], start=(kc == 0), stop=(kc == KC - 1)),
                             [wkey, hk], [pkey], signal=(kc == KC - 1))
                    hcq = j * 4 + c4
                    for hp in range(2):
                        hh = 2 * hcq + hp
                        qb_, qk_ = (sqq if hh < 8 else qz2), ("fp_q", hh)
                        pr_ = slice(hp * 64, hp * 64 + 64)
                        if hp == 0:
                            S.op("act", lambda e: e.copy(out=qb_[pr_, hh % 8, 0:n], in_=pb[pr_, 0:n]), [pkey], [qk_])
                        else:
                            S.op("dve", lambda e: e.tensor_copy(out=qb_[pr_, hh % 8, 0:n], in_=pb[pr_, 0:n]), [pkey], [qk_])
            compute_bias(4 * tt + 4)
            nxt = tiles[idx + 1] if idx + 1 < len(tiles) else None
            gk = gen_KV(nxt) if nxt is not None else None
            state = {"alive": gk is not None}

            def hook():
                if state["alive"]:
                    try:
                        next(gk)
                    except StopIteration:
                        state["alive"] = False

            n_items = 16 * (4 * tt + 4)
            attend_prompt(tt, hook if gk is not None else None, max(1, n_items // 58))
            while state["alive"]:
                hook()
            oT = hT
            for hc in range(KC):
                pb, pkey = nbk("m", [5, 6, 7])
                for qt in range(4):
                    S.op("pe", lambda e: e.matmul(pb[:, qt * 128:(qt + 1) * 128], o_n[:, qt, hc * 128:(hc + 1) * 128], ident_bf[:], start=True, stop=True),
                         ["fp_on", "fp_idbf"], [pkey], signal=(qt == 3))
                if hc % 2 == 0:
                    S.op("act", lambda e: e.copy(out=oT[:, hc, 0:512], in_=pb[:, :]), [pkey], [hk])
                else:
                    S.op("dve", lambda e: e.tensor_copy(out=oT[:, hc, 0:512], in_=pb[:, :]), [pkey], [hk])
            for j in range(2):
                wb, wkey = wload_cols(fox_w_out, j * 512, 512)
                wv = wb[:, 0:KC * 512].rearrange("p (kc c) -> p kc c", c=512)
                for c4 in range(4):
                    ncx = j * 4 + c4
                    pb, pkey = nbk("m", [5, 6, 7])
                    for kc in range(KC):
                        S.op("pe", lambda e: e.matmul(pb[:, 0:n], wv[:, kc, c4 * 128:(c4 + 1) * 128], oT[:, kc, 0:n], start=(kc == 0), stop=(kc == KC - 1)),
                             [wkey, hk], [pkey], signal=(kc == KC - 1))
                    S.op("dve", lambda e: e.tensor_tensor(out=xT[:, ncx, t0:t0 + n], in0=pb[:, 0:n], in1=xT[:, ncx, t0:t0 + n], op=ALU.add),
                         [pkey, ("x", tt)], [("x", tt)], disjoint=True)
        S.barrier()
        release(nalloc)

    if cfg.get("fox", True):
        fox_prompt()
        fox_sample()

    if cfg.get("ffn1", True):
        ffn_phase(1)

    yT = sb("yT", [128, KC, 512], F32)
    sq = sb("fin_sq", [128, KC, 512], BF16)
    rstd = sb("fin_rstd", [128, 512], F32)
    yst = [sb(f"yst{i}", [128, D], F32) for i in range(2)]
    S.add_sem("d_y0", "dma")
    S.add_sem("d_y1", "dma")
    oc = 0
    for tt in range(5):
        t0, n = TILES[tt]
        rmsnorm(tt, 4, lambda kc: yT[:, kc, 0:n], ["yT"], sq, "fin_sq", rstd, "fin_rstd")
        for s in range(n // 128):
            st, skey = yst[oc % 2], f"yst{oc % 2}"
            for half in range(2):
                pb, pkey = next_bank()
                for j in range(4):
                    kc = half * 4 + j
                    S.op("pe", lambda e, pb=pb, j=j, kc=kc, s=s: e.transpose(pb[:, j * 128:(j + 1) * 128], yT[:, kc, s * 128:(s + 1) * 128], ident),
                         ["yT", "cst"], [pkey], signal=(j == 3))
                if half == 0:
                    S.op("act", lambda e, pb=pb, st=st: e.copy(out=st[:, 0:512], in_=pb[:]), [pkey], [skey])
                else:
                    S.op("dve", lambda e, pb=pb, st=st: e.tensor_copy(out=st[:, 512:1024], in_=pb[:]), [pkey], [skey])
            S.dma("sp", [(y_out[t0 + s * 128:t0 + (s + 1) * 128, :], st[:])], [skey], [], f"d_y{oc % 2}")
            oc += 1

    S.barrier()
    release(len(ctxs))
    S.close()
    return nc


_CONST_CACHE = {}


def _consts(inputs):
    c = np.zeros((128, CW), np.float32)
    c[:, C_ID:C_ID + 128] = np.eye(128, dtype=np.float32)
    s = np.arange(128)[:, None]
    t = np.arange(128)[None, :]
    same = (s // 64) == (t // 64)
    c[:, C_UINC:C_UINC + 128] = np.where(same & (s <= t), -1.0 / 16.0, 0.0)
    c[:, C_USTR:C_USTR + 128] = np.where(same & (s > t), -1.0 / 16.0, 0.0)
    c[:, C_MASK2:C_MASK2 + 128] = np.where(same & (s <= t), 1.0, 0.0)
    c[:, C_TRI:C_TRI + 128] = np.where(s <= t, 1.0, 0.0)
    c[:, C_NSTR:C_NSTR + 128] = np.where(s > t, -1.0, 0.0)
    c[:, C_NONE:C_NONE + 128] = -1.0
    vecs = [inputs["norm_mix"][0], inputs["norm_ffn"][0], inputs["norm_mix"][1], inputs["norm_ffn"][1],
            inputs["norm_final"], inputs["gla_norm"][0]]
    for i, v in enumerate(vecs):
        c[:, C_NW + i * 8:C_NW + (i + 1) * 8] = np.asarray(v, np.float32).reshape(8, 128).T
    c[0:16, C_WG2:C_WG2 + 512] = inputs["gla_w_g2"][0]
    c[16, C_WG2:C_WG2 + 512] = inputs["gla_b_g"][0]
    c[:, C_BF:C_BF + 16] = np.broadcast_to(np.asarray(inputs["fox_b_f"][0], np.float32)[None, :], (128, 16))
    return c


def make_in_maps(inputs, ncores=NCORES):
    f = lambda a: np.ascontiguousarray(np.asarray(a, dtype=np.float32))
    cst = _consts(inputs)
    maps = []
    for c in range(ncores):
        xin = np.concatenate([f(inputs["x_prompt"][c]), f(inputs["x_sample"][NS * c:NS * (c + 1)]).reshape(NS * DSEQ, D)], axis=0)
        maps.append({
            "xin": np.ascontiguousarray(xin),
            "cst": cst,
            "gla_w_in": f(inputs["gla_w_in"][0]),
            "gla_w_out": f(inputs["gla_w_out"][0]),
            "fox_w_in": f(inputs["fox_w_in"][0]),
            "fox_w_out": f(inputs["fox_w_out"][0]),
            "ffn_w_in": f(inputs["ffn_w_in"]),
            "ffn_w_down": f(inputs["ffn_w_down"]),
            "state_in": f(inputs["state_gla"][0, NS * c:NS * (c + 1)]),
            "ck_in": f(inputs["cache_fox_k"][0, NS * c:NS * (c + 1)]).reshape(NS, SEQ, D),
            "cv_in": f(inputs["cache_fox_v"][0, NS * c:NS * (c + 1)]).reshape(NS, SEQ, D),
            "clf_in": f(inputs["cache_fox_logf"][0, NS * c:NS * (c + 1)]),
        })
    return maps


def kernel(**inputs):
    nc = build_program({})
    maps = make_in_maps(inputs)
    res = run_bass_kernel_spmd(nc, maps, core_ids=list(range(NCORES)))
    R = res.results
    B = NCORES
    y_prompt = np.stack([R[c]["y_out"][:SEQ] for c in range(B)], 0)
    y_sample = np.concatenate([R[c]["y_out"][SEQ:].reshape(NS, DSEQ, D) for c in range(B)], 0)
    gla_state_p = np.stack([R[c]["gsp_out"] for c in range(B)], 0)[None]
    gla_state_s = np.concatenate([R[c]["gss_out"] for c in range(B)], 0)[None]
    fox_k_p = np.stack([R[c]["k_out"][:SEQ].reshape(SEQ, 16, 64) for c in range(B)], 0)[None]
    fox_v_p = np.stack([R[c]["v_out"][:SEQ].reshape(SEQ, 16, 64) for c in range(B)], 0)[None]
    fox_logf_p = np.stack([R[c]["lf_out"][:SEQ] for c in range(B)], 0)[None]
    fox_k_s = np.concatenate([R[c]["k_out"][SEQ:].reshape(NS, DSEQ, 16, 64) for c in range(B)], 0)[None]
    fox_v_s = np.concatenate([R[c]["v_out"][SEQ:].reshape(NS, DSEQ, 16, 64) for c in range(B)], 0)[None]
    fox_logf_s = np.concatenate([R[c]["lf_out"][SEQ:].reshape(NS, DSEQ, 16) for c in range(B)], 0)[None]
    outs = (y_prompt, y_sample, gla_state_p, fox_k_p, fox_v_p, fox_logf_p, gla_state_s, fox_k_s, fox_v_s, fox_logf_s)
    return tuple(np.ascontiguousarray(o, dtype=np.float32) for o in outs)
```

```python
import numpy as np
import concourse.bass as bass
import concourse.mybir as mybir
from concourse.bass_utils import run_bass_kernel_spmd

F32 = mybir.dt.float32
BF16 = mybir.dt.bfloat16
AF = mybir.ActivationFunctionType
ALU = mybir.AluOpType

D = 1024
KC = 8
SEQ = 2048
NS = 4
DSEQ = 64
NTOK = SEQ + NS * DSEQ
TILES = [(0, 512), (512, 512), (1024, 512), (1536, 512), (2048, 256)]
DFF = 2816
FC = 22
GLA_IN = 3088
EPS = 1e-6
NCORES = 8

C_ID, C_UINC, C_USTR, C_MASK2, C_TRI, C_NSTR, C_NONE = 0, 128, 256, 384, 512, 640, 768
C_NW = 896
C_WG2 = 944
C_BF = 1456
CW = 1472


class Sched:
    def __init__(self, nc):
        self.nc = nc
        self.eng = {"pe": nc.tensor, "act": nc.scalar, "dve": nc.vector, "pool": nc.gpsimd, "sp": nc.sync}
        self.semh = {}
        self.cnt = {}
        self.waited = {e: {} for e in self.eng}
        self.lastw = {}
        self.readers = {}
        self.src_of = {}
        self._stack = []

    def add_sem(self, name, src):
        cm = self.nc.semaphore(name)
        h = cm.__enter__()
        self._stack.append(cm)
        self.semh[name] = h
        self.cnt[name] = 0
        self.src_of[name] = src

    def close(self):
        for cm in reversed(self._stack):
            cm.__exit__(None, None, None)

    def _wait(self, eng, reads, writes, disjoint=False):
        need = {}

        why = {}

        def add(tok, kind, key=None):
            sem, val = tok
            why[(sem, val)] = (key, kind)
            src = self.src_of[sem]
            if src == eng and eng == "pe":
                return
            if need.get(sem, 0) < val:
                need[sem] = val

        for k in reads:
            t = self.lastw.get(k)
            if t is not None:
                add(t, "raw", k)
        for k in writes:
            t = self.lastw.get(k)
            if t is not None and not (disjoint and self.src_of[t[0]] == eng):
                add(t, "waw", k)
            for sem, val in self.readers.get(k, {}).items():
                add((sem, val), "war", k)
        for sem, val in need.items():
            if self.waited[eng].get(sem, 0) >= val:
                continue
            assert self.cnt[sem] >= val, f"dependency on unsignaled op: {sem} {val} > {self.cnt[sem]} {why.get((sem, val))}"
            self.eng[eng].wait_ge(self.semh[sem], val)
            self.waited[eng][sem] = val

    def _record(self, tok, reads, writes):
        sem, val = tok
        for k in reads:
            r = self.readers.setdefault(k, {})
            if r.get(sem, 0) < val:
                r[sem] = val
        for k in writes:
            self.lastw[k] = tok
            self.readers[k] = {}

    def op(self, eng, fn, reads=(), writes=(), signal=True, disjoint=False):
        pk = [k for k in reads if isinstance(k, str) and k.startswith("pb")]
        if pk:
            reads = [k for k in reads if k not in pk]
            writes = list(writes) + [k for k in pk if k not in writes]
        self._wait(eng, reads, writes, disjoint)
        ins = fn(self.eng[eng])
        sem = "c_" + eng
        if signal:
            self.cnt[sem] += 1
            ins.then_inc(self.semh[sem], 1)
            tok = (sem, self.cnt[sem])
        else:
            tok = (sem, self.cnt[sem] + 1)
        self._record(tok, reads, writes)
        return tok

    def dma(self, queue, pairs, reads, writes, sem):
        self._wait(queue, reads, writes)
        for (o, i) in pairs:
            ins = self.eng[queue].dma_start(out=o, in_=i)
            ins.then_inc(self.semh[sem], 16)
            self.cnt[sem] += 16
        tok = (sem, self.cnt[sem])
        self._record(tok, reads, writes)
        return tok

    def barrier(self, engines=("pe", "act", "dve", "pool", "sp")):
        for e in engines:
            for sem, c in self.cnt.items():
                if c > 0 and self.waited[e].get(sem, 0) < c:
                    if sem.startswith("d_w"):
                        continue
                    if self.src_of[sem] == e and e == "pe":
                        continue
                    self.eng[e].wait_ge(self.semh[sem], c)
                    self.waited[e][sem] = c


def mkap(t, off, pat):
    return bass.AP(t.tensor if hasattr(t, "tensor") else t, off, pat)


def build_program(cfg):
    nc = bass.Bass("TRN2", target_bir_lowering=False, dynamic_dma_scratch_size=4096)
    S = Sched(nc)
    for e in ("pe", "act", "dve", "pool"):
        S.add_sem("c_" + e, e)

    def dram_in(name, shape):
        return nc.dram_tensor(name, shape, F32, kind="ExternalInput").ap()

    def dram_out(name, shape):
        return nc.dram_tensor(name, shape, F32, kind="ExternalOutput").ap()

    xin = dram_in("xin", [NTOK, D])
    cst_d = dram_in("cst", [128, CW])
    gla_w_in = dram_in("gla_w_in", [D, GLA_IN])
    gla_w_out = dram_in("gla_w_out", [D, D])
    fox_w_in = dram_in("fox_w_in", [D, GLA_IN])
    fox_w_out = dram_in("fox_w_out", [D, D])
    ffn_w_in = dram_in("ffn_w_in", [2, D, 2 * DFF])
    ffn_w_down = dram_in("ffn_w_down", [2, DFF, D])
    state_in = dram_in("state_in", [NS, 4, 128, 256])
    ck_in = dram_in("ck_in", [NS, SEQ, D])
    cv_in = dram_in("cv_in", [NS, SEQ, D])
    clf_in = dram_in("clf_in", [NS, SEQ, 16])

    y_out = dram_out("y_out", [NTOK, D])
    gsp_out = dram_out("gsp_out", [4, 128, 256])
    gss_out = dram_out("gss_out", [NS, 4, 128, 256])
    k_out = dram_out("k_out", [NTOK, D])
    v_out = dram_out("v_out", [NTOK, D])
    lf_out = dram_out("lf_out", [NTOK, 16])

    ctxs = []

    def sb(name, shape, dt):
        cm = nc.sbuf_tensor(name, shape, dt)
        t = cm.__enter__()
        ctxs.append(cm)
        return t

    def ps(name, shape, dt=F32):
        cm = nc.psum_tensor(name, shape, dt)
        t = cm.__enter__()
        ctxs.append(cm)
        return t

    def release(n):
        for _ in range(n):
            ctxs.pop().__exit__(None, None, None)

    xT = sb("xT", [128, KC, NTOK], F32)
    cst = sb("cst_sb", [128, CW], F32)
    ones_bf = sb("ones_bf", [128, 128], BF16)
    NWB = 2
    WSLOT = 4096
    wbuf = [sb(f"wbuf{i}", [128, WSLOT], BF16) for i in range(NWB)]
    for i in range(NWB):
        S.add_sem(f"d_w{i}", "dma")
    S.add_sem("d_cst", "dma")
    S.add_sem("d_out", "dma")
    for i in range(2):
        S.add_sem(f"d_st{i}", "dma")
    wstate = {"i": 0}

    ident = cst[:, C_ID:C_ID + 128]

    def nwcol(idx, kc):
        return cst[:, C_NW + idx * 8 + kc:C_NW + idx * 8 + kc + 1]

    S.dma("sp", [(cst[:], cst_d[:, :])], [], ["cst"], "d_cst")
    S.op("pool", lambda e: e.memset(ones_bf[:], 1.0), [], ["ones_bf"])

    prefetched = {}

    def wload(pairs_fn, nelem, tag=None):
        if tag is not None and tag in prefetched:
            return prefetched.pop(tag)
        i = wstate["i"] % NWB
        wstate["i"] += 1
        key = f"wbuf{i}"
        pairs = pairs_fn(wbuf[i])
        S.dma("pool", pairs, [], [key], f"d_w{i}")
        return wbuf[i], key

    def wload_cols(W2d, c0, ncols, kcn=KC, rows0=0):
        def f(buf):
            src = W2d[rows0:rows0 + kcn * 128, c0:c0 + ncols].rearrange("(kc p) c -> p kc c", p=128)
            dst = buf[:, 0:kcn * ncols].rearrange("p (kc c) -> p kc c", c=ncols)
            return [(dst, src)]
        return wload(f, kcn * ncols, tag=("cols", id(W2d), c0, ncols, rows0))

    def prefetch_cols(W2d, c0, ncols):
        tag = ("cols", id(W2d), c0, ncols, 0)
        r = wload_cols(W2d, c0, ncols)
        prefetched[tag] = r

    def gu_pairs(layer, fs):
        Win_ = ffn_w_in[layer]

        def f(buf):
            dst = buf[:, 0:KC * 512].rearrange("p (kc c) -> p kc c", c=512)
            s1 = Win_[:, fs * 256:(fs + 1) * 256].rearrange("(kc p) c -> p kc c", p=128)
            s2 = Win_[:, DFF + fs * 256:DFF + (fs + 1) * 256].rearrange("(kc p) c -> p kc c", p=128)
            return [(dst[:, :, 0:256], s1), (dst[:, :, 256:512], s2)]
        return f

    def prefetch_gu(layer, fs):
        r = wload(gu_pairs(layer, fs), KC * 512)
        prefetched[("gu", layer, fs)] = r

    pbank = [ps(f"pb{i}", [128, 512]) for i in range(8)]
    pstate = {"i": 0}

    def next_bank(lo=0, hi=8):
        i = lo + (pstate["i"] % (hi - lo))
        pstate["i"] += 1
        return pbank[i], f"pb{i}"

    NXST = 6
    xst = [sb(f"xst{i}", [128, D], F32) for i in range(NXST)]
    for i in range(2, NXST):
        S.add_sem(f"d_st{i}", "dma")
    for s in range(NTOK // 128):
        st, skey = xst[s % NXST], f"xst{s % NXST}"
        S.dma("sp", [(st[:], xin[s * 128:(s + 1) * 128, :])], [], [skey], f"d_st{s % NXST}")
        for half in range(2):
            pb, pkey = next_bank()
            for j in range(4):
                kc = half * 4 + j
                S.op("pe", lambda e, pb=pb, j=j, kc=kc, st=st: e.transpose(pb[:, j * 128:(j + 1) * 128], st[:, kc * 128:(kc + 1) * 128], ident),
                     [skey, "cst"], [pkey], signal=(j == 3))
            dst = xT[:, half * 4:(half + 1) * 4, s * 128:(s + 1) * 128]
            src = pb[:].rearrange("p (j t) -> p j t", t=128)
            tt = min(s * 128 // 512, 4)
            eng = "act" if half == 0 else "dve"
            if eng == "act":
                S.op("act", lambda e, dst=dst, src=src: e.copy(out=dst, in_=src), [pkey], [("x", tt)])
            else:
                S.op("dve", lambda e, dst=dst, src=src: e.tensor_copy(out=dst, in_=src), [pkey], [("x", tt)])

    def rmsnorm(tt, nidx, out_fn, out_keys, sq, sqkey, rstd, rkey, bank=None):
        t0, n = TILES[tt]
        S.op("act", lambda e: e.activation(out=sq[:, :, 0:n], in_=xT[:, :, t0:t0 + n], func=AF.Square),
             [("x", tt)], [sqkey])
        pb, pkey = bank() if bank is not None else next_bank()
        for kc in range(KC):
            S.op("pe", lambda e, kc=kc: e.matmul(pb[:, 0:n], ones_bf[:], sq[:, kc, 0:n], start=(kc == 0), stop=(kc == KC - 1)),
                 [sqkey, "ones_bf"], [pkey], signal=(kc == KC - 1))
        S.op("dve", lambda e: e.tensor_scalar(out=rstd[:, 0:n], in0=pb[:, 0:n], scalar1=1.0 / D, scalar2=EPS, op0=ALU.mult, op1=ALU.add),
             [pkey], [rkey])
        S.op("act", lambda e: e.activation(out=rstd[:, 0:n], in_=rstd[:, 0:n], func=AF.Ln), [rkey], [rkey])
        S.op("act", lambda e: e.activation(out=rstd[:, 0:n], in_=rstd[:, 0:n], func=AF.Exp, scale=-0.5), [rkey], [rkey])
        for kc in range(KC):
            S.op("dve", lambda e, kc=kc: e.scalar_tensor_tensor(out=out_fn(kc), in0=xT[:, kc, t0:t0 + n], scalar=nwcol(nidx, kc),
                                                               in1=rstd[:, 0:n], op0=ALU.mult, op1=ALU.mult),
                 [("x", tt), rkey, "cst"], out_keys, disjoint=True)

    def ffn_phase(layer):
        nidx = 1 if layer == 0 else 3
        groups = [[0, 1], [2, 3, 4]]
        GT = 1280
        hg = sb(f"ffn_h{layer}", [128, KC, GT], BF16)
        act = sb(f"ffn_act{layer}", [128, FC, GT], BF16)
        sq = sb(f"ffn_sq{layer}", [128, KC, 512], BF16)
        rstd = sb(f"ffn_rstd{layer}", [128, 512], F32)
        sg = [sb(f"ffn_sg{i}_{layer}", [128, 512], BF16) for i in range(2)]
        Win = ffn_w_in[layer]
        Wdn = ffn_w_down[layer]
        for g in groups:
            offs = {}
            o = 0
            for tt in g:
                offs[tt] = o
                o += TILES[tt][1]
            for tt in g:
                n = TILES[tt][1]
                rmsnorm(tt, nidx, lambda kc, tt=tt, n=n: hg[:, kc, offs[tt]:offs[tt] + n], [("ffn_h", tt)], sq, "ffn_sq", rstd, "ffn_rstd")
            for fs in range(FC // 2):
                wb, wkey = wload(gu_pairs(layer, fs), KC * 512, tag=(("gu", layer, fs) if g is groups[0] else None))
                wv = wb[:, 0:KC * 512].rearrange("p (kc c) -> p kc c", c=512)
                for tt in g:
                    n = TILES[tt][1]
                    ho = offs[tt]
                    for fc2 in range(2):
                        fch = fs * 2 + fc2
                        pg, pgk = next_bank()
                        for kc in range(KC):
                            S.op("pe", lambda e, kc=kc, pg=pg, fc2=fc2: e.matmul(pg[:, 0:n], wv[:, kc, fc2 * 128:(fc2 + 1) * 128], hg[:, kc, ho:ho + n],
                                                                               start=(kc == 0), stop=(kc == KC - 1)),
                                 [wkey, ("ffn_h", tt)], [pgk], signal=(kc == KC - 1))
                        pu, puk = next_bank()
                        for kc in range(KC):
                            S.op("pe", lambda e, kc=kc, pu=pu, fc2=fc2: e.matmul(pu[:, 0:n], wv[:, kc, 256 + fc2 * 128:256 + (fc2 + 1) * 128], hg[:, kc, ho:ho + n],
                                                                               start=(kc == 0), stop=(kc == KC - 1)),
                                 [wkey, ("ffn_h", tt)], [puk], signal=(kc == KC - 1))
                        sgi = fch % 2
                        S.op("act", lambda e, pg=pg, sgi=sgi: e.activation(out=sg[sgi][:, 0:n], in_=pg[:, 0:n], func=AF.Silu), [pgk], [f"ffn_sg{sgi}"])
                        S.op("dve", lambda e, pu=pu, sgi=sgi, fch=fch: e.tensor_tensor(out=act[:, fch, ho:ho + n], in0=pu[:, 0:n], in1=sg[sgi][:, 0:n], op=ALU.mult),
                             [puk, f"ffn_sg{sgi}"], [("ffn_act", tt, fch)])
            for ns in range(KC):
                def f(buf, ns=ns):
                    dst = buf[:, 0:FC * 128].rearrange("p (fc c) -> p fc c", c=128)
                    src = Wdn[:, ns * 128:(ns + 1) * 128].rearrange("(fc p) c -> p fc c", p=128)
                    return [(dst, src)]
                wb, wkey = wload(f, FC * 128)
                wv = wb[:, 0:FC * 128].rearrange("p (fc c) -> p fc c", c=128)
                for tt in g:
                    t0, n = TILES[tt]
                    ho = offs[tt]
                    pd, pdk = next_bank()
                    for fch in range(FC):
                        S.op("pe", lambda e, fch=fch, pd=pd: e.matmul(pd[:, 0:n], wv[:, fch, :], act[:, fch, ho:ho + n], start=(fch == 0), stop=(fch == FC - 1)),
                             [wkey, ("ffn_act", tt, fch)], [pdk], signal=(fch == FC - 1))
                    S.op("dve", lambda e, pd=pd, ns=ns: e.tensor_tensor(out=xT[:, ns, t0:t0 + n], in0=pd[:, 0:n], in1=xT[:, ns, t0:t0 + n], op=ALU.add),
                         [pdk, ("x", tt)], [("x", tt)], disjoint=True)
        if layer == 0 and all(cfg.get(k_, True) for k_ in ("gla", "fox", "ffn0", "ffn1")):
            prefetch_cols(fox_w_in, 3072, 16)
            prefetch_cols(fox_w_in, 1024, 512)
        S.barrier()
        release(6)

    release(NXST)
    if all(cfg.get(k_, True) for k_ in ("gla", "fox", "ffn0", "ffn1")):
        prefetch_cols(gla_w_in, 3072, 16)
        prefetch_cols(gla_w_in, 1024, 512)
    S.barrier()


    def gla_phase():
        hT = sb("g_h", [128, KC, 512], BF16)
        sq = sb("g_sq", [128, 4, 512], BF16)
        yT = sb("g_yT", [128, KC, 512], BF16)
        rstd = sb("g_rstd", [128, 512], F32)
        glaug = sb("g_glaug", [17, 512], F32)
        sp = sb("g_sp", [128, 4, 512], F32)
        eb = sb("g_eb", [128, 4, 512], BF16)
        enb = sb("g_enb", [128, 4, 512], BF16)
        edT = sb("g_edT", [128, 4, 512], BF16)
        dec = [sb(f"g_dec{p}", [128, 4, 8], F32) for p in range(2)]
        qeT = [sb(f"g_qeT{p}", [128, 4, 512], BF16) for p in range(2)]
        keT = [sb(f"g_keT{p}", [128, 4, 512], BF16) for p in range(2)]
        kd = [sb(f"g_kd{p}", [128, 4, 512], BF16) for p in range(2)]
        vt = [sb(f"g_v{p}", [128, 4, 1024], BF16) for p in range(2)]
        sr = [sb(f"g_sr{p}", [128, KC, 512], BF16) for p in range(2)]
        Sf = sb("g_Sf", [128, 2, 4, 256], F32)
        Sb = sb("g_Sb", [128, 2, 4, 256], BF16)
        ATm = [sb(f"g_AT{h}", [128, 128], BF16) for h in range(4)]
        obuf = [sb(f"g_obuf{p}", [128, 4, 256], BF16) for p in range(2)]
        osq = sb("g_osq", [128, 4, 256], BF16)
        orstd = sb("g_orstd", [128, 4, 128], F32)
        r3 = sb("g_r3", [128, 8, 128], BF16)
        nalloc = 9 + 10 + 4 + 4 + 5
        S.add_sem("d_gs0", "dma")
        S.add_sem("d_gs1", "dma")
        rot = {"p": 0, "m": 0}
        PB_P = [5, 6, 7]

        def nbp():
            i = PB_P[rot["p"] % 3]
            rot["p"] += 1
            return pbank[i], f"pb{i}"

        Uinc = cst[:, C_UINC:C_UINC + 128]
        Ustr = cst[:, C_USTR:C_USTR + 128]
        mask2 = cst[:, C_MASK2:C_MASK2 + 128]
        wg2 = cst[0:17, C_WG2:C_WG2 + 512]
        S.op("pool", lambda e: e.memset(glaug[:], 1.0), [], ["g_glaug"])
        S.op("pool", lambda e: e.memset(Sf[:], 0.0), [], [("Sf", i_, h_) for i_ in range(2) for h_ in range(4)])
        S.op("pool", lambda e: e.memset(Sb[:], 0.0), [], [("Sb", 0, h) for h in range(4)] + [("Sb", 1, h) for h in range(4)])

        def gen_P(tt):
            par = tt % 2
            t0, n = TILES[tt]
            nsub = n // 128

            def proj_fm(wv, wkey, c0, m, pb, pkey):
                for kc in range(KC):
                    S.op("pe", lambda e: e.matmul(pb[0:m, 0:n], wv[:, kc, c0:c0 + m], hT[:, kc, 0:n], start=(kc == 0), stop=(kc == KC - 1)),
                         [wkey, "g_h"], [pkey], signal=(kc == KC - 1))

            def proj_tm(wv, wkey, sub, ncols, pb, pkey):
                for kc in range(KC):
                    S.op("pe", lambda e: e.matmul(pb[:, 0:ncols], hT[:, kc, sub * 128:(sub + 1) * 128], wv[:, kc, 0:ncols], start=(kc == 0), stop=(kc == KC - 1)),
                         [wkey, "g_h"], [pkey], signal=(kc == KC - 1))

            pb, pkey = nbp()
            for hf in range(2):
                S.op("act", lambda e: e.activation(out=sq[:, :, 0:n], in_=xT[:, hf * 4:(hf + 1) * 4, t0:t0 + n], func=AF.Square), [("x", tt)], ["g_sq"])
                for q in range(4):
                    kc = hf * 4 + q
                    S.op("pe", lambda e: e.matmul(pb[:, 0:n], ones_bf[:], sq[:, q, 0:n], start=(kc == 0), stop=(kc == KC - 1)),
                         ["g_sq", "ones_bf"], [pkey], signal=(q == 3))
                yield
            S.op("dve", lambda e: e.tensor_scalar(out=rstd[:, 0:n], in0=pb[:, 0:n], scalar1=1.0 / D, scalar2=EPS, op0=ALU.mult, op1=ALU.add), [pkey], ["g_rstd"])
            S.op("act", lambda e: e.activation(out=rstd[:, 0:n], in_=rstd[:, 0:n], func=AF.Ln), ["g_rstd"], ["g_rstd"])
            S.op("act", lambda e: e.activation(out=rstd[:, 0:n], in_=rstd[:, 0:n], func=AF.Exp, scale=-0.5), ["g_rstd"], ["g_rstd"])
            for kc in range(KC):
                S.op("dve", lambda e: e.scalar_tensor_tensor(out=hT[:, kc, 0:n], in0=xT[:, kc, t0:t0 + n], scalar=nwcol(0, kc), in1=rstd[:, 0:n], op0=ALU.mult, op1=ALU.mult),
                     [("x", tt), "g_rstd", "cst"], ["g_h"], disjoint=True)
                if kc % 4 == 3:
                    yield
            wb, wkey = wload_cols(gla_w_in, 3072, 16)
            wv = wb[:, 0:KC * 16].rearrange("p (kc c) -> p kc c", c=16)
            pb, pkey = nbp()
            proj_fm(wv, wkey, 0, 16, pb, pkey)
            S.op("act", lambda e: e.copy(out=glaug[0:16, 0:n], in_=pb[0:16, 0:n]), [pkey], ["g_glaug"])
            yield
            for sub in range(nsub):
                pb, pkey = nbp()
                S.op("pe", lambda e: e.matmul(pb[:, :], glaug[0:17, sub * 128:(sub + 1) * 128], wg2, start=True, stop=True), ["g_glaug", "cst"], [pkey])
                S.op("act", lambda e: e.activation(out=sp[:, sub, :], in_=pb[:, :], func=AF.Exp, scale=-1.0), [pkey], [("g_sp", sub)])
                S.op("dve", lambda e: e.tensor_scalar_add(sp[:, sub, :], sp[:, sub, :], 1.0), [("g_sp", sub)], [("g_sp", sub)])
                S.op("act", lambda e: e.activation(out=sp[:, sub, :], in_=sp[:, sub, :], func=AF.Ln), [("g_sp", sub)], [("g_sp", sub)])
                yield
            for j in range(2):
                wb, wkey = wload_cols(gla_w_in, 1024 + j * 512, 512)
                wv = wb[:, 0:KC * 512].rearrange("p (kc c) -> p kc c", c=512)
                for sub in range(nsub):
                    pb, pkey = nbp()
                    proj_tm(wv, wkey, sub, 512, pb, pkey)
                    if (sub + j) % 2 == 0:
                        S.op("act", lambda e: e.copy(out=vt[par][:, sub, j * 512:(j + 1) * 512], in_=pb[:, :]), [pkey], [("g_v", par, sub)])
                    else:
                        S.op("dve", lambda e: e.tensor_copy(out=vt[par][:, sub, j * 512:(j + 1) * 512], in_=pb[:, :]), [pkey], [("g_v", par, sub)])
                    yield
            for sub in range(nsub):
                pb, pkey = nbp()
                S.op("pe", lambda e: e.matmul(pb[:, :], Ustr, sp[:, sub, :], start=True, stop=True), [("g_sp", sub), "cst"], [pkey])
                S.op("act", lambda e: e.activation(out=edT[:, sub, :], in_=pb[:, :], func=AF.Exp), [pkey], [("g_edT", sub)])
                yield
            for h in range(4):
                pb, pkey = nbp()
                for sub in range(nsub):
                    S.op("pe", lambda e: e.matmul(pb[:, sub * 128:(sub + 1) * 128], sp[:, sub, h * 128:(h + 1) * 128], Uinc, start=True, stop=True),
                         [("g_sp", sub), "cst"], [pkey], signal=(sub == nsub - 1))
                S.op("act", lambda e: e.activation(out=eb[:, h, 0:n], in_=pb[:, 0:n], func=AF.Exp), [pkey], [("g_eb", h)])
                S.op("act", lambda e: e.activation(out=enb[:, h, 0:n], in_=pb[:, 0:n], func=AF.Exp, scale=-1.0), [pkey], [("g_enb", h)])
                S.op("act", lambda e: e.activation(out=dec[par][:, h, 0:n // 64], in_=pb[:, 0:n].rearrange("p (c j) -> p c j", j=64)[:, :, 63], func=AF.Exp),
                     [pkey], [("g_dec", par, h)])
                yield
            wb, wkey = wload_cols(gla_w_in, 0, 512)
            wv = wb[:, 0:KC * 512].rearrange("p (kc c) -> p kc c", c=512)
            for h in range(4):
                pb, pkey = nbp()
                proj_fm(wv, wkey, h * 128, 128, pb, pkey)
                S.op("dve", lambda e: e.scalar_tensor_tensor(out=qeT[par][:, h, 0:n], in0=pb[:, 0:n], scalar=float(128 ** -0.5), in1=eb[:, h, 0:n], op0=ALU.mult, op1=ALU.mult),
                     [pkey, ("g_eb", h)], [("g_qeT", par, h)])
                yield
            wb, wkey = wload_cols(gla_w_in, 512, 512)
            wv = wb[:, 0:KC * 512].rearrange("p (kc c) -> p kc c", c=512)
            for h in range(4):
                pb, pkey = nbp()
                proj_fm(wv, wkey, h * 128, 128, pb, pkey)
                S.op("dve", lambda e: e.tensor_tensor(out=keT[par][:, h, 0:n], in0=pb[:, 0:n], in1=enb[:, h, 0:n], op=ALU.mult),
                     [pkey, ("g_enb", h)], [("g_keT", par, h)])
                yield
            for sub in range(nsub):
                pb, pkey = nbp()
                proj_tm(wv, wkey, sub, 512, pb, pkey)
                S.op("dve", lambda e: e.tensor_tensor(out=kd[par][:, sub, :], in0=pb[:, :], in1=edT[:, sub, :], op=ALU.mult),
                     [pkey, ("g_edT", sub)], [("g_kd", par, sub)])
                yield
            for j in range(2):
                wb, wkey = wload_cols(gla_w_in, 2048 + j * 512, 512)
                wv = wb[:, 0:KC * 512].rearrange("p (kc c) -> p kc c", c=512)
                for c4 in range(4):
                    pb, pkey = nbp()
                    proj_fm(wv, wkey, c4 * 128, 128, pb, pkey)
                    S.op("act", lambda e: e.activation(out=sr[par][:, j * 4 + c4, 0:n], in_=pb[:, 0:n], func=AF.Silu), [pkey], [("g_sr", par, j * 4 + c4)])
                    yield

        def gen_R(tt):
            par = tt % 2
            t0, n = TILES[tt]
            is_sample = (tt == 4)
            pending_norms = []

            def norm_stages(pair, pc):
                ob_ = obuf[pair % 2]
                okb = ("g_obuf", pair % 2)
                S.op("act", lambda e: e.activation(out=osq[:, :, :], in_=ob_[:, :, :], func=AF.Square), [okb], ["g_osq"])
                yield
                pb, pkey = nbp()
                for h in range(4):
                    for dvc in range(2):
                        S.op("pe", lambda e: e.matmul(pb[:, h * 128:(h + 1) * 128], ones_bf[:], osq[:, h, dvc * 128:(dvc + 1) * 128], start=(dvc == 0), stop=(dvc == 1)),
                             ["g_osq", "ones_bf"], [pkey], signal=(h == 3 and dvc == 1))
                yield
                ov = orstd[:, :, :].rearrange("p h t -> p (h t)")
                S.op("dve", lambda e: e.tensor_scalar(out=ov, in0=pb[:, :], scalar1=1.0 / 256.0, scalar2=EPS, op0=ALU.mult, op1=ALU.add), [pkey], ["g_orstd"])
                S.op("act", lambda e: e.activation(out=ov, in_=ov, func=AF.Ln), ["g_orstd"], ["g_orstd"])
                S.op("act", lambda e: e.activation(out=ov, in_=ov, func=AF.Exp, scale=-0.5), ["g_orstd"], ["g_orstd"])
                yield
                for h in range(4):
                    for dvc in range(2):
                        hc = h * 2 + dvc
                        S.op("dve", lambda e: e.scalar_tensor_tensor(out=r3[:, hc, :], in0=sr[par][:, hc, pc:pc + 128], scalar=nwcol(5, hc), in1=orstd[:, h, :], op0=ALU.mult, op1=ALU.mult),
                             [("g_sr", par, hc), "g_orstd", "cst"], [("g_r3", hc)])
                    if h % 2 == 1:
                        yield
                for h in range(4):
                    for dvc in range(2):
                        hc = h * 2 + dvc
                        S.op("dve", lambda e: e.tensor_tensor(out=yT[:, hc, pc:pc + 128], in0=ob_[:, h, dvc * 128:(dvc + 1) * 128], in1=r3[:, hc, :], op=ALU.mult),
                             [okb, ("g_r3", hc)], ["g_yT"], disjoint=True)
                    if h % 2 == 1:
                        yield
            for pair in range(n // 128):
                sub = pair
                pc = pair * 128
                if is_sample:
                    for i in range(2):
                        seq = pair * 2 + i
                        S.dma("sp", [(Sf[:, i, :, :], state_in[seq].rearrange("h p v -> p h v"))], [], [("Sf", i, h_) for h_ in range(4)], f"d_st{i}")
                        for h in range(4):
                            if h % 2 == 0:
                                S.op("act", lambda e: e.copy(out=Sb[:, i, h, :], in_=Sf[:, i, h, :]), [("Sf", i, h)], [("Sb", i, h)])
                            else:
                                S.op("dve", lambda e: e.tensor_copy(out=Sb[:, i, h, :], in_=Sf[:, i, h, :]), [("Sf", i, h)], [("Sb", i, h)])
                for h in range(4):
                    S.op("pe", lambda e: e.matmul(pbank[2][:, h * 128:(h + 1) * 128], keT[par][:, h, pc:pc + 128], qeT[par][:, h, pc:pc + 128], start=True, stop=True),
                         [("g_keT", par, h), ("g_qeT", par, h)], ["pb2"])
                yield
                for h in range(4):
                    S.op("dve", lambda e: e.tensor_tensor(out=ATm[h][:], in0=pbank[2][:, h * 128:(h + 1) * 128], in1=mask2, op=ALU.mult), ["pb2", "cst"], [f"g_AT{h}"])
                yield
                for i in range(2):
                    c = pair * 2 + i
                    fs = i if is_sample else 0
                    bs = i if is_sample else (c % 2)
                    r0 = i * 64
                    for h in range(4):
                        ub = pbank[3 + h // 2]
                        uc = (h % 2) * 256
                        S.op("pe", lambda e: e.matmul(ub[:, uc:uc + 256], kd[par][r0:r0 + 64, sub, h * 128:(h + 1) * 128], vt[par][r0:r0 + 64, sub, h * 256:(h + 1) * 256], start=True, stop=True),
                             [("g_kd", par, sub), ("g_v", par, sub)], [f"pb{3 + h // 2}"])
                    yield
                    for h in range(4):
                        ob, okey = pbank[h // 2], f"pb{h // 2}"
                        for dvc in range(2):
                            ocol = (h % 2) * 256 + dvc * 128
                            if i == 0:
                                S.op("pe", lambda e: e.matmul(ob[:, ocol:ocol + 128], vt[par][:, sub, h * 256 + dvc * 128:h * 256 + (dvc + 1) * 128], ATm[h][:], start=(h % 2 == 0 and dvc == 0), stop=False, skip_group_check=True),
                                     [("g_v", par, sub), f"g_AT{h}"], [okey], signal=False)
                            S.op("pe", lambda e: e.matmul(ob[:, ocol + r0:ocol + r0 + 64], Sb[:, bs, h, dvc * 128:(dvc + 1) * 128], qeT[par][:, h, pc + r0:pc + r0 + 64], start=False, stop=(i == 1), skip_group_check=True),
                                 [("Sb", bs, h), ("g_qeT", par, h)], [okey], signal=(i == 1 or dvc == 1))
                    yield
                    for h in range(4):
                        ub = pbank[3 + h // 2]
                        uc = (h % 2) * 256
                        S.op("dve", lambda e: e.scalar_tensor_tensor(out=Sf[:, fs, h, :], in0=Sf[:, fs, h, :], scalar=dec[par][:, h, c:c + 1], in1=ub[:, uc:uc + 256], op0=ALU.mult, op1=ALU.add),
                             [("Sf", fs, h), ("g_dec", par, h), f"pb{3 + h // 2}"], [("Sf", fs, h)])
                        if not is_sample:
                            nbs = (c + 1) % 2
                            S.op("act", lambda e: e.copy(out=Sb[:, nbs, h, :], in_=Sf[:, 0, h, :]), [("Sf", 0, h)], [("Sb", nbs, h)])
                    yield
                if is_sample:
                    for i in range(2):
                        seq = pair * 2 + i
                        S.dma("sp", [(gss_out[seq].rearrange("h p v -> p h v"), Sf[:, i, :, :])], [("Sf", i, h_) for h_ in range(4)], [], f"d_gs{i}")
                ob_ = obuf[pair % 2]
                okb = ("g_obuf", pair % 2)
                S.op("act", lambda e: e.copy(out=ob_[:, 0:2, :], in_=pbank[0][:, :].rearrange("p (h c) -> p h c", h=2)), ["pb0"], [okb])
                S.op("dve", lambda e: e.tensor_copy(out=ob_[:, 2:4, :], in_=pbank[1][:, :].rearrange("p (h c) -> p h c", h=2)), ["pb1"], [okb])
                yield
                pending_norms.append((pair, pc))
                while len(pending_norms) > 1:
                    for _ in norm_stages(*pending_norms.pop(0)):
                        yield
            while pending_norms:
                for _ in norm_stages(*pending_norms.pop(0)):
                    yield
            if tt == 3:
                S.dma("sp", [(gsp_out.rearrange("h p v -> p h v"), Sf[:, 0, :, :])], [("Sf", 0, h_) for h_ in range(4)], [], "d_gs0")
            for j in range(2):
                wb, wkey = wload_cols(gla_w_out, j * 512, 512)
                wv = wb[:, 0:KC * 512].rearrange("p (kc c) -> p kc c", c=512)
                for c4 in range(4):
                    ncx = j * 4 + c4
                    pb, pkey = nbp()
                    for kc in range(KC):
                        S.op("pe", lambda e: e.matmul(pb[:, 0:n], wv[:, kc, c4 * 128:(c4 + 1) * 128], yT[:, kc, 0:n], start=(kc == 0), stop=(kc == KC - 1)),
                             [wkey, "g_yT"], [pkey], signal=(kc == KC - 1))
                    S.op("dve", lambda e: e.tensor_tensor(out=xT[:, ncx, t0:t0 + n], in0=pb[:, 0:n], in1=xT[:, ncx, t0:t0 + n], op=ALU.add),
                         [pkey, ("x", tt)], [("x", tt)], disjoint=True)
                    yield

        def run(g):
            for _ in g:
                pass

        def interleave(gr, gp, ratio):
            ar, ap = True, gp is not None
            while ar or ap:
                if ar:
                    try:
                        next(gr)
                    except StopIteration:
                        ar = False
                for _ in range(ratio):
                    if ap:
                        try:
                            next(gp)
                        except StopIteration:
                            ap = False

        tiles = list(cfg.get("gla_tiles", range(5)))
        run(gen_P(tiles[0]))
        for idx, tt in enumerate(tiles):
            nxt = tiles[idx + 1] if idx + 1 < len(tiles) else None
            interleave(gen_R(tt), gen_P(nxt) if nxt is not None else None, cfg.get("gla_ratio", 1))
        if all(cfg.get(k_, True) for k_ in ("gla", "fox", "ffn0", "ffn1")):
            prefetch_gu(0, 0)
            prefetch_gu(0, 1)
        S.barrier()
        release(nalloc)

    if cfg.get("gla", True):
        gla_phase()

    if cfg.get("ffn0", True):
        ffn_phase(0)

    def fox_phase(sample):
        sfx = "s" if sample else "p"
        hT = sb("f_h" + sfx, [128, KC, 512], BF16)
        sqq = sb("f_sqq" + sfx, [128, KC, 512], BF16)
        rstd = sb("f_rstd" + sfx, [128, 512], F32)
        kT = sb("f_kT" + sfx, [128, KC, 2112 if sample else 2048], BF16)
        v65 = sb("f_v65" + sfx, [128, 16, 1040], BF16)
        PT = [sb(f"f_PT{i}" + sfx, [128, 512], BF16) for i in range(3)]
        o_n = sb("f_on" + sfx, [128, 4, 1024], BF16)
        spl = sb("f_spl" + sfx, [128, 17, 16], F32)
        bias = sb("f_bias" + sfx, [128, 17, 16], F32)
        lfz = sb("f_lfz" + sfx, [128, 16], F32)
        lfst = sb("f_lfst" + sfx, [128, 16], F32)
        rl = sb("f_rl" + sfx, [128, 4], F32)
        zeros = sb("f_zeros" + sfx, [128, 512], BF16)
        ident_bf = sb("f_idbf" + sfx, [128, 128], BF16)
        cstage = [sb(f"f_cst{i}" + sfx, [128, 1024 if sample else 512], F32) for i in range(2)]
        nalloc = 18
        if sample:
            kTn = sb("f_kTn", [128, KC, 256], BF16)
            lfc = sb("f_lfc", [128, 2, 16], F32)
            lfc_c = sb("f_lfcc", [128, 17, 16], F32)
            vn = sb("f_vn", [128, 2, 1040], BF16)
            wexp = lfc_c
            nalloc += 4
        else:
            qz2 = sb("f_qz2", [128, KC, 512], BF16)
            ttot = sb("f_ttot", [128, 16, 16], F32)
            sfx = sb("f_sfx", [128, 16, 16], F32)
            nalloc += 3
        oT = hT
        qT = sqq
        if not sample:
            for nm in ("d_c0", "d_c1", "d_lf", "d_lfc"):
                S.add_sem(nm, "dma")
        tri = cst[:, C_TRI:C_TRI + 128]
        NSTR = cst[:, C_NSTR:C_NSTR + 128]
        NONE = cst[:, C_NONE:C_NONE + 128]
        bfb = cst[:, C_BF:C_BF + 16]
        S.op("pool", lambda e: e.memset(zeros[:], 0.0), [], ["f_zeros"])
        S.op("pool", lambda e: e.memset(v65[:], 1.0), [], [("f_v65", kt) for kt in range(16)])
        if not sample:
            S.op("pool", lambda e: e.memset(qz2[:], 0.0), [], ["f_qz2"])
        S.op("dve", lambda e: e.tensor_copy(out=ident_bf[:], in_=ident), ["cst"], ["f_idbf"])
        if sample:
            S.op("pool", lambda e: e.memset(vn[:], 1.0), [], [("f_vn", 0), ("f_vn", 1)])
        qoff = {"c": 0}
        rot = {"s": 0, "o": 0, "m": 0, "c": 0, "p": 0}

        def nbk(which, lst):
            i = lst[rot[which] % len(lst)]
            rot[which] += 1
            return pbank[i], f"pb{i}"

        def stage_slot():
            i = rot["c"] % 2
            rot["c"] += 1
            return cstage[i], f"f_cst{i}", f"d_c{i}"

        def compute_bias(nk):
            pb, pkey = pbank[4], "pb4"
            rhs = spl[:, 0:nk, :]
            S.op("pe", lambda e: e.matmul(pb[:, 0:nk * 16], NSTR, rhs, start=True, stop=True), [("f_spl", a) for a in range(nk)] + ["cst"], [pkey])
            S.op("dve", lambda e: e.tensor_copy(out=bias[:, 0:nk, :], in_=pb[:, 0:nk * 16].rearrange("p (a h) -> p a h", h=16)), [pkey], ["f_bias"])
            S.op("pe", lambda e: e.matmul(pb[:, 0:nk * 16], NONE, rhs, start=True, stop=True), [("f_spl", a) for a in range(nk)] + ["cst"], [pkey])
            S.op("dve", lambda e: e.tensor_copy(out=ttot[:, 0:nk, :], in_=pb[:, 0:nk * 16].rearrange("p (a h) -> p a h", h=16)), [pkey], ["f_ttot"])
            for a in range(nk - 2, -1, -1):
                if a == nk - 2:
                    S.op("dve", lambda e: e.tensor_copy(out=sfx[:, a, :], in_=ttot[:, a + 1, :]), ["f_ttot"], ["f_sfx"])
                else:
                    S.op("dve", lambda e: e.tensor_tensor(out=sfx[:, a, :], in0=sfx[:, a + 1, :], in1=ttot[:, a + 1, :], op=ALU.add), ["f_ttot", "f_sfx"], ["f_sfx"])
            if nk > 1:
                S.op("dve", lambda e: e.tensor_tensor(out=bias[:, 0:nk - 1, :], in0=bias[:, 0:nk - 1, :], in1=sfx[:, 0:nk - 1, :], op=ALU.add), ["f_bias", "f_sfx"], ["f_bias"])

        SB_P = [0, 1, 5, 6]

        def attend_prompt(tt):
            nk = 4 * tt + 4
            items = []
            for h in range(16):
                for a in range(nk):
                    j = a - 4 * tt
                    items.append(dict(h=h, a=a, c0=(j * 128 if j > 0 else 0), diag=(j >= 0), first=(a == 0), last=(a == nk - 1)))
            n_it = len(items)
            Sres = [None] * n_it
            Eres = [None] * n_it
            obank = {}

            def do_S(i):
                it = items[i]
                h, a, c0 = it["h"], it["a"], it["c0"]
                hp, hc = h % 2, h // 2
                pr = slice(hp * 64, hp * 64 + 64)
                pbS, skey = nbk("s", SB_P)
                qb_, qk_ = (sqq, "f_sqq") if h < 8 else (qz2, "f_qz2")
                S.op("pe", lambda e: e.matmul(pbS[:, c0:512], kT[:, hc, a * 128:(a + 1) * 128], qb_[:, h % 8, c0:512], start=True, stop=True),
                     [("f_kT", a), qk_], [skey])
                Sres[i] = (pbS, skey)

            def do_E(i):
                it = items[i]
                h, a, c0 = it["h"], it["a"], it["c0"]
                pbS, skey = Sres[i]
                pi = rot["p"] % 3
                rot["p"] += 1
                pt, ptk = PT[pi], f"f_PT{pi}"
                S.op("act", lambda e: e.activation(out=pt[:, c0:512], in_=pbS[:, c0:512], func=AF.Exp, bias=bias[:, a, h:h + 1], scale=0.125),
                     [skey, "f_bias"], [ptk])
                if it["diag"]:
                    S.op("dve", lambda e: e.tensor_tensor(out=pt[:, c0:c0 + 128], in0=pt[:, c0:c0 + 128], in1=tri, op=ALU.mult), [ptk, "cst"], [ptk])
                Eres[i] = (pt, ptk)

            def do_PV(i):
                it = items[i]
                h, a, c0 = it["h"], it["a"], it["c0"]
                pt, ptk = Eres[i]
                if it["first"]:
                    obank[h] = nbk("o", [2, 3])
                    pbO, okey = obank[h]
                    S.op("pe", lambda e: e.matmul(pbO[:, :], zeros[:, 0:128], zeros[:, :], start=True, stop=False, skip_group_check=True), ["f_zeros"], [okey], signal=False)
                pbO, okey = obank[h]
                for qt in range(c0 // 128, 4):
                    is_last = (it["last"] and qt == 3)
                    S.op("pe", lambda e: e.matmul(pbO[:, qt * 128:qt * 128 + 65], pt[:, qt * 128:(qt + 1) * 128], v65[:, a, h * 65:(h + 1) * 65], start=False, stop=is_last, skip_group_check=True),
                         [ptk, ("f_v65", a)], [okey], signal=(qt == 3))
                if it["last"]:
                    ov = pbO[:, :].rearrange("p (q c) -> p q c", c=128)
                    S.op("dve", lambda e: e.reciprocal(out=rl[:, 0:4], in_=ov[:, 0:4, 64]), [okey], ["f_rl"])
                    rb = mkap(rl, 0, [[4, 128], [1, 4], [0, 64]])
                    S.op("dve", lambda e: e.tensor_tensor(out=o_n[:, 0:4, h * 64:(h + 1) * 64], in0=ov[:, 0:4, 0:64], in1=rb, op=ALU.mult),
                         [okey, "f_rl"], ["f_on"])

            do_S(0)
            if n_it > 1:
                do_S(1)
            for i in range(n_it):
                do_E(i)
                if i + 2 < n_it:
                    do_S(i + 2)
                if i >= 1:
                    do_PV(i - 1)
            do_PV(n_it - 1)

        SB_S = [0, 1, 5, 6, 7, 4]
        PTS = [(PT[0], "f_PT0"), (PT[1], "f_PT1"), (PT[2], "f_PT2")]

        def attend_sample(seq, sub, r0):
            ptl = PTS + [(o_n[:, 1, 0:512], "f_on1a"), (o_n[:, 1, 512:1024], "f_on1b"), (o_n[:, 2, 0:512], "f_on2a")]
            qc = seq * 64
            Sres = {}
            Eres = {}

            def do_S(h):
                hp, hc = h % 2, h // 2
                pr = slice(hp * 64, hp * 64 + 64)
                banks = [nbk("s", SB_S) for _ in range(3)]
                for g in range(2):
                    pbS, skey = banks[g]
                    for j in range(8):
                        a = g * 8 + j
                        S.op("pe", lambda e: e.matmul(pbS[:, j * 64:(j + 1) * 64], kT[pr, hc, a * 128:(a + 1) * 128], qT[pr, hc, qc:qc + 64], start=True, stop=True),
                             [("f_kT", a), "f_sqq"], [skey], signal=(j == 7))
                pbS, skey = banks[2]
                S.op("pe", lambda e: e.matmul(pbS[r0:r0 + 64, 0:64], kT[pr, hc, 2048:2112], qT[pr, hc, qc:qc + 64], start=True, stop=True),
                     [("f_kT", 16), "f_sqq"], [skey])
                Sres[h] = banks

            def do_E(h):
                banks = Sres[h]
                pts = []
                for g in range(3):
                    pi = rot["p"] % 6
                    rot["p"] += 1
                    pt, ptk = ptl[pi]
                    pbS, skey = banks[g]
                    if g < 2:
                        S.op("act", lambda e: e.activation(out=pt[:, 0:512], in_=pbS[:, 0:512], func=AF.Exp, scale=0.125), [skey], [ptk])
                    else:
                        S.op("act", lambda e: e.activation(out=pt[r0:r0 + 64, 0:64], in_=pbS[r0:r0 + 64, 0:64], func=AF.Exp, scale=0.125), [skey], [ptk])
                        S.op("dve", lambda e: e.tensor_tensor(out=pt[r0:r0 + 64, 0:64], in0=pt[r0:r0 + 64, 0:64], in1=tri[r0:r0 + 64, r0:r0 + 64], op=ALU.mult), [ptk, "cst"], [ptk])
                    pts.append((pt, ptk))
                Eres[h] = pts

            def do_PV(h):
                pts = Eres[h]
                pbO, okey = nbk("o", [2, 3])
                for a in range(16):
                    pt, ptk = pts[a // 8]
                    j = a % 8
                    S.op("pe", lambda e: e.matmul(pbO[0:64, 0:65], pt[:, j * 64:(j + 1) * 64], v65[:, a, h * 65:(h + 1) * 65], start=(a == 0), stop=False),
                         [ptk, ("f_v65", a)], [okey], signal=False)
                pt, ptk = pts[2]
                S.op("pe", lambda e: e.matmul(pbO[0:64, 0:65], pt[r0:r0 + 64, 0:64], vn[r0:r0 + 64, sub, h * 65:(h + 1) * 65], start=False, stop=True),
                     [ptk, ("f_vn", sub)], [okey])
                S.op("dve", lambda e: e.reciprocal(out=rl[0:64, 0:1], in_=pbO[0:64, 64:65]), [okey], ["f_rl"])
                S.op("dve", lambda e: e.tensor_scalar(out=o_n[0:64, 0, h * 64:(h + 1) * 64], in0=pbO[0:64, 0:64], scalar1=rl[0:64, 0:1], scalar2=None, op0=ALU.mult),
                     [okey, "f_rl"], ["f_on"])

            do_S(0)
            for h in range(16):
                do_E(h)
                if h + 1 < 16:
                    do_S(h + 1)
                do_PV(h)

        def proj_fm(wv, wkey, c0, n, pb, pkey):
            for kc in range(KC):
                S.op("pe", lambda e: e.matmul(pb[:, 0:n], wv[:, kc, c0:c0 + 128], hT[:, kc, 0:n], start=(kc == 0), stop=(kc == KC - 1)),
                     [wkey, "f_h"], [pkey], signal=(kc == KC - 1))

        def proj_tm(wv, wkey, sub, ncols, pb, pkey):
            for kc in range(KC):
                S.op("pe", lambda e: e.matmul(pb[:, 0:ncols], hT[:, kc, sub * 128:(sub + 1) * 128], wv[:, kc, 0:ncols], start=(kc == 0), stop=(kc == KC - 1)),
                     [wkey, "f_h"], [pkey], signal=(kc == KC - 1))

        for tt in ([4] if sample else cfg.get("fox_tiles", [0, 1, 2, 3])):
            t0, n = TILES[tt]
            nsub = n // 128
            is_sample = (tt == 4)
            rmsnorm(tt, 2, lambda kc: hT[:, kc, 0:n], ["f_h"], sqq, "f_sqq", rstd, "f_rstd")
            if not is_sample:
                sq4 = sqq[:, :, :].rearrange("p (a b) c -> p a b c", b=2)
                S.op("pool", lambda e: e.memset(sq4[64:128, :, 0, :], 0.0), ["f_rstd"], ["f_sqq"])
                S.op("pool", lambda e: e.memset(sq4[0:64, :, 1, :], 0.0), ["f_rstd"], ["f_sqq"])
            wb, wkey = wload_cols(fox_w_in, 3072, 16)
            wv = wb[:, 0:KC * 16].rearrange("p (kc c) -> p kc c", c=16)
            for sub in range(nsub):
                kt = (4 * tt + sub) if not is_sample else 16
                pb, pkey = nbk("m", [5, 6, 7])
                proj_tm(wv, wkey, sub, 16, pb, pkey)
                S.op("dve", lambda e: e.tensor_tensor(out=lfz[:, :], in0=pb[:, 0:16], in1=bfb, op=ALU.add), [pkey, "cst"], ["f_lfz"])
                S.op("act", lambda e: e.activation(out=lfz[:, :], in_=lfz[:, :], func=AF.Exp, scale=-1.0), ["f_lfz"], ["f_lfz"])
                S.op("dve", lambda e: e.tensor_scalar_add(lfz[:, :], lfz[:, :], 1.0), ["f_lfz"], ["f_lfz"])
                if not is_sample:
                    S.op("act", lambda e: e.activation(out=spl[:, kt, :], in_=lfz[:, :], func=AF.Ln), ["f_lfz"], [("f_spl", kt)])
                    src, skey2 = spl[:, kt, :], ("f_spl", kt)
                else:
                    S.op("act", lambda e: e.activation(out=lfc[:, sub, :], in_=lfz[:, :], func=AF.Ln), ["f_lfz"], [("f_spn", sub)])
                    src, skey2 = lfc[:, sub, :], ("f_spn", sub)
                S.op("act", lambda e: e.mul(out=lfst[:, :], in_=src, mul=-1.0), [skey2], ["f_lfst"])
                S.dma("sp", [(lf_out[t0 + sub * 128:t0 + (sub + 1) * 128, :], lfst[:, :])], ["f_lfst"], [], "d_lf")
            for j in range(2):
                wb, wkey = wload_cols(fox_w_in, j * 512, 512)
                wv = wb[:, 0:KC * 512].rearrange("p (kc c) -> p kc c", c=512)
                for c4 in range(4):
                    pb, pkey = nbk("m", [5, 6, 7])
                    proj_fm(wv, wkey, c4 * 128, n, pb, pkey)
                    if is_sample:
                        if c4 % 2 == 0:
                            S.op("act", lambda e: e.copy(out=qT[:, j * 4 + c4, 0:n], in_=pb[:, 0:n]), [pkey], ["f_sqq"])
                        else:
                            S.op("dve", lambda e: e.tensor_copy(out=qT[:, j * 4 + c4, 0:n], in_=pb[:, 0:n]), [pkey], ["f_sqq"])
                    else:
                        hcq = j * 4 + c4
                        for hp in range(2):
                            hh = 2 * hcq + hp
                            qb_, qk_ = (sqq, "f_sqq") if hh < 8 else (qz2, "f_qz2")
                            pr_ = slice(hp * 64, hp * 64 + 64)
                            if hp == 0:
                                S.op("act", lambda e: e.copy(out=qb_[pr_, hh % 8, 0:n], in_=pb[pr_, 0:n]), [pkey], [qk_])
                            else:
                                S.op("dve", lambda e: e.tensor_copy(out=qb_[pr_, hh % 8, 0:n], in_=pb[pr_, 0:n]), [pkey], [qk_])
            for j in range(2):
                wb, wkey = wload_cols(fox_w_in, 1024 + j * 512, 512)
                wv = wb[:, 0:KC * 512].rearrange("p (kc c) -> p kc c", c=512)
                for c4 in range(4):
                    pb, pkey = nbk("m", [5, 6, 7])
                    proj_fm(wv, wkey, c4 * 128, n, pb, pkey)
                    if not is_sample:
                        dst = kT[:, j * 4 + c4, t0:t0 + n]
                        dkeys = [("f_kT", 4 * tt + q) for q in range(4)]
                    else:
                        dst = kTn[:, j * 4 + c4, 0:n]
                        dkeys = ["f_kTn"]
                    if c4 % 2 == 0:
                        S.op("act", lambda e: e.copy(out=dst, in_=pb[:, 0:n]), [pkey], dkeys)
                    else:
                        S.op("dve", lambda e: e.tensor_copy(out=dst, in_=pb[:, 0:n]), [pkey], dkeys)
                for sub in range(nsub):
                    pb, pkey = nbk("m", [5, 6, 7])
                    proj_tm(wv, wkey, sub, 512, pb, pkey)
                    st, stk, sem = stage_slot()
                    S.op("act", lambda e: e.copy(out=st[:, 0:512], in_=pb[:, :]), [pkey], [stk])
                    S.dma("sp", [(k_out[t0 + sub * 128:t0 + (sub + 1) * 128, j * 512:(j + 1) * 512], st[:, 0:512])], [stk], [], sem)
            for j in range(2):
                wb, wkey = wload_cols(fox_w_in, 2048 + j * 512, 512)
                wv = wb[:, 0:KC * 512].rearrange("p (kc c) -> p kc c", c=512)
                for sub in range(nsub):
                    pb, pkey = nbk("m", [5, 6, 7])
                    proj_tm(wv, wkey, sub, 512, pb, pkey)
                    st, stk, sem = stage_slot()
                    S.op("act", lambda e: e.copy(out=st[:, 0:512], in_=pb[:, :]), [pkey], [stk])
                    S.dma("sp", [(v_out[t0 + sub * 128:t0 + (sub + 1) * 128, j * 512:(j + 1) * 512], st[:, 0:512])], [stk], [], sem)
                    if not is_sample:
                        kt = 4 * tt + sub
                        dst = v65[:, kt, j * 520:(j + 1) * 520].rearrange("p (h c) -> p h c", c=65)[:, :, 0:64]
                        S.op("dve", lambda e: e.tensor_copy(out=dst, in_=pb[:, :].rearrange("p (h c) -> p h c", c=64)), [pkey], [("f_v65", kt)])
                    else:
                        dst = vn[:, sub, j * 520:(j + 1) * 520].rearrange("p (h c) -> p h c", c=65)[:, :, 0:64]
                        S.op("dve", lambda e: e.tensor_copy(out=dst, in_=pb[:, :].rearrange("p (h c) -> p h c", c=64)), [pkey], [("f_vn", sub)])
            if not is_sample:
                nk = 4 * tt + 4
                compute_bias(nk)
                attend_prompt(tt)
                for hc in range(KC):
                    pb, pkey = nbk("m", [5, 6, 7])
                    for qt in range(4):
                        S.op("pe", lambda e: e.matmul(pb[:, qt * 128:(qt + 1) * 128], o_n[:, qt, hc * 128:(hc + 1) * 128], ident_bf[:], start=True, stop=True),
                             ["f_on", "f_idbf"], [pkey], signal=(qt == 3))
                    if hc % 2 == 0:
                        S.op("act", lambda e: e.copy(out=oT[:, hc, 0:512], in_=pb[:, :]), [pkey], ["f_h"])
                    else:
                        S.op("dve", lambda e: e.tensor_copy(out=oT[:, hc, 0:512], in_=pb[:, :]), [pkey], ["f_h"])
            else:
                for seq in range(NS):
                    sub, r0 = seq // 2, (seq % 2) * 64
                    S.dma("sp", [(lfc_c[:, 0:16, :], clf_in[seq].rearrange("(a p) h -> p a h", p=128))], [], ["f_lfcc"], "d_lfc")
                    S.op("act", lambda e: e.mul(out=spl[:, 0:16, :], in_=lfc_c[:, 0:16, :], mul=-1.0), ["f_lfcc"], [("f_spl", a) for a in range(16)])
                    S.op("dve", lambda e: e.tensor_copy(out=spl[:, 16, :], in_=lfc[:, sub, :]), [("f_spn", sub)], [("f_spl", 16)])
                    compute_bias([(a, 0, 128) for a in range(16)] + [(16, r0, 64)])
                    S.op("act", lambda e: e.activation(out=wexp[:, :, :], in_=bias[:, :, :], func=AF.Exp), ["f_bias"], ["f_lfcc"])
                    vn3 = vn[r0:r0 + 64, sub, :].rearrange("p (h c) -> p h c", c=65)
                    wn3 = mkap(wexp, r0 * 17 * 16 + 16 * 16, [[17 * 16, 64], [1, 16], [0, 64]])
                    S.op("pool", lambda e: e.tensor_tensor(out=vn3[:, :, 0:64], in0=vn3[:, :, 0:64], in1=wn3, op=ALU.mult), [("f_vn", sub), "f_lfcc"], [("f_vn", sub)])
                    S.op("pool", lambda e: e.tensor_copy(out=vn3[:, :, 64], in_=wexp[r0:r0 + 64, 16, :]), ["f_lfcc"], [("f_vn", sub)])
                    for a in range(16):
                        st, stk, sem = stage_slot()
                        S.dma("sp", [(st[:, :], ck_in[seq, a * 128:(a + 1) * 128, :])], [], [stk], sem)
                        for half in range(2):
                            pb, pkey = nbk("m", [5, 6, 7])
                            for q in range(4):
                                kc = half * 4 + q
                                S.op("pe", lambda e: e.transpose(pb[:, q * 128:(q + 1) * 128], st[:, kc * 128:(kc + 1) * 128], ident), [stk, "cst"], [pkey], signal=(q == 3))
                            dst = kT[:, half * 4:(half + 1) * 4, a * 128:(a + 1) * 128]
                            srcp = pb[:, :].rearrange("p (q t) -> p q t", t=128)
                            if half == 0:
                                S.op("act", lambda e: e.copy(out=dst, in_=srcp), [pkey], [("f_kT", a)])
                            else:
                                S.op("dve", lambda e: e.tensor_copy(out=dst, in_=srcp), [pkey], [("f_kT", a)])
                        st, stk, sem = stage_slot()
                        S.dma("sp", [(st[:, :], cv_in[seq, a * 128:(a + 1) * 128, :])], [], [stk], sem)
                        v3 = v65[:, a, :].rearrange("p (h c) -> p h c", c=65)
                        wb3 = mkap(wexp, a * 16, [[17 * 16, 128], [1, 16], [0, 64]])
                        S.op("pool", lambda e: e.tensor_tensor(out=v3[:, :, 0:64], in0=st[:, :].rearrange("p (h c) -> p h c", c=64), in1=wb3, op=ALU.mult),
                             [stk, "f_lfcc"], [("f_v65", a)])
                        S.op("pool", lambda e: e.tensor_copy(out=v3[:, :, 64], in_=wexp[:, a, :]), ["f_lfcc"], [("f_v65", a)])
                    S.op("dve", lambda e: e.tensor_copy(out=kT[:, :, 2048:2112], in_=kTn[:, :, seq * 64:(seq + 1) * 64]), ["f_kTn"], [("f_kT", 16)])
                    attend_sample(seq, sub, r0)
                    pb, pkey = nbk("m", [5, 6, 7])
                    for hc in range(KC):
                        S.op("pe", lambda e: e.matmul(pb[:, hc * 64:(hc + 1) * 64], o_n[0:64, 0, hc * 128:(hc + 1) * 128], ident_bf[0:64, 0:64], start=True, stop=True),
                             ["f_on", "f_idbf"], [pkey], signal=(hc == KC - 1))
                    S.op("act", lambda e: e.copy(out=oT[:, :, seq * 64:(seq + 1) * 64], in_=pb[:, :].rearrange("p (c q) -> p c q", q=64)), [pkey], ["f_h"])
            for j in range(2):
                wb, wkey = wload_cols(fox_w_out, j * 512, 512)
                wv = wb[:, 0:KC * 512].rearrange("p (kc c) -> p kc c", c=512)
                for c4 in range(4):
                    ncx = j * 4 + c4
                    pb, pkey = nbk("m", [5, 6, 7])
                    for kc in range(KC):
                        S.op("pe", lambda e: e.matmul(pb[:, 0:n], wv[:, kc, c4 * 128:(c4 + 1) * 128], oT[:, kc, 0:n], start=(kc == 0), stop=(kc == KC - 1)),
                             [wkey, "f_h"], [pkey], signal=(kc == KC - 1))
                    S.op("dve", lambda e: e.tensor_tensor(out=xT[:, ncx, t0:t0 + n], in0=pb[:, 0:n], in1=xT[:, ncx, t0:t0 + n], op=ALU.add),
                         [pkey, ("x", tt)], [("x", tt)], disjoint=True)
        S.barrier()
        release(nalloc)


    def fox_sample():
        tt = 4
        t0, n = TILES[tt]
        hT = sb("fs_h", [128, KC, 256], BF16)
        sq = sb("fs_sq", [128, KC, 256], BF16)
        rstd = sb("fs_rstd", [128, 512], F32)
        qz = sb("fs_qz", [128, 16, 256], BF16)
        kTn = sb("fs_kTn", [128, KC, 256], BF16)
        vn = sb("fs_vn", [128, 2, 1040], BF16)
        lfn = sb("fs_lfn", [128, 2, 16], F32)
        lfz = sb("fs_lfz", [128, 16], F32)
        lfst = sb("fs_lfst", [128, 16], F32)
        lfcc = sb("fs_lfcc", [128, NS, 16, 16], F32)
        spl4 = sb("fs_spl4", [128, NS, 17, 16], F32)
        ttot4 = sb("fs_ttot4", [128, NS, 17, 16], F32)
        sfx4 = sb("fs_sfx4", [128, NS, 17, 16], F32)
        wexp = sb("fs_wexp", [128, NS, 17, 16], F32)
        NST = 4
        kst = [sb(f"fs_kst{i}", [128, D], F32) for i in range(NST)]
        vst = [sb(f"fs_vst{i}", [128, D], F32) for i in range(NST)]
        kTt = [sb(f"fs_kTt{i}", [128, KC, 128], BF16) for i in range(3)]
        v65t = [sb(f"fs_v65t{i}", [128, 1040], BF16) for i in range(3)]
        PTs = [sb(f"fs_PT{i}", [128, 512], BF16) for i in range(4)]
        o_n = sb("fs_on", [128, D], BF16)
        rl = sb("fs_rl", [128, 16], F32)
        zeros = sb("fs_zeros", [128, 512], BF16)
        ident_bf = sb("fs_idbf", [128, 128], BF16)
        nalloc = 19 + 2 * NST + 3 + 3 + 4
        oT = hT
        for i in range(NST):
            S.add_sem(f"d_ks{i}", "dma")
            S.add_sem(f"d_vs{i}", "dma")
        S.add_sem("d_lfcc", "dma")
        tri = cst[:, C_TRI:C_TRI + 128]
        NSTR = cst[:, C_NSTR:C_NSTR + 128]
        NONE = cst[:, C_NONE:C_NONE + 128]
        bfb = cst[:, C_BF:C_BF + 16]
        S.op("pool", lambda e: e.memset(zeros[:], 0.0), [], ["fs_zeros"])
        S.op("pool", lambda e: e.memset(qz[:], 0.0), [], [("fs_q", h_) for h_ in range(16)])
        S.op("pool", lambda e: e.memset(vn[:], 1.0), [], [("fs_vn", 0), ("fs_vn", 1)])
        S.op("dve", lambda e: e.tensor_copy(out=ident_bf[:], in_=ident), ["cst"], ["fs_idbf"])
        S.dma("sp", [(lfcc[:, q, :, :], clf_in[q].rearrange("(a p) h -> p a h", p=128)) for q in range(NS)], [], ["fs_lfcc"], "d_lfcc")
        rot = {"m": 0, "k": 0, "v": 0, "kt": 0, "vt": 0, "p": 0, "o": 0}

        def nbk(which, lst):
            i = lst[rot[which] % len(lst)]
            rot[which] += 1
            return pbank[i], f"pb{i}"

        def proj_fm(wv, wkey, c0, pb, pkey):
            for kc in range(KC):
                S.op("pe", lambda e: e.matmul(pb[:, 0:n], wv[:, kc, c0:c0 + 128], hT[:, kc, 0:n], start=(kc == 0), stop=(kc == KC - 1)),
                     [wkey, "fs_h"], [pkey], signal=(kc == KC - 1))

        def proj_tm(wv, wkey, sub, ncols, pb, pkey):
            for kc in range(KC):
                S.op("pe", lambda e: e.matmul(pb[:, 0:ncols], hT[:, kc, sub * 128:(sub + 1) * 128], wv[:, kc, 0:ncols], start=(kc == 0), stop=(kc == KC - 1)),
                     [wkey, "fs_h"], [pkey], signal=(kc == KC - 1))

        MB = [5, 6, 7]
        rmsnorm(tt, 2, lambda kc: hT[:, kc, 0:n], ["fs_h"], sq, "fs_sq", rstd, "fs_rstd")
        wb, wkey = wload_cols(fox_w_in, 3072, 16)
        wv = wb[:, 0:KC * 16].rearrange("p (kc c) -> p kc c", c=16)
        for sub in range(2):
            pb, pkey = nbk("m", MB)
            proj_tm(wv, wkey, sub, 16, pb, pkey)
            S.op("dve", lambda e: e.tensor_tensor(out=lfz[:, :], in0=pb[:, 0:16], in1=bfb, op=ALU.add), [pkey, "cst"], ["fs_lfz"])
            S.op("act", lambda e: e.activation(out=lfz[:, :], in_=lfz[:, :], func=AF.Exp, scale=-1.0), ["fs_lfz"], ["fs_lfz"])
            S.op("dve", lambda e: e.tensor_scalar_add(lfz[:, :], lfz[:, :], 1.0), ["fs_lfz"], ["fs_lfz"])
            S.op("act", lambda e: e.activation(out=lfn[:, sub, :], in_=lfz[:, :], func=AF.Ln), ["fs_lfz"], [("fs_lfn", sub)])
            S.op("act", lambda e: e.mul(out=lfst[:, :], in_=lfn[:, sub, :], mul=-1.0), [("fs_lfn", sub)], ["fs_lfst"])
            S.dma("sp", [(lf_out[t0 + sub * 128:t0 + (sub + 1) * 128, :], lfst[:, :])], ["fs_lfst"], [], "d_lf")
        for j in range(2):
            wb, wkey = wload_cols(fox_w_in, j * 512, 512)
            wv = wb[:, 0:KC * 512].rearrange("p (kc c) -> p kc c", c=512)
            for c4 in range(4):
                pb, pkey = nbk("m", MB)
                proj_fm(wv, wkey, c4 * 128, pb, pkey)
                hcq = j * 4 + c4
                S.op("act", lambda e: e.copy(out=qz[0:64, 2 * hcq, 0:n], in_=pb[0:64, 0:n]), [pkey], [("fs_q", 2 * hcq)])
                S.op("dve", lambda e: e.tensor_copy(out=qz[64:128, 2 * hcq + 1, 0:n], in_=pb[64:128, 0:n]), [pkey], [("fs_q", 2 * hcq + 1)])
        for j in range(2):
            wb, wkey = wload_cols(fox_w_in, 1024 + j * 512, 512)
            wv = wb[:, 0:KC * 512].rearrange("p (kc c) -> p kc c", c=512)
            for c4 in range(4):
                pb, pkey = nbk("m", MB)
                proj_fm(wv, wkey, c4 * 128, pb, pkey)
                if c4 % 2 == 0:
                    S.op("act", lambda e: e.copy(out=kTn[:, j * 4 + c4, 0:n], in_=pb[:, 0:n]), [pkey], ["fs_kTn"])
                else:
                    S.op("dve", lambda e: e.tensor_copy(out=kTn[:, j * 4 + c4, 0:n], in_=pb[:, 0:n]), [pkey], ["fs_kTn"])
            for sub in range(2):
                pb, pkey = nbk("m", MB)
                proj_tm(wv, wkey, sub, 512, pb, pkey)
                i = rot["k"] % NST
                rot["k"] += 1
                S.op("act", lambda e: e.copy(out=kst[i][:, 0:512], in_=pb[:, :]), [pkey], [f"fs_kst{i}"])
                S.dma("sp", [(k_out[t0 + sub * 128:t0 + (sub + 1) * 128, j * 512:(j + 1) * 512], kst[i][:, 0:512])], [f"fs_kst{i}"], [], f"d_ks{i}")
        for j in range(2):
            wb, wkey = wload_cols(fox_w_in, 2048 + j * 512, 512)
            wv = wb[:, 0:KC * 512].rearrange("p (kc c) -> p kc c", c=512)
            for sub in range(2):
                pb, pkey = nbk("m", MB)
                proj_tm(wv, wkey, sub, 512, pb, pkey)
                i = rot["v"] % NST
                rot["v"] += 1
                S.op("act", lambda e: e.copy(out=vst[i][:, 0:512], in_=pb[:, :]), [pkey], [f"fs_vst{i}"])
                S.dma("sp", [(v_out[t0 + sub * 128:t0 + (sub + 1) * 128, j * 512:(j + 1) * 512], vst[i][:, 0:512])], [f"fs_vst{i}"], [], f"d_vs{i}")
                dst = vn[:, sub, j * 520:(j + 1) * 520].rearrange("p (h c) -> p h c", c=65)[:, :, 0:64]
                S.op("dve", lambda e: e.tensor_copy(out=dst, in_=pb[:, :].rearrange("p (h c) -> p h c", c=64)), [pkey], [("fs_vn", sub)])
        S.op("act", lambda e: e.mul(out=spl4[:, :, 0:16, :], in_=lfcc[:, :, :, :], mul=-1.0), ["fs_lfcc"], ["fs_spl4"])
        S.op("pool", lambda e: e.memset(spl4[:, :, 16, :], 0.0), [], ["fs_spl4"])
        for seq in range(NS):
            sub, r0 = seq // 2, (seq % 2) * 64
            S.op("dve", lambda e: e.tensor_copy(out=spl4[r0:r0 + 64, seq, 16, :], in_=lfn[r0:r0 + 64, sub, :]), [("fs_lfn", sub)], ["fs_spl4"])
        flat = spl4[:, :, :, :].rearrange("p s a h -> p (s a h)")
        wflat = wexp[:, :, :, :].rearrange("p s a h -> p (s a h)")
        tflat = ttot4[:, :, :, :].rearrange("p s a h -> p (s a h)")
        NT = NS * 272
        for (c0, c1) in ((0, 512), (512, 1024), (1024, NT)):
            pb, pkey = nbk("m", MB)
            S.op("pe", lambda e: e.matmul(pb[:, 0:c1 - c0], NSTR, flat[:, c0:c1], start=True, stop=True), ["fs_spl4", "cst"], [pkey])
            S.op("dve", lambda e: e.tensor_copy(out=wflat[:, c0:c1], in_=pb[:, 0:c1 - c0]), [pkey], ["fs_wexp"])
            pb, pkey = nbk("m", MB)
            S.op("pe", lambda e: e.matmul(pb[:, 0:c1 - c0], NONE, flat[:, c0:c1], start=True, stop=True), ["fs_spl4", "cst"], [pkey])
            S.op("dve", lambda e: e.tensor_copy(out=tflat[:, c0:c1], in_=pb[:, 0:c1 - c0]), [pkey], ["fs_ttot4"])
        for a in range(15, -1, -1):
            if a == 15:
                S.op("dve", lambda e: e.tensor_copy(out=sfx4[:, :, a, :], in_=ttot4[:, :, a + 1, :]), ["fs_ttot4"], ["fs_sfx4"])
            else:
                S.op("dve", lambda e: e.tensor_tensor(out=sfx4[:, :, a, :], in0=sfx4[:, :, a + 1, :], in1=ttot4[:, :, a + 1, :], op=ALU.add), ["fs_ttot4", "fs_sfx4"], ["fs_sfx4"])
        S.op("dve", lambda e: e.tensor_tensor(out=wexp[:, :, 0:16, :], in0=wexp[:, :, 0:16, :], in1=sfx4[:, :, 0:16, :], op=ALU.add), ["fs_wexp", "fs_sfx4"], ["fs_wexp"])
        S.op("act", lambda e: e.activation(out=wflat[:, :], in_=wflat[:, :], func=AF.Exp), ["fs_wexp"], ["fs_wexp"])
        for seq in range(NS):
            sub, r0 = seq // 2, (seq % 2) * 64
            vn3 = vn[r0:r0 + 64, sub, :].rearrange("p (h c) -> p h c", c=65)
            wn3 = mkap(wexp, r0 * NS * 272 + seq * 272 + 256, [[NS * 272, 64], [1, 16], [0, 64]])
            S.op("pool", lambda e: e.tensor_tensor(out=vn3[:, :, 0:64], in0=vn3[:, :, 0:64], in1=wn3, op=ALU.mult), [("fs_vn", sub), "fs_wexp"], [("fs_vn", sub)])
            S.op("pool", lambda e: e.tensor_copy(out=vn3[:, :, 64], in_=wexp[r0:r0 + 64, seq, 16, :]), ["fs_wexp"], [("fs_vn", sub)])

        ACC = [2, 3, 4]

        def acc_of(h):
            b = ACC[h // 7]
            return pbank[b], f"pb{b}", (h % 7) * 65

        def do_T(seq, a):
            ki = rot["k"] % NST
            rot["k"] += 1
            S.dma("sp", [(kst[ki][:, :], ck_in[seq, a * 128:(a + 1) * 128, :])], [], [f"fs_kst{ki}"], f"d_ks{ki}")
            vi = rot["v"] % NST
            rot["v"] += 1
            S.dma("sp", [(vst[vi][:, :], cv_in[seq, a * 128:(a + 1) * 128, :])], [], [f"fs_vst{vi}"], f"d_vs{vi}")
            return ki, vi

        def do_X(seq, a, ki, vi):
            kt_i = rot["kt"] % 3
            rot["kt"] += 1
            for half in range(2):
                pb, pkey = nbk("m", [5, 6])
                for q in range(4):
                    kc = half * 4 + q
                    S.op("pe", lambda e: e.transpose(pb[:, q * 128:(q + 1) * 128], kst[ki][:, kc * 128:(kc + 1) * 128], ident), [f"fs_kst{ki}", "cst"], [pkey], signal=(q == 3))
                dst = kTt[kt_i][:, half * 4:(half + 1) * 4, :]
                srcp = pb[:, :].rearrange("p (q t) -> p q t", t=128)
                if half == 0:
                    S.op("act", lambda e: e.copy(out=dst, in_=srcp), [pkey], [f"fs_kTt{kt_i}"])
                else:
                    S.op("dve", lambda e: e.tensor_copy(out=dst, in_=srcp), [pkey], [f"fs_kTt{kt_i}"])
            vt_i = rot["vt"] % 3
            rot["vt"] += 1
            v3 = v65t[vt_i][:, :].rearrange("p (h c) -> p h c", c=65)
            wb3 = mkap(wexp, seq * 272 + a * 16, [[NS * 272, 128], [1, 16], [0, 64]])
            S.op("pool", lambda e: e.tensor_tensor(out=v3[:, :, 0:64], in0=vst[vi][:, :].rearrange("p (h c) -> p h c", c=64), in1=wb3, op=ALU.mult),
                 [f"fs_vst{vi}", "fs_wexp"], [f"fs_v65t{vt_i}"])
            S.op("pool", lambda e: e.tensor_copy(out=v3[:, :, 64], in_=wexp[:, seq, a, :]), ["fs_wexp"], [f"fs_v65t{vt_i}"])
            return kt_i, vt_i

        def do_S(seq, kt_i):
            qc = seq * 64
            res = []
            for g in range(2):
                pbS, skey = pbank[g], f"pb{g}"
                for j in range(8):
                    h = g * 8 + j
                    S.op("pe", lambda e: e.matmul(pbS[:, j * 64:(j + 1) * 64], kTt[kt_i][:, h // 2, :], qz[:, h, qc:qc + 64], start=True, stop=True),
                         [f"fs_kTt{kt_i}", ("fs_q", h)], [skey], signal=(j == 7))
                pi = rot["p"] % 4
                rot["p"] += 1
                S.op("act", lambda e: e.activation(out=PTs[pi][:, :], in_=pbS[:, :], func=AF.Exp, scale=0.125), [skey], [f"fs_PT{pi}"])
                res.append(pi)
            return res

        def do_PV(pis, vt_i):
            for h in range(16):
                pbO, okey, oc = acc_of(h)
                pi = pis[h // 8]
                S.op("pe", lambda e: e.matmul(pbO[0:64, oc:oc + 65], PTs[pi][:, (h % 8) * 64:(h % 8 + 1) * 64], v65t[vt_i][:, h * 65:(h + 1) * 65], start=False, stop=False, skip_group_check=True),
                     [f"fs_PT{pi}", f"fs_v65t{vt_i}"], [okey], signal=(h in (6, 13, 15)))

        for seq in range(NS):
            sub, r0 = seq // 2, (seq % 2) * 64
            qc = seq * 64
            for b in ACC:
                S.op("pe", lambda e: e.matmul(pbank[b][:, :], zeros[:, 0:128], zeros[:, :], start=True, stop=False, skip_group_check=True), ["fs_zeros"], [f"pb{b}"], signal=False)
            slots = {}
            xs = {}
            slots[0] = do_T(seq, 0)
            slots[1] = do_T(seq, 1)
            xs[0] = do_X(seq, 0, *slots[0])
            prev = None
            for a in range(16):
                if a + 2 < 16:
                    slots[a + 2] = do_T(seq, a + 2)
                if a + 1 < 16:
                    xs[a + 1] = do_X(seq, a + 1, *slots[a + 1])
                pis = do_S(seq, xs[a][0])
                if prev is not None:
                    do_PV(*prev)
                prev = (pis, xs[a][1])
            pisn = []
            for g in range(2):
                pbS, skey = pbank[g], f"pb{g}"
                for j in range(8):
                    h = g * 8 + j
                    S.op("pe", lambda e: e.matmul(pbS[r0:r0 + 64, j * 64:(j + 1) * 64], kTn[:, h // 2, qc:qc + 64], qz[:, h, qc:qc + 64], start=True, stop=True),
                         ["fs_kTn", ("fs_q", h)], [skey], signal=(j == 7))
                pi = rot["p"] % 4
                rot["p"] += 1
                S.op("act", lambda e: e.activation(out=PTs[pi][r0:r0 + 64, :], in_=pbS[r0:r0 + 64, :], func=AF.Exp, scale=0.125), [skey], [f"fs_PT{pi}"])
                pv3 = PTs[pi][r0:r0 + 64, :].rearrange("p (h c) -> p h c", c=64)
                trb = mkap(cst, r0 * CW + C_TRI + r0, [[CW, 64], [0, 8], [1, 64]])
                S.op("dve", lambda e: e.tensor_tensor(out=pv3, in0=pv3, in1=trb, op=ALU.mult), [f"fs_PT{pi}", "cst"], [f"fs_PT{pi}"])
                pisn.append(pi)
            do_PV(*prev)
            for h in range(16):
                pbO, okey, oc = acc_of(h)
                pi = pisn[h // 8]
                S.op("pe", lambda e: e.matmul(pbO[0:64, oc:oc + 65], PTs[pi][r0:r0 + 64, (h % 8) * 64:(h % 8 + 1) * 64], vn[r0:r0 + 64, sub, h * 65:(h + 1) * 65], start=False, stop=(h in (6, 13, 15)), skip_group_check=True),
                     [f"fs_PT{pi}", ("fs_vn", sub)], [okey], signal=(h in (6, 13, 15)))
            for bi, b in enumerate(ACC):
                nh = 7 if bi < 2 else 2
                h0 = bi * 7
                av = pbank[b][0:64, 0:nh * 65].rearrange("p (h c) -> p h c", c=65)
                S.op("dve", lambda e: e.reciprocal(out=rl[0:64, h0:h0 + nh], in_=av[:, :, 64]), [f"pb{b}"], ["fs_rl"])
                rb = mkap(rl, h0, [[16, 64], [1, nh], [0, 64]])
                S.op("dve", lambda e: e.tensor_tensor(out=o_n[0:64, h0 * 64:(h0 + nh) * 64].rearrange("p (h c) -> p h c", c=64), in0=av[:, :, 0:64], in1=rb, op=ALU.mult),
                     [f"pb{b}", "fs_rl"], ["fs_on"])
            pb, pkey = nbk("m", [5, 6])
            for hc in range(KC):
                S.op("pe", lambda e: e.matmul(pb[:, hc * 64:(hc + 1) * 64], o_n[0:64, hc * 128:(hc + 1) * 128], ident_bf[0:64, 0:64], start=True, stop=True),
                     ["fs_on", "fs_idbf"], [pkey], signal=(hc == KC - 1))
            S.op("act", lambda e: e.copy(out=oT[:, :, qc:qc + 64], in_=pb[:, :].rearrange("p (c q) -> p c q", q=64)), [pkey], ["fs_h"])
        for j in range(2):
            wb, wkey = wload_cols(fox_w_out, j * 512, 512)
            wv = wb[:, 0:KC * 512].rearrange("p (kc c) -> p kc c", c=512)
            for c4 in range(4):
                ncx = j * 4 + c4
                pb, pkey = nbk("m", MB)
                for kc in range(KC):
                    S.op("pe", lambda e: e.matmul(pb[:, 0:n], wv[:, kc, c4 * 128:(c4 + 1) * 128], oT[:, kc, 0:n], start=(kc == 0), stop=(kc == KC - 1)),
                         [wkey, "fs_h"], [pkey], signal=(kc == KC - 1))
                S.op("dve", lambda e: e.tensor_tensor(out=xT[:, ncx, t0:t0 + n], in0=pb[:, 0:n], in1=xT[:, ncx, t0:t0 + n], op=ALU.add),
                     [pkey, ("x", tt)], [("x", tt)], disjoint=True)
        if all(cfg.get(k_, True) for k_ in ("gla", "fox", "ffn0", "ffn1")):
            prefetch_gu(1, 0)
            prefetch_gu(1, 1)
        S.barrier()
        release(nalloc)


    def fox_prompt():
        hTb = [sb(f"fp_h{p}", [128, KC, 512], BF16) for p in range(2)]
        sqq = sb("fp_sqq", [128, KC, 512], BF16)
        qz2 = sb("fp_qz2", [128, KC, 512], BF16)
        rstd = sb("fp_rstd", [128, 512], F32)
        kT = sb("fp_kT", [128, KC, 2048], BF16)
        v65 = sb("fp_v65", [128, 16, 1040], BF16)
        PT = [sb(f"fp_PT{i}", [128, 512], BF16) for i in range(3)]
        o_n = sb("fp_on", [128, 4, 1024], BF16)
        spl = sb("fp_spl", [128, 16, 16], F32)
        bias = sb("fp_bias", [128, 16, 16], F32)
        ttot = sb("fp_ttot", [128, 16, 16], F32)
        sfx = sb("fp_sfx", [128, 16, 16], F32)
        lfz = sb("fp_lfz", [128, 16], F32)
        lfst = sb("fp_lfst", [128, 16], F32)
        rl = sb("fp_rl", [128, 4], F32)
        zeros = sb("fp_zeros", [128, 512], BF16)
        ident_bf = sb("fp_idbf", [128, 128], BF16)
        cstage = [sb(f"fp_cst{i}", [128, 512], F32) for i in range(2)]
        nalloc = 2 + 5 + 3 + 10 + 2
        for nm in ("d_c0", "d_c1", "d_lf"):
            S.add_sem(nm, "dma")
        sqv = o_n[:, :, :].rearrange("p a (b c) -> p (a b) c", b=2)
        tri = cst[:, C_TRI:C_TRI + 128]
        NSTR = cst[:, C_NSTR:C_NSTR + 128]
        NONE = cst[:, C_NONE:C_NONE + 128]
        bfb = cst[:, C_BF:C_BF + 16]
        S.op("pool", lambda e: e.memset(zeros[:], 0.0), [], ["fp_zeros"])
        S.op("pool", lambda e: e.memset(v65[:], 1.0), [], [("fp_v65", kt) for kt in range(16)])
        S.op("pool", lambda e: e.memset(sqq[:], 0.0), [], [("fp_q", h_) for h_ in range(8)])
        S.op("pool", lambda e: e.memset(qz2[:], 0.0), [], [("fp_q", h_) for h_ in range(8, 16)])
        S.op("dve", lambda e: e.tensor_copy(out=ident_bf[:], in_=ident), ["cst"], ["fp_idbf"])
        rot = {"s": 0, "o": 0, "m": 0, "c": 0, "p": 0, "k": 0}

        def nbk(which, lst):
            i = lst[rot[which] % len(lst)]
            rot[which] += 1
            return pbank[i], f"pb{i}"

        def stage_slot():
            i = rot["c"] % 2
            rot["c"] += 1
            return cstage[i], f"fp_cst{i}", f"d_c{i}"

        def gen_KV(tt):
            KVB = [6, 7]
            t0, n = TILES[tt]
            p = tt % 2
            hT = hTb[p]
            hk = ("fp_h", p)

            def proj_fm(wv, wkey, c0, pb, pkey):
                for kc in range(KC):
                    S.op("pe", lambda e: e.matmul(pb[:, 0:n], wv[:, kc, c0:c0 + 128], hT[:, kc, 0:n], start=(kc == 0), stop=(kc == KC - 1)),
                         [wkey, hk], [pkey], signal=(kc == KC - 1))
                    if kc == 3:
                        yield

            def proj_tm(wv, wkey, sub, ncols, pb, pkey):
                for kc in range(KC):
                    S.op("pe", lambda e: e.matmul(pb[:, 0:ncols], hT[:, kc, sub * 128:(sub + 1) * 128], wv[:, kc, 0:ncols], start=(kc == 0), stop=(kc == KC - 1)),
                         [wkey, hk], [pkey], signal=(kc == KC - 1))
                    if kc == 3 and ncols > 16:
                        yield

            rmsnorm(tt, 2, lambda kc: hT[:, kc, 0:n], [hk], sqv, "fp_on", rstd, "fp_rstd", bank=lambda: nbk("k", KVB))
            yield
            wb, wkey = wload_cols(fox_w_in, 3072, 16)
            wv = wb[:, 0:KC * 16].rearrange("p (kc c) -> p kc c", c=16)
            for sub in range(4):
                kt = 4 * tt + sub
                pb, pkey = nbk("k", KVB)
                yield from proj_tm(wv, wkey, sub, 16, pb, pkey)
                S.op("dve", lambda e: e.tensor_tensor(out=lfz[:, :], in0=pb[:, 0:16], in1=bfb, op=ALU.add), [pkey, "cst"], ["fp_lfz"])
                S.op("act", lambda e: e.activation(out=lfz[:, :], in_=lfz[:, :], func=AF.Exp, scale=-1.0), ["fp_lfz"], ["fp_lfz"])
                S.op("dve", lambda e: e.tensor_scalar_add(lfz[:, :], lfz[:, :], 1.0), ["fp_lfz"], ["fp_lfz"])
                S.op("act", lambda e: e.activation(out=spl[:, kt, :], in_=lfz[:, :], func=AF.Ln), ["fp_lfz"], [("fp_spl", kt)])
                S.op("dve", lambda e: e.tensor_scalar(out=lfst[:, :], in0=spl[:, kt, :], scalar1=-1.0, scalar2=None, op0=ALU.mult), [("fp_spl", kt)], ["fp_lfst"])
                S.dma("sp", [(lf_out[t0 + sub * 128:t0 + (sub + 1) * 128, :], lfst[:, :])], ["fp_lfst"], [], "d_lf")
                yield
            for j in range(2):
                wb, wkey = wload_cols(fox_w_in, 1024 + j * 512, 512)
                wv = wb[:, 0:KC * 512].rearrange("p (kc c) -> p kc c", c=512)
                for c4 in range(4):
                    pb, pkey = nbk("k", KVB)
                    yield from proj_fm(wv, wkey, c4 * 128, pb, pkey)
                    S.op("dve", lambda e: e.tensor_copy(out=kT[:, j * 4 + c4, t0:t0 + n], in_=pb[:, 0:n]), [pkey], [("fp_kT", 4 * tt + q) for q in range(4)], disjoint=True)
                    yield
                for sub in range(4):
                    pb, pkey = nbk("k", KVB)
                    yield from proj_tm(wv, wkey, sub, 512, pb, pkey)
                    st, stk, sem = stage_slot()
                    S.op("dve", lambda e: e.tensor_copy(out=st[:, 0:512], in_=pb[:, :]), [pkey], [stk])
                    S.dma("sp", [(k_out[t0 + sub * 128:t0 + (sub + 1) * 128, j * 512:(j + 1) * 512], st[:, 0:512])], [stk], [], sem)
                    yield
            for j in range(2):
                wb, wkey = wload_cols(fox_w_in, 2048 + j * 512, 512)
                wv = wb[:, 0:KC * 512].rearrange("p (kc c) -> p kc c", c=512)
                for sub in range(4):
                    kt = 4 * tt + sub
                    pb, pkey = nbk("k", KVB)
                    yield from proj_tm(wv, wkey, sub, 512, pb, pkey)
                    st, stk, sem = stage_slot()
                    S.op("dve", lambda e: e.tensor_copy(out=st[:, 0:512], in_=pb[:, :]), [pkey], [stk])
                    S.dma("sp", [(v_out[t0 + sub * 128:t0 + (sub + 1) * 128, j * 512:(j + 1) * 512], st[:, 0:512])], [stk], [], sem)
                    dst = v65[:, kt, j * 520:(j + 1) * 520].rearrange("p (h c) -> p h c", c=65)[:, :, 0:64]
                    S.op("dve", lambda e: e.tensor_copy(out=dst, in_=pb[:, :].rearrange("p (h c) -> p h c", c=64)), [pkey], [("fp_v65", kt)])
                    yield

        def compute_bias(nk):
            pb, pkey = pbank[4], "pb4"
            rhs = spl[:, 0:nk, :]
            S.op("pe", lambda e: e.matmul(pb[:, 0:nk * 16], NSTR, rhs, start=True, stop=True), [("fp_spl", a) for a in range(nk)] + ["cst"], [pkey])
            S.op("dve", lambda e: e.tensor_copy(out=bias[:, 0:nk, :], in_=pb[:, 0:nk * 16].rearrange("p (a h) -> p a h", h=16)), [pkey], ["fp_bias"])
            S.op("pe", lambda e: e.matmul(pb[:, 0:nk * 16], NONE, rhs, start=True, stop=True), [("fp_spl", a) for a in range(nk)] + ["cst"], [pkey])
            S.op("dve", lambda e: e.tensor_copy(out=ttot[:, 0:nk, :], in_=pb[:, 0:nk * 16].rearrange("p (a h) -> p a h", h=16)), [pkey], ["fp_ttot"])
            for a in range(nk - 2, -1, -1):
                if a == nk - 2:
                    S.op("dve", lambda e: e.tensor_copy(out=sfx[:, a, :], in_=ttot[:, a + 1, :]), ["fp_ttot"], ["fp_sfx"])
                else:
                    S.op("dve", lambda e: e.tensor_tensor(out=sfx[:, a, :], in0=sfx[:, a + 1, :], in1=ttot[:, a + 1, :], op=ALU.add), ["fp_ttot", "fp_sfx"], ["fp_sfx"])
            S.op("dve", lambda e: e.tensor_tensor(out=bias[:, 0:nk - 1, :], in0=bias[:, 0:nk - 1, :], in1=sfx[:, 0:nk - 1, :], op=ALU.add), ["fp_bias", "fp_sfx"], ["fp_bias"])

        SB_P = [0, 1, 5]

        def attend_prompt(tt, hook, period):
            nk = 4 * tt + 4
            items = []
            for h in range(16):
                for a in range(nk):
                    j = a - 4 * tt
                    items.append(dict(h=h, a=a, c0=(j * 128 if j > 0 else 0), diag=(j >= 0), first=(a == 0), last=(a == nk - 1)))
            n_it = len(items)
            Sres = [None] * n_it
            Eres = [None] * n_it
            obank = {}

            def do_S(i):
                it = items[i]
                h, a, c0 = it["h"], it["a"], it["c0"]
                pbS, skey = nbk("s", SB_P)
                qb_, qk_ = (sqq if h < 8 else qz2), ("fp_q", h)
                S.op("pe", lambda e: e.matmul(pbS[:, c0:512], kT[:, h // 2, a * 128:(a + 1) * 128], qb_[:, h % 8, c0:512], start=True, stop=True),
                     [("fp_kT", a), qk_], [skey])
                Sres[i] = (pbS, skey)

            def do_E(i):
                it = items[i]
                h, a, c0 = it["h"], it["a"], it["c0"]
                pbS, skey = Sres[i]
                pi = rot["p"] % 3
                rot["p"] += 1
                pt, ptk = PT[pi], f"fp_PT{pi}"
                S.op("act", lambda e: e.activation(out=pt[:, c0:512], in_=pbS[:, c0:512], func=AF.Exp, bias=bias[:, a, h:h + 1], scale=0.125),
                     [skey, "fp_bias"], [ptk])
                if it["diag"]:
                    S.op("dve", lambda e: e.tensor_tensor(out=pt[:, c0:c0 + 128], in0=pt[:, c0:c0 + 128], in1=tri, op=ALU.mult), [ptk, "cst"], [ptk])
                Eres[i] = (pt, ptk)

            def do_PV(i):
                it = items[i]
                h, a, c0 = it["h"], it["a"], it["c0"]
                pt, ptk = Eres[i]
                if it["first"]:
                    obank[h] = nbk("o", [2, 3])
                    pbO, okey = obank[h]
                    S.op("pe", lambda e: e.matmul(pbO[:, :], zeros[:, 0:128], zeros[:, :], start=True, stop=False, skip_group_check=True), ["fp_zeros"], [okey], signal=False)
                pbO, okey = obank[h]
                for qt in range(c0 // 128, 4):
                    is_last = (it["last"] and qt == 3)
                    S.op("pe", lambda e: e.matmul(pbO[:, qt * 128:qt * 128 + 65], pt[:, qt * 128:(qt + 1) * 128], v65[:, a, h * 65:(h + 1) * 65], start=False, stop=is_last, skip_group_check=True),
                         [ptk, ("fp_v65", a)], [okey], signal=(qt == 3))
                if it["last"]:
                    ov = pbO[:, :].rearrange("p (q c) -> p q c", c=128)
                    S.op("dve", lambda e: e.reciprocal(out=rl[:, 0:4], in_=ov[:, 0:4, 64]), [okey], ["fp_rl"])
                    rb = mkap(rl, 0, [[4, 128], [1, 4], [0, 64]])
                    S.op("dve", lambda e: e.tensor_tensor(out=o_n[:, 0:4, h * 64:(h + 1) * 64], in0=ov[:, 0:4, 0:64], in1=rb, op=ALU.mult),
                         [okey, "fp_rl"], ["fp_on"])

            do_S(0)
            if n_it > 1:
                do_S(1)
            for i in range(n_it):
                do_E(i)
                if i + 2 < n_it:
                    do_S(i + 2)
                if i >= 1:
                    do_PV(i - 1)
                if hook is not None and i % period == period - 1:
                    hook()
            do_PV(n_it - 1)

        tiles = list(cfg.get("fox_tiles", [0, 1, 2, 3]))
        g0 = gen_KV(tiles[0])
        for _ in g0:
            pass
        for idx, tt in enumerate(tiles):
            t0, n = TILES[tt]
            p = tt % 2
            hT = hTb[p]
            hk = ("fp_h", p)
            for j in range(2):
                wb, wkey = wload_cols(fox_w_in, j * 512, 512)
                wv = wb[:, 0:KC * 512].rearrange("p (kc c) -> p kc c", c=512)
                for c4 in range(4):
                    pb, pkey = nbk("m", [5, 6, 7])
                    for kc in range(KC):
                        S.op("pe", lambda e: e.matmul(pb[:, 0:n], wv[:, kc, c4 * 128:(c4 + 1) * 128], hT[:, kc, 0:n], start=(kc == 0), stop=(kc == KC - 1)),
                             [wkey, hk], [pkey], signal=(kc == KC - 1))
                    hcq = j * 4 + c4
                    for hp in range(2):
                        hh = 2 * hcq + hp
                        qb_, qk_ = (sqq if hh < 8 else qz2), ("fp_q", hh)
                        pr_ = slice(hp * 64, hp * 64 + 64)
                        if hp == 0:
                            S.op("act", lambda e: e.copy(out=qb_[pr_, hh % 8, 0:n], in_=pb[pr_, 0:n]), [pkey], [qk_])
                        else:
                            S.op("dve", lambda e: e.tensor_copy(out=qb_[pr_, hh % 8, 0:n], in_=pb[pr_, 0:n]), [pkey], [qk_])
            compute_bias(4 * tt + 4)
            nxt = tiles[idx + 1] if idx + 1 < len(tiles) else None
            gk = gen_KV(nxt) if nxt is not None else None
            state = {"alive": gk is not None}

            def hook():
                if state["alive"]:
                    try:
                        next(gk)
                    except StopIteration:
                        state["alive"] = False

            n_items = 16 * (4 * tt + 4)
            attend_prompt(tt, hook if gk is not None else None, max(1, n_items // 58))
            while state["alive"]:
                hook()
            oT = hT
            for hc in range(KC):
                pb, pkey = nbk("m", [5, 6, 7])
                for qt in range(4):
                    S.op("pe", lambda e: e.matmul(pb[:, qt * 128:(qt + 1) * 128], o_n[:, qt, hc * 128:(hc + 1) * 128], ident_bf[:], start=True, stop=True),
                         ["fp_on", "fp_idbf"], [pkey], signal=(qt == 3))
                if hc % 2 == 0:
                    S.op("act", lambda e: e.copy(out=oT[:, hc, 0:512], in_=pb[:, :]), [pkey], [hk])
                else:
                    S.op("dve", lambda e: e.tensor_copy(out=oT[:, hc, 0:512], in_=pb[:, :]), [pkey], [hk])
            for j in range(2):
                wb, wkey = wload_cols(fox_w_out, j * 512, 512)
                wv = wb[:, 0:KC * 512].rearrange("p (kc c) -> p kc c", c=512)
                for c4 in range(4):
                    ncx = j * 4 + c4
                    pb, pkey = nbk("m", [5, 6, 7])
                    for kc in range(KC):
                        S.op("pe", lambda e: e.matmul(pb[:, 0:n], wv[:, kc, c4 * 128:(c4 + 1) * 128], oT[:, kc, 0:n], start=(kc == 0), stop=(kc == KC - 1)),
                             [wkey, hk], [pkey], signal=(kc == KC - 1))
                    S.op("dve", lambda e: e.tensor_tensor(out=xT[:, ncx, t0:t0 + n], in0=pb[:, 0:n], in1=xT[:, ncx, t0:t0 + n], op=ALU.add),
                         [pkey, ("x", tt)], [("x", tt)], disjoint=True)
        if all(cfg.get(k_, True) for k_ in ("gla", "fox", "ffn0", "ffn1")):
            prefetch_cols(fox_w_in, 3072, 16)
            prefetch_cols(fox_w_in, 0, 512)
        S.barrier()
        release(nalloc)

    if cfg.get("fox", True):
        fox_prompt()
        fox_sample()

    if cfg.get("ffn1", True):
        ffn_phase(1)

    yT = sb("yT", [128, KC, 512], F32)
    sq = sb("fin_sq", [128, KC, 512], BF16)
    rstd = sb("fin_rstd", [128, 512], F32)
    yst = [sb(f"yst{i}", [128, D], F32) for i in range(2)]
    S.add_sem("d_y0", "dma")
    S.add_sem("d_y1", "dma")
    oc = 0
    for tt in range(5):
        t0, n = TILES[tt]
        rmsnorm(tt, 4, lambda kc: yT[:, kc, 0:n], ["yT"], sq, "fin_sq", rstd, "fin_rstd")
        for s in range(n // 128):
            st, skey = yst[oc % 2], f"yst{oc % 2}"
            for half in range(2):
                pb, pkey = next_bank()
                for j in range(4):
                    kc = half * 4 + j
                    S.op("pe", lambda e, pb=pb, j=j, kc=kc, s=s: e.transpose(pb[:, j * 128:(j + 1) * 128], yT[:, kc, s * 128:(s + 1) * 128], ident),
                         ["yT", "cst"], [pkey], signal=(j == 3))
                if half == 0:
                    S.op("act", lambda e, pb=pb, st=st: e.copy(out=st[:, 0:512], in_=pb[:]), [pkey], [skey])
                else:
                    S.op("dve", lambda e, pb=pb, st=st: e.tensor_copy(out=st[:, 512:1024], in_=pb[:]), [pkey], [skey])
            S.dma("sp", [(y_out[t0 + s * 128:t0 + (s + 1) * 128, :], st[:])], [skey], [], f"d_y{oc % 2}")
            oc += 1

    S.barrier()
    release(len(ctxs))
    S.close()
    return nc


_CONST_CACHE = {}


def _consts(inputs):
    c = np.zeros((128, CW), np.float32)
    c[:, C_ID:C_ID + 128] = np.eye(128, dtype=np.float32)
    s = np.arange(128)[:, None]
    t = np.arange(128)[None, :]
    same = (s // 64) == (t // 64)
    c[:, C_UINC:C_UINC + 128] = np.where(same & (s <= t), -1.0 / 16.0, 0.0)
    c[:, C_USTR:C_USTR + 128] = np.where(same & (s > t), -1.0 / 16.0, 0.0)
    c[:, C_MASK2:C_MASK2 + 128] = np.where(same & (s <= t), 1.0, 0.0)
    c[:, C_TRI:C_TRI + 128] = np.where(s <= t, 1.0, 0.0)
    c[:, C_NSTR:C_NSTR + 128] = np.where(s > t, -1.0, 0.0)
    c[:, C_NONE:C_NONE + 128] = -1.0
    vecs = [inputs["norm_mix"][0], inputs["norm_ffn"][0], inputs["norm_mix"][1], inputs["norm_ffn"][1],
            inputs["norm_final"], inputs["gla_norm"][0]]
    for i, v in enumerate(vecs):
        c[:, C_NW + i * 8:C_NW + (i + 1) * 8] = np.asarray(v, np.float32).reshape(8, 128).T
    c[0:16, C_WG2:C_WG2 + 512] = inputs["gla_w_g2"][0]
    c[16, C_WG2:C_WG2 + 512] = inputs["gla_b_g"][0]
    c[:, C_BF:C_BF + 16] = np.broadcast_to(np.asarray(inputs["fox_b_f"][0], np.float32)[None, :], (128, 16))
    return c


def make_in_maps(inputs, ncores=NCORES):
    f = lambda a: np.ascontiguousarray(np.asarray(a, dtype=np.float32))
    cst = _consts(inputs)
    maps = []
    for c in range(ncores):
        xin = np.concatenate([f(inputs["x_prompt"][c]), f(inputs["x_sample"][NS * c:NS * (c + 1)]).reshape(NS * DSEQ, D)], axis=0)
        maps.append({
            "xin": np.ascontiguousarray(xin),
            "cst": cst,
            "gla_w_in": f(inputs["gla_w_in"][0]),
            "gla_w_out": f(inputs["gla_w_out"][0]),
            "fox_w_in": f(inputs["fox_w_in"][0]),
            "fox_w_out": f(inputs["fox_w_out"][0]),
            "ffn_w_in": f(inputs["ffn_w_in"]),
            "ffn_w_down": f(inputs["ffn_w_down"]),
            "state_in": f(inputs["state_gla"][0, NS * c:NS * (c + 1)]),
            "ck_in": f(inputs["cache_fox_k"][0, NS * c:NS * (c + 1)]).reshape(NS, SEQ, D),
            "cv_in": f(inputs["cache_fox_v"][0, NS * c:NS * (c + 1)]).reshape(NS, SEQ, D),
            "clf_in": f(inputs["cache_fox_logf"][0, NS * c:NS * (c + 1)]),
        })
    return maps


def kernel(**inputs):
    nc = build_program({})
    maps = make_in_maps(inputs)
    res = run_bass_kernel_spmd(nc, maps, core_ids=list(range(NCORES)))
    R = res.results
    B = NCORES
    y_prompt = np.stack([R[c]["y_out"][:SEQ] for c in range(B)], 0)
    y_sample = np.concatenate([R[c]["y_out"][SEQ:].reshape(NS, DSEQ, D) for c in range(B)], 0)
    gla_state_p = np.stack([R[c]["gsp_out"] for c in range(B)], 0)[None]
    gla_state_s = np.concatenate([R[c]["gss_out"] for c in range(B)], 0)[None]
    fox_k_p = np.stack([R[c]["k_out"][:SEQ].reshape(SEQ, 16, 64) for c in range(B)], 0)[None]
    fox_v_p = np.stack([R[c]["v_out"][:SEQ].reshape(SEQ, 16, 64) for c in range(B)], 0)[None]
    fox_logf_p = np.stack([R[c]["lf_out"][:SEQ] for c in range(B)], 0)[None]
    fox_k_s = np.concatenate([R[c]["k_out"][SEQ:].reshape(NS, DSEQ, 16, 64) for c in range(B)], 0)[None]
    fox_v_s = np.concatenate([R[c]["v_out"][SEQ:].reshape(NS, DSEQ, 16, 64) for c in range(B)], 0)[None]
    fox_logf_s = np.concatenate([R[c]["lf_out"][SEQ:].reshape(NS, DSEQ, 16) for c in range(B)], 0)[None]
    outs = (y_prompt, y_sample, gla_state_p, fox_k_p, fox_v_p, fox_logf_p, gla_state_s, fox_k_s, fox_v_s, fox_logf_s)
    return tuple(np.ascontiguousarray(o, dtype=np.float32) for o in outs)
```
